# Optimizing a Trainium2 kernel written in Bass

```python
import jax, jax.numpy as jnp
from jax import lax
import numpy as np

D_MODEL = 1024
BATCH = 4
SEQ = 8192
DEPTH = 4
DEC_BATCH = 32
DEC_SEQ = 64
PAST_LEN = 1024

CHUNK = 64
N_MIXERS = 3
N_HEADS = 8
HEAD_DIM = D_MODEL // N_HEADS
D_ATT = N_HEADS * HEAD_DIM
IDX_HEADS = 4
IDX_DIM = 64
TOPK_MAX = 256
Q_BLOCK = 64
ROPE_THETA = 500000.0
D_CONV = D_MODEL
CONV_WIDTH = 31
D_POOL = D_MODEL
POOL_WINDOWS = (2, 4, 8, 16)
N_POOL_GROUPS = 4
POOL_GROUP = D_POOL // N_POOL_GROUPS
POOL_HIST = max(POOL_WINDOWS) - 1
LN_EPS = 1e-5
DEEPNORM_ALPHA = (2.0 * DEPTH) ** 0.25
DEEPNORM_BETA = (8.0 * DEPTH) ** -0.25
N_A = (DEPTH + 2) // 3
N_B = (DEPTH + 1) // 3
N_C = DEPTH // 3
A_SPLITS = (D_ATT, 2 * D_ATT, 3 * D_ATT, 4 * D_ATT,
            4 * D_ATT + IDX_HEADS * IDX_DIM,
            4 * D_ATT + IDX_HEADS * IDX_DIM + IDX_DIM)
A_IN = A_SPLITS[-1] + IDX_HEADS

kernel_name = 'hybrid_dsa_conv_pool_stream_step'

F32 = jnp.float32


def _layer_norm(x, g, b):
    xf = x.astype(F32)
    mu = jnp.mean(xf, axis=-1, keepdims=True)
    var = jnp.mean(jnp.square(xf - mu), axis=-1, keepdims=True)
    return ((xf - mu) * lax.rsqrt(var + LN_EPS) * g.astype(F32) + b.astype(F32)).astype(x.dtype)


def _rope(x, pos):
    r = x.shape[-1] // 4
    half = r // 2
    inv = ROPE_THETA ** (-jnp.arange(half, dtype=F32) * 2.0 / r)
    ang = pos.astype(F32)[:, None] * inv[None, :]
    cos = jnp.cos(ang)[None, :, None, :]
    sin = jnp.sin(ang)[None, :, None, :]
    xf = x.astype(F32)
    x1 = xf[..., :half]
    x2 = xf[..., half:r]
    out = jnp.concatenate([x1 * cos - x2 * sin, x2 * cos + x1 * sin, xf[..., r:]], axis=-1)
    return out.astype(x.dtype)


def _attn_project(x, w_in, pos):
    B, T, _ = x.shape
    h = x @ w_in
    q, k, v, g, qi, ki, wi = jnp.split(h, A_SPLITS, axis=-1)
    q = _rope(q.reshape(B, T, N_HEADS, HEAD_DIM), pos)
    k = _rope(k.reshape(B, T, N_HEADS, HEAD_DIM), pos)
    v = v.reshape(B, T, N_HEADS, HEAD_DIM)
    qi = _rope(qi.reshape(B, T, IDX_HEADS, IDX_DIM), pos)
    ki = _rope(ki[:, :, None, :], pos)[:, :, 0, :]
    wi = wi * (IDX_HEADS ** -0.5)
    return q, k, v, g, qi, ki, wi


def _sparse_attend(q, qi, wi, qpos, K, V, KI, kpos, topk):
    dots = jnp.einsum('bthd,bsd->bths', qi.astype(F32), KI.astype(F32)) * (IDX_DIM ** -0.5)
    iscore = jnp.einsum('bth,bths->bts', wi.astype(F32), jax.nn.relu(dots))
    adm = (kpos[None, :] // CHUNK) <= (qpos[:, None] // CHUNK)
    iscore = jnp.where(adm[None], iscore, -jnp.inf)
    _, idx = lax.top_k(iscore, topk)
    valid = (kpos[idx] // CHUNK) <= (qpos[None, :, None] // CHUNK)
    gather = jax.vmap(lambda kb, ib: kb[ib])
    Kg = gather(K, idx)
    Vg = gather(V, idx)
    s = jnp.einsum('bthd,btjhd->bthj', q.astype(F32), Kg.astype(F32)) * (HEAD_DIM ** -0.5)
    s = jnp.where(valid[:, :, None, :], s, -jnp.inf)
    p = jax.nn.softmax(s, axis=-1)
    o = jnp.einsum('bthj,btjhd->bthd', p, Vg.astype(F32))
    return o.astype(q.dtype)


def _attn_mixer_prompt(x, w_in, w_out):
    B, T, _ = x.shape
    pos = jnp.arange(T)
    q, k, v, g, qi, ki, wi = _attn_project(x, w_in, pos)
    topk = min(TOPK_MAX, T // 4)
    nb = T // Q_BLOCK

    def blk(a):
        return jnp.swapaxes(a.reshape(B, nb, Q_BLOCK, *a.shape[2:]), 0, 1)

    ob = lax.map(lambda a: _sparse_attend(a[0], a[1], a[2], a[3], k, v, ki, pos, topk),
                 (blk(q), blk(qi), blk(wi), pos.reshape(nb, Q_BLOCK)))
    o = jnp.swapaxes(ob, 0, 1).reshape(B, T, D_ATT)
    y = (o * jax.nn.silu(g)) @ w_out
    return y, k, v, ki


def _attn_mixer_sample(x, ck, cv, cki, w_in, w_out):
    B, T, _ = x.shape
    past = ck.shape[1]
    pos = past + jnp.arange(T)
    q, k, v, g, qi, ki, wi = _attn_project(x, w_in, pos)
    K = jnp.concatenate([ck.astype(k.dtype), k], axis=1)
    V = jnp.concatenate([cv.astype(v.dtype), v], axis=1)
    KI = jnp.concatenate([cki.astype(ki.dtype), ki], axis=1)
    kpos = jnp.arange(past + T)
    topk = min(TOPK_MAX, (past + T) // 4)
    o = _sparse_attend(q, qi, wi, pos, K, V, KI, kpos, topk).reshape(B, T, D_ATT)
    y = (o * jax.nn.silu(g)) @ w_out
    return y, k, v, ki


def _conv_mixer(x, prev, w_in, conv_w, conv_b, n_g, n_b, w_out):
    h = x @ w_in
    a, b, g = jnp.split(h, 3, axis=-1)
    u = a * jax.nn.sigmoid(b)
    ext = jnp.concatenate([prev.astype(u.dtype), u], axis=1)
    c = lax.conv_general_dilated(ext, conv_w[:, None, :].astype(u.dtype), window_strides=(1,),
                                 padding='VALID', dimension_numbers=('NWC', 'WIO', 'NWC'),
                                 feature_group_count=D_CONV) + conv_b
    c = jax.nn.silu(_layer_norm(c, n_g, n_b))
    y = (c * jax.nn.silu(g)) @ w_out
    return y, ext[:, -(CONV_WIDTH - 1):]


def _pool_mixer(x, prev, pos, w_in, w_grp, scale, w_out):
    B, T, _ = x.shape
    h = x @ w_in
    u, g = jnp.split(h, 2, axis=-1)
    ext = jnp.concatenate([prev.astype(u.dtype), u], axis=1)
    P = POOL_HIST
    cs = jnp.cumsum(jnp.concatenate([jnp.zeros_like(ext[:, :1]), ext], axis=1).astype(F32), axis=1)
    hi = cs[:, P + 1:P + 1 + T]
    uf = u.astype(F32)
    outs = []
    for gi, w in enumerate(POOL_WINDOWS):
        sl = slice(gi * POOL_GROUP, (gi + 1) * POOL_GROUP)
        lo = cs[:, P + 1 - w:P + 1 - w + T, sl]
        cnt = jnp.minimum(pos + 1, w).astype(F32)[None, :, None]
        outs.append((hi[..., sl] - lo) / cnt - uf[..., sl])
    d = jnp.stack(outs, axis=2)
    mixed = jnp.einsum('btgc,gcd->btgd', d, w_grp.astype(F32)).reshape(B, T, D_POOL) * scale.astype(F32)
    y = (mixed.astype(x.dtype) * jax.nn.silu(g)) @ w_out
    return y, ext[:, -P:]


def setup_inputs(seed: int = 0) -> dict:
    key = jax.random.key(seed)
    ks = jax.random.split(key, 24)

    def nrm(k, shape, s):
        return jax.random.normal(k, shape, F32) * s

    return {
        'x_prompt': nrm(ks[0], (BATCH, SEQ, D_MODEL), 1.0),
        'x_sample': nrm(ks[1], (DEC_BATCH, DEC_SEQ, D_MODEL), 1.0),
        'cache_k': nrm(ks[2], (N_A, DEC_BATCH, PAST_LEN, N_HEADS, HEAD_DIM), 1.0),
        'cache_v': nrm(ks[3], (N_A, DEC_BATCH, PAST_LEN, N_HEADS, HEAD_DIM), 1.0),
        'cache_kidx': nrm(ks[4], (N_A, DEC_BATCH, PAST_LEN, IDX_DIM), 1.0),
        'state_conv': nrm(ks[5], (N_B, DEC_BATCH, CONV_WIDTH - 1, D_CONV), 0.5),
        'state_pool': nrm(ks[6], (N_C, DEC_BATCH, POOL_HIST, D_POOL), 1.0),
        'w_in_a': nrm(ks[7], (N_A, D_MODEL, A_IN), D_MODEL ** -0.5),
        'w_out_a': nrm(ks[8], (N_A, D_ATT, D_MODEL), D_ATT ** -0.5 * DEEPNORM_BETA),
        'w_in_b': nrm(ks[9], (N_B, D_MODEL, 3 * D_CONV), D_MODEL ** -0.5),
        'conv_w_b': nrm(ks[10], (N_B, CONV_WIDTH, D_CONV), CONV_WIDTH ** -0.5),
        'conv_bias_b': nrm(ks[11], (N_B, D_CONV), 0.02),
        'norm_g_b': 1.0 + nrm(ks[12], (N_B, D_CONV), 0.02),
        'norm_b_b': nrm(ks[13], (N_B, D_CONV), 0.02),
        'w_out_b': nrm(ks[14], (N_B, D_CONV, D_MODEL), D_CONV ** -0.5 * DEEPNORM_BETA),
        'w_in_c': nrm(ks[15], (N_C, D_MODEL, 2 * D_POOL), D_MODEL ** -0.5),
        'w_grp_c': nrm(ks[16], (N_C, N_POOL_GROUPS, POOL_GROUP, POOL_GROUP), POOL_GROUP ** -0.5),
        'scale_c': 1.0 + nrm(ks[17], (N_C, D_POOL), 0.02),
        'w_out_c': nrm(ks[18], (N_C, D_POOL, D_MODEL), D_POOL ** -0.5 * DEEPNORM_BETA),
        'ln_g': 1.0 + nrm(ks[19], (DEPTH, D_MODEL), 0.02),
        'ln_b': nrm(ks[20], (DEPTH, D_MODEL), 0.02),
    }


def reference(x_prompt, x_sample, cache_k, cache_v, cache_kidx, state_conv, state_pool,
              w_in_a, w_out_a, w_in_b, conv_w_b, conv_bias_b, norm_g_b, norm_b_b, w_out_b,
              w_in_c, w_grp_c, scale_c, w_out_c, ln_g, ln_b):
    xp, xs = x_prompt, x_sample
    Tp = xp.shape[1]
    Ts = xs.shape[1]
    past = cache_k.shape[2]
    kp, vp, kip, cvp, plp = [], [], [], [], []
    ksm, vsm, kism, cvs, pls = [], [], [], [], []
    for i in range(DEPTH):
        m = i % N_MIXERS
        j = i // N_MIXERS
        if m == 0:
            yp, k1, v1, ki1 = _attn_mixer_prompt(xp, w_in_a[j], w_out_a[j])
            ys, k2, v2, ki2 = _attn_mixer_sample(xs, cache_k[j], cache_v[j], cache_kidx[j],
                                                 w_in_a[j], w_out_a[j])
            kp.append(k1); vp.append(v1); kip.append(ki1)
            ksm.append(k2); vsm.append(v2); kism.append(ki2)
        elif m == 1:
            zeros = jnp.zeros((xp.shape[0], CONV_WIDTH - 1, D_CONV), xp.dtype)
            yp, c1 = _conv_mixer(xp, zeros, w_in_b[j], conv_w_b[j], conv_bias_b[j],
                                 norm_g_b[j], norm_b_b[j], w_out_b[j])
            ys, c2 = _conv_mixer(xs, state_conv[j], w_in_b[j], conv_w_b[j], conv_bias_b[j],
                                 norm_g_b[j], norm_b_b[j], w_out_b[j])
            cvp.append(c1); cvs.append(c2)
        else:
            zeros = jnp.zeros((xp.shape[0], POOL_HIST, D_POOL), xp.dtype)
            yp, p1 = _pool_mixer(xp, zeros, jnp.arange(Tp), w_in_c[j], w_grp_c[j], scale_c[j], w_out_c[j])
            ys, p2 = _pool_mixer(xs, state_pool[j], past + jnp.arange(Ts), w_in_c[j], w_grp_c[j],
                                 scale_c[j], w_out_c[j])
            plp.append(p1); pls.append(p2)
        xp = _layer_norm(DEEPNORM_ALPHA * xp + yp, ln_g[i], ln_b[i])
        xs = _layer_norm(DEEPNORM_ALPHA * xs + ys, ln_g[i], ln_b[i])
    return (xp, xs,
            jnp.stack(kp), jnp.stack(vp), jnp.stack(kip), jnp.stack(cvp), jnp.stack(plp),
            jnp.stack(ksm), jnp.stack(vsm), jnp.stack(kism), jnp.stack(cvs), jnp.stack(pls))
```

```python
import numpy as np
from contextlib import ExitStack
import concourse.bass as bass
import concourse.mybir as mybir
from concourse.bass_utils import run_bass_kernel_spmd

F32 = mybir.dt.float32
BF16 = mybir.dt.bfloat16
U8 = mybir.dt.uint8
ALU = mybir.AluOpType
AF = mybir.ActivationFunctionType

D = 1024
NH = 8
A_IN = 4420
DS = 64
ALPHA = 8.0 ** 0.25
EPS = 1e-5
NEG = -1.0e30
MBIG = -30000.0
SCALE = 128.0 ** -0.5
ROPE_THETA = 500000.0


class Cfg:
    def __init__(self, SEQ=8192, NS=4, PAST=1024, topk_p=256, topk_s=256, ksteps=20):
        self.SEQ, self.NS, self.PAST = SEQ, NS, PAST
        self.topk_p, self.topk_s, self.ksteps = topk_p, topk_s, ksteps
        self.NTP = SEQ // 128
        self.NCT = PAST // 128
        self.TT = SEQ + NS * DS
        self.NKT = self.NTP + NS * (self.NCT + 1)
        self.KSW = PAST + DS


class SemC:
    def __init__(self, h):
        self.h = h
        self.n = 0


class Stream:
    def __init__(self, name, semc, serial):
        self.name, self.sem, self.serial = name, semc, serial
        self.ops = []
        self.seen = {}
        self.last = None

    def add(self, fn, waits=(), inc=None):
        ws = [w for w in waits if w is not None]
        if self.serial and self.last is not None:
            ws.append(self.last)
        if inc is None:
            self.sem.n += 1
            tok = (self.sem, self.sem.n)
            spec = (self.sem, 1)
            if self.serial:
                self.last = tok
        else:
            semc, amt = inc
            semc.n += amt
            tok = (semc, semc.n)
            spec = (semc, amt)
        self.ops.append((fn, ws, spec))
        return tok

    def emit(self, eng):
        for fn, ws, spec in self.ops:
            best = {}
            for (s, v) in ws:
                if self.name == "pe" and s is self.sem:
                    continue
                if best.get(s, 0) < v:
                    best[s] = v
            for s, v in best.items():
                if self.seen.get(s, 0) >= v:
                    continue
                self.seen[s] = v
                eng.wait_ge(s.h, v)
            inst = fn(eng)
            inst.then_inc(spec[0].h, spec[1])
        self.ops = []


class Buf:
    def __init__(self, ap=None):
        self.ap = ap
        self.w = None
        self.rs = {}

    def __getitem__(self, k):
        return self.ap[k]


class K:
    def __init__(self, nc, es):
        self.nc, self.es = nc, es
        mk = lambda n: SemC(es.enter_context(nc.semaphore(n)))
        self.pe = Stream("pe", mk("s_pe"), False)
        self.act = Stream("act", mk("s_act"), True)
        self.dve = Stream("dve", mk("s_dve"), True)
        self.pool = Stream("pool", mk("s_pool"), True)
        self.sp = Stream("sp", mk("s_sp"), False)
        self.stq = self.sp
        self.ldsem = [mk("ld%d" % i) for i in range(16)]
        self.stsem = [mk("st%d" % i) for i in range(16)]
        self.ldi = 0
        self.sti = 0
        self.lasttok = {}
        self.dram = {}

    def op(self, stream, fn, reads=(), writes=(), extra=(), inc=None):
        waits = list(extra)
        for b in reads:
            waits.append(b.w)
        for b in writes:
            waits.append(b.w)
            waits.extend(b.rs.items())
        tok = stream.add(fn, waits, inc)
        for b in reads:
            s, v = tok
            if b.rs.get(s, 0) < v:
                b.rs[s] = v
        for b in writes:
            b.w = tok
            b.rs = {}
        return tok

    def dbuf(self, key):
        if key not in self.dram:
            self.dram[key] = Buf()
        return self.dram[key]

    def load(self, out_ap, in_ap, reads=(), writes=(), **kw):
        semc = self.ldsem[self.ldi % len(self.ldsem)]
        self.ldi += 1
        prev = self.lasttok.get(semc)
        tok = self.op(self.sp, lambda e: e.dma_start(out=out_ap, in_=in_ap, **kw), reads, writes,
                      extra=[prev], inc=(semc, 16))
        self.lasttok[semc] = tok
        return tok

    def store(self, out_ap, in_ap, reads=(), writes=(), **kw):
        semc = self.stsem[self.sti % len(self.stsem)]
        self.sti += 1
        prev = self.lasttok.get(semc)
        tok = self.op(self.stq, lambda e: e.dma_start(out=out_ap, in_=in_ap, **kw), reads, writes,
                      extra=[prev], inc=(semc, 16))
        self.lasttok[semc] = tok
        return tok

    def flush(self):
        nc = self.nc
        with nc.Block() as block:
            @block.sync
            def _(e):
                self.sp.emit(e)

            @block.gpsimd
            def _(e):
                self.pool.emit(e)

            @block.scalar
            def _(e):
                self.act.emit(e)

            @block.vector
            def _(e):
                self.dve.emit(e)

            @block.tensor
            def _(e):
                self.pe.emit(e)

    def barrier_tokens(self):
        toks = []
        for s in (self.pe, self.act, self.dve, self.pool):
            if s.sem.n:
                toks.append((s.sem, s.sem.n))
        for semc in self.ldsem + self.stsem:
            if semc.n:
                toks.append((semc, semc.n))
        return toks


def build(cfg):
    nc = bass.Bass("TRN2", target_bir_lowering=False)
    SEQ, NS, PAST, NTP, NCT, TT, NKT = cfg.SEQ, cfg.NS, cfg.PAST, cfg.NTP, cfg.NCT, cfg.TT, cfg.NKT
    NSR = NS * DS

    def din(name, shape, dt=F32):
        return nc.dram_tensor(name, list(shape), dt, kind="ExternalInput").ap()

    def dout(name, shape):
        return nc.dram_tensor(name, list(shape), F32, kind="ExternalOutput").ap()

    def dscr(name, shape, dt):
        if getattr(cfg, "debug", False) and dt == F32:
            return nc.dram_tensor(name, list(shape), dt, kind="ExternalOutput").ap()
        return nc.dram_tensor(name, list(shape), dt).ap()

    I = dict(
        xp=din("xp", [SEQ, D]), xs=din("xs", [NSR, D]),
        ck=din("ck", [2, NS, PAST, D]), cv=din("cv", [2, NS, PAST, D]), cki=din("cki", [2, NS, PAST, 64]),
        sconv=din("sconv", [NS, 30, D]), spool=din("spool", [NS, 15, D]),
        w_in_a=din("w_in_a", [2, D, A_IN]), w_out_a=din("w_out_a", [2, D, D]),
        w_in_b=din("w_in_b", [D, 3 * D]), conv_w=din("conv_w", [31, D]), conv_b=din("conv_b", [D]),
        ng=din("ng", [D]), nb=din("nb", [D]), w_out_b=din("w_out_b", [D, D]),
        w_in_c=din("w_in_c", [D, 2 * D]), w_grp=din("w_grp", [4, 256, 256]), scale_c=din("scale_c", [D]),
        w_out_c=din("w_out_c", [D, D]), ln_g=din("ln_g", [4, D]), ln_b=din("ln_b", [4, D]),
        rope=din("rope", [TT, 48]),
    )
    O = dict(
        yp=dout("yp", [SEQ, D]), ys=dout("ys", [NSR, D]),
        nkp=dout("nkp", [2, SEQ, D]), nvp=dout("nvp", [2, SEQ, D]), nkip=dout("nkip", [2, SEQ, 64]),
        ncp=dout("ncp", [30, D]), npp=dout("npp", [15, D]),
        nks=dout("nks", [2, NSR, D]), nvs=dout("nvs", [2, NSR, D]), nkis=dout("nkis", [2, NSR, 64]),
        ncs=dout("ncs", [NS, 30, D]), nps=dout("nps", [NS, 15, D]),
    )
    NT = NTP + NS
    SC = dict(
        xres=[None] + [dscr("xres%d" % l, [TT, D], F32) for l in (1, 2, 3)],
        QT=dscr("QT", [NT, 128, 1024], BF16), GT=dscr("GT", [NT, 128, 1024], BF16),
        QIT=dscr("QIT", [NT, 128, 256], BF16), SG=dscr("SG", [TT, 4], F32),
        KT=dscr("KT", [NKT, 128, 1024], BF16), VV=dscr("VV", [NKT, 128, 1024], BF16),
        KIP=dscr("KIP", [64, SEQ], BF16), KIS=dscr("KIS", [NS, 64, PAST + 128], BF16),
    )

    def tile_rows(i):
        if i < NTP:
            return i * 128, 128
        return SEQ + (i - NTP) * DS, DS

    def xsrc(l, i):
        r0, nt = tile_rows(i)
        if l == 0:
            if i < NTP:
                return I["xp"][r0:r0 + nt, :]
            return I["xs"][r0 - SEQ:r0 - SEQ + nt, :]
        return SC["xres"][l][r0:r0 + nt, :]

    def xdst(l, i):
        r0, nt = tile_rows(i)
        if l == 3:
            if i < NTP:
                return O["yp"][r0:r0 + nt, :]
            return O["ys"][r0 - SEQ:r0 - SEQ + nt, :]
        return SC["xres"][l + 1][r0:r0 + nt, :]

    with ExitStack() as es0:
        k = K(nc, es0)
        op, pe, act, dve, pool = k.op, k.pe, k.act, k.dve, k.pool

        uid = [0]

        def sb(es, name, shape, dt):
            uid[0] += 1
            return Buf(es.enter_context(nc.sbuf_tensor("%s_%d" % (name, uid[0]), list(shape), dt)))

        def psb(es, name, shape, dt):
            uid[0] += 1
            return Buf(es.enter_context(nc.psum_tensor("%s_%d" % (name, uid[0]), list(shape), dt)))

        identb = sb(es0, "identb", [128, 128], BF16)
        identf = sb(es0, "identf", [128, 128], F32)
        onesb = sb(es0, "onesb", [128, 128], BF16)
        onesf = sb(es0, "onesf", [128, 128], F32)
        invc = sb(es0, "invc", [128, 4, 16], F32)
        for ib in (identb, identf):
            op(pool, lambda e, ib=ib: e.memset(ib.ap[:], 1.0), writes=[ib])
            op(pool, lambda e, ib=ib: e.affine_select(ib.ap[:], ib.ap[:], [[-1, 128]], ALU.is_equal, 0.0,
                                                        base=0, channel_multiplier=1), writes=[ib])
        op(pool, lambda e: e.memset(onesb.ap[:], 1.0), writes=[onesb])
        op(pool, lambda e: e.memset(onesf.ap[:], 1.0), writes=[onesf])
        iot = sb(es0, "iot", [128, 16], F32)
        op(pool, lambda e: e.iota(iot.ap[:], [[1, 16]], base=1, channel_multiplier=0,
                                  allow_small_or_imprecise_dtypes=True), writes=[iot])
        for g in range(4):
            op(dve, lambda e, g=g: e.tensor_scalar(invc.ap[:, g, :], iot.ap[:], float(2 ** (g + 1)), None, ALU.min),
               reads=[iot], writes=[invc])
        op(dve, lambda e: e.reciprocal(invc.ap[:], invc.ap[:]), writes=[invc])

        def phase_barrier():
            toks = k.barrier_tokens()
            for s in (k.pe, k.act, k.dve, k.pool, k.sp):
                s.add(lambda e: e.nop(), toks)

        def load_weight_bf16(es, wtile, src_ap, ncols, stage, pieces=4):
            cw = (ncols + pieces - 1) // pieces
            cnt = 0
            for kk in range(8):
                for pc in range(pieces):
                    c0 = pc * cw
                    c1 = min(ncols, c0 + cw)
                    if c0 >= c1:
                        continue
                    st = stage[cnt % len(stage)]
                    cnt += 1
                    k.load(st.ap[:, 0:c1 - c0], src_ap[kk * 128:(kk + 1) * 128, c0:c1], writes=[st])
                    eng = act if cnt % 2 else dve
                    if eng is act:
                        op(act, lambda e, st=st, kk=kk, c0=c0, c1=c1: e.activation(wtile.ap[:, kk, c0:c1], st.ap[:, 0:c1 - c0], AF.Copy),
                           reads=[st], writes=[wtile])
                    else:
                        op(dve, lambda e, st=st, kk=kk, c0=c0, c1=c1: e.tensor_copy(wtile.ap[:, kk, c0:c1], st.ap[:, 0:c1 - c0]),
                           reads=[st], writes=[wtile])

        def make_xT(xt, xb, pstr, xT, nt):
            op(act, lambda e: e.activation(xb.ap[:nt, :], xt.ap[:nt, :], AF.Copy), reads=[xt], writes=[xb])
            for kk in range(8):
                op(pe, lambda e, kk=kk: e.transpose(pstr.ap[:, kk, :nt], xb.ap[:nt, kk * 128:(kk + 1) * 128], identb.ap[:nt, :nt]),
                   reads=[xb, identb], writes=[pstr] if kk == 0 else [], extra=[pstr.w] if kk else [])
            pstr.w = (pe.sem, pe.sem.n)
            op(dve, lambda e: e.tensor_copy(xT.ap[:, :, :nt], pstr.ap[:, :, :nt]), reads=[pstr], writes=[xT])

        def post(l, i, aT, wout, xt, ps_y, z, stats, mv, rstd, xn, gbc, bbc):
            r0, nt = tile_rows(i)
            for half in range(2):
                for kk in range(8):
                    op(pe, lambda e, half=half, kk=kk: e.matmul(ps_y[half].ap[:nt, :], aT.ap[:, kk, :nt],
                                                              wout.ap[:, kk, half * 512:(half + 1) * 512],
                                                              start=(kk == 0), stop=(kk == 7)),
                       reads=[aT, wout], writes=[ps_y[half]] if kk == 0 else [])
                ps_y[half].w = (pe.sem, pe.sem.n)
                op(dve, lambda e, half=half: e.scalar_tensor_tensor(z.ap[:nt, half * 512:(half + 1) * 512],
                                                                    xt.ap[:nt, half * 512:(half + 1) * 512], ALPHA,
                                                                    ps_y[half].ap[:nt, :], ALU.mult, ALU.add),
                   reads=[xt, ps_y[half]], writes=[z])
                op(dve, lambda e, half=half: e.bn_stats(stats.ap[:nt, half, :], z.ap[:nt, half * 512:(half + 1) * 512]),
                   reads=[z], writes=[stats])
            op(dve, lambda e: e.bn_aggr(mv.ap[:nt, :], stats.ap[:nt, :, :]), reads=[stats], writes=[mv])
            op(act, lambda e: e.activation(rstd.ap[:nt, :], mv.ap[:nt, 1:2], AF.Sqrt, bias=epsb.ap[:nt, :], scale=1.0),
               reads=[mv, epsb], writes=[rstd])
            op(dve, lambda e: e.reciprocal(rstd.ap[:nt, :], rstd.ap[:nt, :]), writes=[rstd])
            op(dve, lambda e: e.tensor_scalar(xn.ap[:nt, :], z.ap[:nt, :], mv.ap[:nt, 0:1], rstd.ap[:nt, 0:1],
                                              ALU.subtract, ALU.mult), reads=[z, mv, rstd], writes=[xn])
            op(dve, lambda e: e.tensor_tensor(xn.ap[:nt, :], xn.ap[:nt, :], gbc.ap[:nt, :], ALU.mult), reads=[gbc], writes=[xn])
            op(dve, lambda e: e.tensor_tensor(xn.ap[:nt, :], xn.ap[:nt, :], bbc.ap[:nt, :], ALU.add), reads=[bbc], writes=[xn])
            db = k.dbuf(("x", l + 1, i))
            k.store(xdst(l, i), xn.ap[:nt, :], reads=[xn], writes=[db])

        epsb = sb(es0, "epsb", [128, 1], F32)
        op(pool, lambda e: e.memset(epsb.ap[:], EPS), writes=[epsb])

        def load_ln(es, l):
            gbc = sb(es, "gbc", [128, D], F32)
            bbc = sb(es, "bbc", [128, D], F32)
            k.load(gbc.ap[:], I["ln_g"][l, :].partition_broadcast(128), writes=[gbc])
            k.load(bbc.ap[:], I["ln_b"][l, :].partition_broadcast(128), writes=[bbc])
            return gbc, bbc

        def post_bufs(es):
            return dict(
                z=sb(es, "z", [128, D], F32), stats=sb(es, "stats", [128, 2, 6], F32), mv=sb(es, "mv", [128, 2], F32),
                rstd=sb(es, "rstd", [128, 1], F32), xn=sb(es, "xn", [128, D], F32))

        def layer_A(l, j):
            topk_of = lambda i: cfg.topk_p if i < NTP else cfg.topk_s
            with ExitStack() as es:
                win = sb(es, "win", [128, 8, A_IN], BF16)
                stage = [sb(es, "wst%d" % s, [128, 1105], F32) for s in range(3)]
                load_weight_bf16(es, win, I["w_in_a"][j], A_IN, stage, pieces=4)
                xt = [sb(es, "xt%d" % s, [128, D], F32) for s in range(2)]
                rp = [sb(es, "rp%d" % s, [128, 48], F32) for s in range(2)]
                xb = sb(es, "xb", [128, D], BF16)
                xT = sb(es, "xT", [128, 8, 128], BF16)
                hbuf = [sb(es, "h%d" % s, [128, A_IN], F32) for s in range(2)]
                tmp = [sb(es, "rt%d" % s, [128, 16, 16], F32) for s in range(4)]
                hqk = sb(es, "hqk", [128, 2048], BF16)
                vb = sb(es, "vb", [128, D], BF16)
                gs = sb(es, "gs", [128, D], BF16)
                qib = sb(es, "qib", [128, 320], BF16)
                aw = sb(es, "aw", [128, 4], F32)
                sg = sb(es, "sg", [128, 4], F32)
                qT = sb(es, "qT", [128, 8, 128], BF16)
                kT = sb(es, "kT", [128, 8, 128], BF16)
                gT = sb(es, "gT", [128, 8, 128], BF16)
                qiT = sb(es, "qiT", [128, 2, 128], BF16)
                kiT = sb(es, "kiT", [64, 128], BF16)
                ckf = [sb(es, "ckf%d" % s, [128, D], F32) for s in range(2)]
                cvf = [sb(es, "cvf%d" % s, [128, D], F32) for s in range(2)]
                ckif = [sb(es, "ckif%d" % s, [128, 64], F32) for s in range(2)]
                ckb = sb(es, "ckb", [128, D], BF16)
                ckib = sb(es, "ckib", [128, 64], BF16)
                psp = [psb(es, "psp%d" % s, [128, 512], F32) for s in range(5)]
                pstr = psb(es, "pstr", [128, 8, 128], BF16)
                pst2 = [psb(es, "pst2%d" % s, [128, 8, 128], BF16) for s in range(2)]
                chunks = [(c * 512, min(A_IN, (c + 1) * 512)) for c in range(9)]

                def rope_block(hv, c, s, nt, nh, half, rb, h):
                    x1 = hv[:, :, 0:half]
                    x2 = hv[:, :, half:2 * half]
                    cb_ = c.unsqueeze(1).to_broadcast([nt, nh, half])
                    sb_ = s.unsqueeze(1).to_broadcast([nt, nh, half])
                    t = [tt.ap[:nt, 0:nh, 0:half] for tt in tmp]
                    op(dve, lambda e: e.tensor_tensor(t[0], x1, cb_, ALU.mult), reads=[h, rb], writes=[tmp[0]])
                    op(dve, lambda e: e.tensor_tensor(t[1], x2, sb_, ALU.mult), reads=[h, rb], writes=[tmp[1]])
                    op(dve, lambda e: e.tensor_tensor(t[2], x1, sb_, ALU.mult), reads=[h, rb], writes=[tmp[2]])
                    op(dve, lambda e: e.tensor_tensor(t[3], x2, cb_, ALU.mult), reads=[h, rb], writes=[tmp[3]])
                    op(dve, lambda e: e.tensor_tensor(x1, t[0], t[1], ALU.subtract), reads=[tmp[0], tmp[1]], writes=[h])
                    op(dve, lambda e: e.tensor_tensor(x2, t[3], t[2], ALU.add), reads=[tmp[2], tmp[3]], writes=[h])

                def loads(i):
                    r0, nt = tile_rows(i)
                    k.load(xt[i % 2].ap[:nt, :], xsrc(l, i), reads=[k.dbuf(("x", l, i))], writes=[xt[i % 2]])
                    k.load(rp[i % 2].ap[:nt, :], I["rope"][r0:r0 + nt, :], writes=[rp[i % 2]])

                def computeP1(i):
                    r0, nt = tile_rows(i)
                    x_, rp_ = xt[i % 2], rp[i % 2]
                    h = hbuf[i % 2]
                    make_xT(x_, xb, pstr, xT, nt)
                    for ci, (c0, c1) in enumerate(chunks):
                        ps = psp[ci % 5]
                        for kk in range(8):
                            op(pe, lambda e, ps=ps, kk=kk, c0=c0, c1=c1: e.matmul(ps.ap[:nt, 0:c1 - c0], xT.ap[:, kk, :nt], win.ap[:, kk, c0:c1],
                                                                                  start=(kk == 0), stop=(kk == 7)),
                               reads=[xT, win], writes=[ps] if kk == 0 else [])
                        ps.w = (pe.sem, pe.sem.n)
                        if ci % 2 == 0:
                            op(act, lambda e, ps=ps, c0=c0, c1=c1: e.activation(h.ap[:nt, c0:c1], ps.ap[:nt, 0:c1 - c0], AF.Copy),
                               reads=[ps], writes=[h])
                        else:
                            op(dve, lambda e, ps=ps, c0=c0, c1=c1: e.tensor_copy(h.ap[:nt, c0:c1], ps.ap[:nt, 0:c1 - c0]),
                               reads=[ps], writes=[h])

                def computeP2(i):
                    r0, nt = tile_rows(i)
                    x_, rp_ = xt[i % 2], rp[i % 2]
                    h = hbuf[i % 2]
                    hv = h.ap[:nt, 0:2048].rearrange("p (h d) -> p h d", d=128)
                    rope_block(hv, rp_.ap[:nt, 0:16], rp_.ap[:nt, 16:32], nt, 16, 16, rp_, h)
                    hv2 = h.ap[:nt, 4096:4416].rearrange("p (h d) -> p h d", d=64)
                    rope_block(hv2, rp_.ap[:nt, 32:40], rp_.ap[:nt, 40:48], nt, 5, 8, rp_, h)
                    if i < NTP:
                        ko, vo, kio = O["nkp"][j, r0:r0 + nt, :], O["nvp"][j, r0:r0 + nt, :], O["nkip"][j, r0:r0 + nt, :]
                    else:
                        q0 = r0 - SEQ
                        ko, vo, kio = O["nks"][j, q0:q0 + nt, :], O["nvs"][j, q0:q0 + nt, :], O["nkis"][j, q0:q0 + nt, :]
                    k.store(ko, h.ap[:nt, 1024:2048], reads=[h])
                    k.store(vo, h.ap[:nt, 2048:3072], reads=[h])
                    k.store(kio, h.ap[:nt, 4352:4416], reads=[h])
                    op(dve, lambda e: e.tensor_copy(hqk.ap[:nt, :], h.ap[:nt, 0:2048]), reads=[h], writes=[hqk])
                    op(act, lambda e: e.activation(vb.ap[:nt, :], h.ap[:nt, 2048:3072], AF.Copy), reads=[h], writes=[vb])
                    op(act, lambda e: e.activation(gs.ap[:nt, :], h.ap[:nt, 3072:4096], AF.Silu), reads=[h], writes=[gs])
                    op(dve, lambda e: e.tensor_scalar(sg.ap[:nt, :], h.ap[:nt, 4416:4420], 0.0, 2.0, ALU.is_ge, ALU.mult),
                       reads=[h], writes=[sg])
                    op(dve, lambda e: e.tensor_scalar(sg.ap[:nt, :], sg.ap[:nt, :], -1.0, None, ALU.add), writes=[sg])
                    op(dve, lambda e: e.scalar_tensor_tensor(aw.ap[:nt, :], h.ap[:nt, 4416:4420], 0.0625, sg.ap[:nt, :], ALU.mult, ALU.mult),
                       reads=[h, sg], writes=[aw])
                    for hh in range(4):
                        op(dve, lambda e, hh=hh: e.tensor_scalar(qib.ap[:nt, hh * 64:(hh + 1) * 64], h.ap[:nt, 4096 + hh * 64:4096 + (hh + 1) * 64],
                                                                 aw.ap[:nt, hh:hh + 1], None, ALU.mult), reads=[h, aw], writes=[qib])
                    op(dve, lambda e: e.tensor_copy(qib.ap[:nt, 256:320], h.ap[:nt, 4352:4416]), reads=[h], writes=[qib])
                    def trn(ps, src, ncol, dst, dsl, evac):
                        nblk = (ncol + 127) // 128
                        for b in range(nblk):
                            w = min(128, ncol - b * 128)
                            op(pe, lambda e, b=b, w=w: e.transpose(ps.ap[0:w, b, :nt], src.ap[:nt, b * 128:b * 128 + w], identb.ap[:nt, :nt]),
                               reads=[src, identb], writes=[ps] if b == 0 else [])
                        ps.w = (pe.sem, pe.sem.n)
                    trn(pst2[0], hqk, 1024, None, None, None)
                    op(act, lambda e: e.activation(qT.ap[:, :, :nt], pst2[0].ap[:, :, :nt], AF.Copy), reads=[pst2[0]], writes=[qT])
                    for b in range(8):
                        op(pe, lambda e, b=b: e.transpose(pst2[1].ap[:, b, :nt], hqk.ap[:nt, 1024 + b * 128:1024 + (b + 1) * 128], identb.ap[:nt, :nt]),
                           reads=[hqk, identb], writes=[pst2[1]] if b == 0 else [])
                    pst2[1].w = (pe.sem, pe.sem.n)
                    op(dve, lambda e: e.tensor_copy(kT.ap[:, :, :nt], pst2[1].ap[:, :, :nt]), reads=[pst2[1]], writes=[kT])
                    trn(pst2[0], gs, 1024, None, None, None)
                    op(act, lambda e: e.activation(gT.ap[:, :, :nt], pst2[0].ap[:, :, :nt], AF.Copy), reads=[pst2[0]], writes=[gT])
                    trn(pst2[1], qib, 320, None, None, None)
                    op(dve, lambda e: e.tensor_copy(qiT.ap[:, :, :nt], pst2[1].ap[:, 0:2, :nt]), reads=[pst2[1]], writes=[qiT])
                    op(dve, lambda e: e.tensor_copy(kiT.ap[:, :nt], pst2[1].ap[0:64, 2, :nt]), reads=[pst2[1]], writes=[kiT])
                    kidx = i if i < NTP else NTP + (i - NTP) * (NCT + 1) + NCT
                    k.store(SC["QT"][i].rearrange("p (h t) -> p h t", t=128)[:, :, :nt], qT.ap[:, :, :nt], reads=[qT], writes=[k.dbuf(("QT", i))])
                    k.store(SC["GT"][i].rearrange("p (h t) -> p h t", t=128)[:, :, :nt], gT.ap[:, :, :nt], reads=[gT], writes=[k.dbuf(("GT", i))])
                    k.store(SC["QIT"][i].rearrange("p (h t) -> p h t", t=128)[:, :, :nt], qiT.ap[:, :, :nt], reads=[qiT], writes=[k.dbuf(("QIT", i))])
                    k.store(SC["KT"][kidx].rearrange("p (h t) -> p h t", t=128)[:, :, :nt], kT.ap[:, :, :nt], reads=[kT], writes=[k.dbuf(("KT", kidx))])
                    k.store(SC["VV"][kidx][:nt, :], vb.ap[:nt, :], reads=[vb], writes=[k.dbuf(("VV", kidx))])
                    k.store(SC["SG"][r0:r0 + nt, :], sg.ap[:nt, :], reads=[sg], writes=[k.dbuf(("SG", i))])
                    if i < NTP:
                        k.store(SC["KIP"][:, r0:r0 + nt], kiT.ap[:, :nt], reads=[kiT], writes=[k.dbuf(("KIP",))])
                    else:
                        k.store(SC["KIS"][i - NTP][:, PAST:PAST + nt], kiT.ap[:, :nt], reads=[kiT], writes=[k.dbuf(("KIS", i - NTP))])

                def cache_tile(s, c):
                    n = s * NCT + c
                    kf, vf, kif = ckf[n % 2], cvf[n % 2], ckif[n % 2]
                    k.load(kf.ap[:], I["ck"][j, s, c * 128:(c + 1) * 128, :], writes=[kf])
                    k.load(vf.ap[:], I["cv"][j, s, c * 128:(c + 1) * 128, :], writes=[vf])
                    k.load(kif.ap[:], I["cki"][j, s, c * 128:(c + 1) * 128, :], writes=[kif])
                    kidx = NTP + s * (NCT + 1) + c
                    op(act, lambda e: e.activation(ckb.ap[:], kf.ap[:], AF.Copy), reads=[kf], writes=[ckb])
                    for b in range(8):
                        op(pe, lambda e, b=b: e.transpose(pst2[0].ap[:, b, :], ckb.ap[:, b * 128:(b + 1) * 128], identb.ap[:]),
                           reads=[ckb, identb], writes=[pst2[0]] if b == 0 else [])
                    pst2[0].w = (pe.sem, pe.sem.n)
                    op(dve, lambda e: e.tensor_copy(kT.ap[:], pst2[0].ap[:]), reads=[pst2[0]], writes=[kT])
                    k.store(SC["KT"][kidx].rearrange("p (h t) -> p h t", t=128), kT.ap[:], reads=[kT], writes=[k.dbuf(("KT", kidx))])
                    op(act, lambda e: e.activation(vb.ap[:], vf.ap[:], AF.Copy), reads=[vf], writes=[vb])
                    k.store(SC["VV"][kidx], vb.ap[:], reads=[vb], writes=[k.dbuf(("VV", kidx))])
                    op(act, lambda e: e.activation(ckib.ap[:], kif.ap[:], AF.Copy), reads=[kif], writes=[ckib])
                    op(pe, lambda e: e.transpose(pst2[1].ap[0:64, 0, :], ckib.ap[:, :], identb.ap[:]), reads=[ckib, identb], writes=[pst2[1]])
                    op(dve, lambda e: e.tensor_copy(kiT.ap[:, :], pst2[1].ap[0:64, 0, :]), reads=[pst2[1]], writes=[kiT])
                    k.store(SC["KIS"][s][:, c * 128:(c + 1) * 128], kiT.ap[:, :], reads=[kiT], writes=[k.dbuf(("KIS", s))])

                loads(0)
                if NT > 1:
                    loads(1)
                computeP1(0)
                for i in range(NT):
                    if i + 1 < NT:
                        computeP1(i + 1)
                    computeP2(i)
                    if i + 2 < NT:
                        loads(i + 2)
                for s in range(NS):
                    for c in range(NCT):
                        cache_tile(s, c)
                phase_barrier()
                k.flush()

            if getattr(cfg, "skip_p2", False):
                return
            with ExitStack() as es:
                NMAX = max(SEQ, PAST + 128)
                wout = sb(es, "wout", [128, 8, D], BF16)
                stage = [sb(es, "wst%d" % s, [128, D], F32) for s in range(2)]
                load_weight_bf16(es, wout, I["w_out_a"][j], D, stage, pieces=1)
                gbc, bbc = load_ln(es, l)
                pb = post_bufs(es)
                ki2 = sb(es, "ki2", [128, NMAX], BF16)
                kis = sb(es, "kis", [128, PAST + 128], BF16)
                isc = sb(es, "isc", [128, NMAX], F32)
                junk = sb(es, "junk", [128, NMAX], U8)
                mb = [sb(es, "mb%d" % s, [128, 512], F32) for s in range(2)]
                maskT = sb(es, "maskT", [128, NMAX // 128 + 1, 128], BF16)
                rl = [sb(es, "rl%d" % s, [128, 512], F32) for s in range(2)]
                qT = [sb(es, "qT%d" % s, [128, 8, 128], BF16) for s in range(2)]
                gT = [sb(es, "gT%d" % s, [128, 8, 128], BF16) for s in range(2)]
                qiT = [sb(es, "qiT%d" % s, [128, 2, 128], BF16) for s in range(2)]
                sg = [sb(es, "sg%d" % s, [128, 4], F32) for s in range(2)]
                xt = [sb(es, "xt%d" % s, [128, D], F32) for s in range(2)]
                NKB = 4
                ktb = [sb(es, "ktb%d" % s, [128, 8, 128], BF16) for s in range(NKB)]
                vtb = [sb(es, "vtb%d" % s, [128, D], BF16) for s in range(NKB)]
                pT = [sb(es, "pT%d" % s, [128, 4, 128], BF16) for s in range(4)]
                oT = sb(es, "oT", [128, 8, 128], BF16)
                rcp = sb(es, "rcp", [128, 4, 128], F32)
                otmp = sb(es, "otmp", [128, 4, 128], F32)
                probe = sb(es, "probe", [128, 1], F32)
                cnt = sb(es, "cnt", [128, 1], F32)
                inc = sb(es, "inc", [128, 1], F32)
                thr = sb(es, "thr", [128, 1], F32)
                rlb = [sb(es, "rlb%d" % q, [128, 512], BF16) for q in range(4)]
                Dsg = [sb(es, "Dsg%d" % q, [128, 4, 128], BF16) for q in range(2)]
                dcount = [0]
                sbias_pool = [sb(es, "sbias%d" % q, [128, 1], F32) for q in range(2)]
                sbias = {}
                ps_d = [psb(es, "ps_d%d" % s, [128, 512], F32) for s in range(2)]
                ps_s = [psb(es, "ps_s%d" % s, [128, 4, 128], F32) for s in range(2)]
                ps_o = [psb(es, "ps_o%d" % s, [128, 4, 128], F32) for s in range(2)]
                ps_l = [psb(es, "ps_l%d" % s, [128, 4, 128], F32) for s in range(2)]
                sbank = [ps_s[0], ps_s[1], ps_d[0], ps_d[1]]
                sview = [ps_s[0].ap, ps_s[1].ap, ps_d[0].ap.rearrange("p (h q) -> p h q", q=128), ps_d[1].ap.rearrange("p (h q) -> p h q", q=128)]
                kipb = k.dbuf(("KIP",))
                k.load(ki2.ap[0:64, 0:SEQ], SC["KIP"][:, :], reads=[kipb], writes=[ki2])
                k.load(ki2.ap[64:128, 0:SEQ], SC["KIP"][:, :], reads=[kipb], writes=[ki2])
                kvcount = [0]

                def loads(i):
                    r0, nt = tile_rows(i)
                    s2 = i % 2
                    k.load(qT[s2].ap[:, :, :nt], SC["QT"][i].rearrange("p (h t) -> p h t", t=128)[:, :, :nt], reads=[k.dbuf(("QT", i))], writes=[qT[s2]])
                    k.load(gT[s2].ap[:, :, :nt], SC["GT"][i].rearrange("p (h t) -> p h t", t=128)[:, :, :nt], reads=[k.dbuf(("GT", i))], writes=[gT[s2]])
                    k.load(qiT[s2].ap[:, :, :nt], SC["QIT"][i].rearrange("p (h t) -> p h t", t=128)[:, :, :nt], reads=[k.dbuf(("QIT", i))], writes=[qiT[s2]])
                    k.load(sg[s2].ap[:nt, :], SC["SG"][r0:r0 + nt, :], reads=[k.dbuf(("SG", i))], writes=[sg[s2]])
                    k.load(xt[s2].ap[:nt, :], xsrc(l, i), reads=[k.dbuf(("x", l, i))], writes=[xt[s2]])

                def tinfo(i):
                    r0, nq = tile_rows(i)
                    if i < NTP:
                        n = 128 * (i + 1)
                        ktiles = [(t, 128) for t in range(i + 1)]
                        kib = ki2
                    else:
                        s = i - NTP
                        n = PAST + DS
                        base = NTP + s * (NCT + 1)
                        ktiles = [(base + c, 128) for c in range(NCT)] + [(base + NCT, DS)]
                        kib = kis
                    return r0, nq, i % 2, topk_of(i), n, ktiles, kib

                def stageA1(i):
                    r0, nq, s2, topk, n, ktiles, kib = tinfo(i)
                    if i >= NTP:
                        s = i - NTP
                        ksb = k.dbuf(("KIS", s))
                        k.load(kis.ap[0:64, 0:n], SC["KIS"][s][:, 0:n], reads=[ksb], writes=[kis])
                        k.load(kis.ap[64:128, 0:n], SC["KIS"][s][:, 0:n], reads=[ksb], writes=[kis])
                    dsg = Dsg[i % 2]
                    for hh in range(4):
                        op(dve, lambda e, hh=hh: e.tensor_scalar(dsg.ap[:nq, hh, :nq], identb.ap[:nq, :nq], sg[s2].ap[:nq, hh:hh + 1], None, ALU.mult),
                           reads=[identb, sg[s2]], writes=[dsg])
                    dbanks = [(ps_d[0], ps_d[0].ap), (ps_s[0], ps_s[0].ap.rearrange("p h q -> p (h q)")), (ps_s[1], ps_s[1].ap.rearrange("p h q -> p (h q)"))]
                    pacc = ps_d[1]
                    for c0 in range(0, n, 512):
                        wk = min(512, n - c0)
                        for hh in range(4):
                            b0 = 64 * (hh % 2)
                            ps, pv_ = dbanks[(dcount[0]) % 3]
                            dcount[0] += 1
                            op(pe, lambda e, pv_=pv_, hh=hh, b0=b0, c0=c0, wk=wk: e.matmul(pv_[:nq, 0:wk], qiT[s2].ap[b0:b0 + 64, hh // 2, :nq],
                                                                                           kib.ap[b0:b0 + 64, c0:c0 + wk], start=True, stop=True),
                               reads=[qiT[s2], kib], writes=[ps])
                            r_ = rlb[hh]
                            op(act, lambda e, pv_=pv_, r_=r_, wk=wk: e.activation(r_.ap[:nq, 0:wk], pv_[:nq, 0:wk], AF.Relu), reads=[ps], writes=[r_])
                        for hh in range(4):
                            op(pe, lambda e, hh=hh, wk=wk: e.matmul(pacc.ap[:nq, 0:wk], dsg.ap[:nq, hh, :nq], rlb[hh].ap[:nq, 0:wk], start=(hh == 0), stop=(hh == 3)),
                               reads=[dsg, rlb[hh]], writes=[pacc] if hh == 0 else [])
                        pacc.w = (pe.sem, pe.sem.n)
                        op(act, lambda e, c0=c0, wk=wk: e.activation(isc.ap[:nq, c0:c0 + wk], pacc.ap[:nq, 0:wk], AF.Copy), reads=[pacc], writes=[isc])
                    if i < NTP:
                        op(dve, lambda e: e.memset(isc.ap[0:64, n - 64:n], NEG), writes=[isc])
                    if i < NTP and n <= topk:
                        op(dve, lambda e: e.memset(thr.ap[:], -1.0e29), writes=[thr])
                        return []
                    steps = []
                    op(dve, lambda e: e.memset(probe.ap[:], 0.0), writes=[probe])
                    KA = getattr(cfg, "ka", 0)

                    def mk_dve(hw):
                        def f():
                            op(dve, lambda e: e.tensor_scalar(junk.ap[:nq, 0:n], isc.ap[:nq, 0:n], probe.ap[:nq, 0:1], None, ALU.is_ge, ALU.add,
                                                              accum_out=cnt.ap[:nq, :]), reads=[isc, probe], writes=[junk, cnt])
                            op(dve, lambda e: e.tensor_scalar(inc.ap[:nq, :], cnt.ap[:nq, :], float(topk), hw, ALU.is_ge, ALU.mult),
                               reads=[cnt], writes=[inc])
                            op(dve, lambda e: e.scalar_tensor_tensor(probe.ap[:nq, :], inc.ap[:nq, :], -hw / 2, probe.ap[:nq, :], ALU.add, ALU.add),
                               reads=[inc], writes=[probe])
                        return f

                    def mk_act(hw):
                        def f():
                            op(act, lambda e: e.activation(junk.ap[:nq, 0:n], isc.ap[:nq, 0:n], AF.Sign, bias=probe.ap[:nq, 0:1], scale=-1.0,
                                                           accum_out=cnt.ap[:nq, :]), reads=[isc, probe], writes=[junk, cnt])
                            op(act, lambda e: e.activation(inc.ap[:nq, :], cnt.ap[:nq, :], AF.Sign, bias=sb_.ap[:nq, 0:1], scale=-1.0),
                               reads=[cnt, sb_], writes=[inc])
                            op(act, lambda e: e.activation(probe.ap[:nq, :], inc.ap[:nq, :], AF.Identity, bias=probe.ap[:nq, 0:1], scale=hw / 2),
                               reads=[inc], writes=[probe])
                        return f

                    sb_ = sbias_pool[i % 2]
                    op(dve, lambda e: e.memset(sb_.ap[:], float(n - 2 * topk) + 0.5), writes=[sb_])
                    hw = 8.0
                    for st in range(cfg.ksteps):
                        steps.append(mk_act(hw) if 1 <= st <= KA else mk_dve(hw))
                        hw = hw / 2
                    hwf = hw
                    steps.append(lambda: op(dve, lambda e: e.tensor_scalar(thr.ap[:nq, :], probe.ap[:nq, :], -hwf, None, ALU.add), reads=[probe], writes=[thr]))
                    return steps

                def stageA2(i):
                    r0, nq, s2, topk, n, ktiles, kib = tinfo(i)
                    for c0 in range(0, n, 512):
                        wk = min(512, n - c0)
                        m_ = mb[(c0 // 512) % 2]
                        op(dve, lambda e, m_=m_, c0=c0, wk=wk: e.tensor_scalar(m_.ap[:nq, 0:wk], isc.ap[:nq, c0:c0 + wk], thr.ap[:nq, 0:1], MBIG, ALU.is_lt, ALU.mult),
                           reads=[isc, thr], writes=[m_])
                        pm = ps_s[(c0 // 512) % 2]
                        nb_ = (wk + 127) // 128
                        for b in range(nb_):
                            w = min(128, wk - b * 128)
                            op(pe, lambda e, pm=pm, m_=m_, b=b, w=w: e.transpose(pm.ap[0:w, b, :nq], m_.ap[:nq, b * 128:b * 128 + w], identf.ap[:nq, :nq]),
                               reads=[m_, identf], writes=[pm] if b == 0 else [])
                        pm.w = (pe.sem, pe.sem.n)
                        kt0 = c0 // 128
                        if wk % 128 == 0:
                            op(act, lambda e, pm=pm, kt0=kt0, nb_=nb_: e.activation(maskT.ap[:, kt0:kt0 + nb_, :nq], pm.ap[:, 0:nb_, :nq], AF.Copy),
                               reads=[pm], writes=[maskT])
                        else:
                            nf = wk // 128
                            if nf:
                                op(act, lambda e, pm=pm, kt0=kt0, nf=nf: e.activation(maskT.ap[:, kt0:kt0 + nf, :nq], pm.ap[:, 0:nf, :nq], AF.Copy),
                                   reads=[pm], writes=[maskT])
                            w = wk - nf * 128
                            op(act, lambda e, pm=pm, kt0=kt0, nf=nf, w=w: e.activation(maskT.ap[0:w, kt0 + nf, :nq], pm.ap[0:w, nf, :nq], AF.Copy),
                               reads=[pm], writes=[maskT])

                def stageB(i, pending):
                    r0, nq, s2, topk, n, ktiles, kib = tinfo(i)
                    nk = len(ktiles)
                    kvs = {}

                    def emitS(ti):
                        kidx, ns = ktiles[ti]
                        slot = kvcount[0] % NKB
                        kvcount[0] += 1
                        kb, vbf = ktb[slot], vtb[slot]
                        kvs[ti] = (kb, vbf)
                        k.load(kb.ap[:, :, :ns], SC["KT"][kidx].rearrange("p (h t) -> p h t", t=128)[:, :, :ns], reads=[k.dbuf(("KT", kidx))], writes=[kb])
                        k.load(vbf.ap[:ns, :], SC["VV"][kidx][:ns, :], reads=[k.dbuf(("VV", kidx))], writes=[vbf])
                        for hg in range(2):
                            bi = (2 * ti + hg) % 4
                            pss = sbank[bi]
                            pv = sview[bi]
                            mbc_ = maskT.ap[:ns, ti, :nq].unsqueeze(1).to_broadcast([ns, 4, nq])
                            op(pe, lambda e, pv=pv, mbc_=mbc_, ns=ns: e.matmul(pv[:ns, :, :nq], identb.ap[:ns, :ns], mbc_, start=True, stop=False, skip_group_check=True),
                               reads=[maskT, identb], writes=[pss])
                            for h4 in range(4):
                                hh = hg * 4 + h4
                                op(pe, lambda e, pv=pv, hh=hh, h4=h4, kb=kb, ns=ns: e.matmul(pv[:ns, h4, :nq], kb.ap[:, hh, :ns], qT[s2].ap[:, hh, :nq], start=False, stop=(h4 == 3), skip_group_check=True),
                                   reads=[kb, qT[s2]])
                            pss.w = (pe.sem, pe.sem.n)
                            p_ = pT[bi]
                            op(act, lambda e, pv=pv, p_=p_, ns=ns: e.activation(p_.ap[:ns, :, :nq], pv[:ns, :, :nq], AF.Exp, scale=SCALE),
                               reads=[pss], writes=[p_])

                    def emitPV(ti):
                        kidx, ns = ktiles[ti]
                        kb, vbf = kvs.pop(ti)
                        for hg in range(2):
                            p_ = pT[(2 * ti + hg) % 4]
                            for h4 in range(4):
                                hh = hg * 4 + h4
                                op(pe, lambda e, hg=hg, hh=hh, h4=h4, vbf=vbf, p_=p_, ns=ns, ti=ti: e.matmul(ps_o[hg].ap[:, h4, :nq], vbf.ap[:ns, hh * 128:(hh + 1) * 128], p_.ap[:ns, h4, :nq],
                                                                                                         start=(ti == 0 and h4 == 0), stop=(ti == nk - 1), skip_group_check=True),
                                   reads=[vbf, p_], writes=[ps_o[hg]] if (ti == 0 and h4 == 0) else [])
                            op(pe, lambda e, hg=hg, p_=p_, ns=ns, ti=ti: e.matmul(ps_l[hg].ap[:, :, :nq], onesb.ap[:ns, :], p_.ap[:ns, :, :nq],
                                                                                  start=(ti == 0), stop=(ti == nk - 1), skip_group_check=True),
                               reads=[p_, onesb], writes=[ps_l[hg]] if ti == 0 else [])

                    emitS(0)
                    for ti in range(nk):
                        if ti + 1 < nk:
                            emitS(ti + 1)
                        emitPV(ti)
                        if pending:
                            pending.pop(0)()
                    while pending:
                        pending.pop(0)()
                    for hg in range(2):
                        ps_o[hg].w = (pe.sem, pe.sem.n)
                        ps_l[hg].w = (pe.sem, pe.sem.n)
                        op(dve, lambda e, hg=hg: e.reciprocal(rcp.ap[:, :, :nq], ps_l[hg].ap[:, :, :nq]), reads=[ps_l[hg]], writes=[rcp])
                        op(dve, lambda e, hg=hg: e.tensor_tensor(otmp.ap[:, :, :nq], ps_o[hg].ap[:, :, :nq], rcp.ap[:, :, :nq], ALU.mult),
                           reads=[ps_o[hg], rcp], writes=[otmp])
                        op(dve, lambda e, hg=hg: e.tensor_tensor(oT.ap[:, hg * 4:hg * 4 + 4, :nq], otmp.ap[:, :, :nq], gT[s2].ap[:, hg * 4:hg * 4 + 4, :nq], ALU.mult),
                           reads=[otmp, gT[s2]], writes=[oT])

                def stagePost(i):
                    r0, nq, s2, topk, n, ktiles, kib = tinfo(i)
                    post(l, i, oT, wout, xt[s2], ps_d, pb["z"], pb["stats"], pb["mv"], pb["rstd"], pb["xn"], gbc, bbc)

                loads(0)
                for f in stageA1(0):
                    f()
                stageA2(0)
                for i in range(NT):
                    pend = []
                    if i + 1 < NT:
                        loads(i + 1)
                        pend = stageA1(i + 1)
                    stageB(i, pend)
                    if i + 1 < NT:
                        stageA2(i + 1)
                    stagePost(i)
                phase_barrier()
                k.flush()

        def layer_BC(l, kind):
            with ExitStack() as es:
                isB = kind == "B"
                NW = 3 * D if isB else 2 * D
                win = sb(es, "win", [128, 8, NW], BF16)
                wout = sb(es, "wout", [128, 8, D], BF16)
                stage = [sb(es, "wst%d" % s, [128, D], F32) for s in range(2)]
                load_weight_bf16(es, win, I["w_in_b"] if isB else I["w_in_c"], NW, stage, pieces=NW // D)
                load_weight_bf16(es, wout, I["w_out_b"] if isB else I["w_out_c"], D, stage, pieces=1)
                gbc, bbc = load_ln(es, l)
                pb = post_bufs(es)
                xt = [sb(es, "xt%d" % s, [128, D], F32) for s in range(2)]
                xb = sb(es, "xb", [128, D], BF16)
                xT = sb(es, "xT", [128, 8, 128], BF16)
                gs = sb(es, "gs", [128, 8, 128], BF16)
                aT = sb(es, "aT", [128, 8, 128], BF16)
                ut = pb["xn"]
                pstr = psb(es, "pstr", [128, 8, 128], BF16)
                psA = [psb(es, "psA%d" % s, [128, 4, 128], F32) for s in range(2)]
                psB = [psb(es, "psB%d" % s, [128, 4, 128], F32) for s in range(2)]
                psG = [psb(es, "psG%d" % s, [128, 4, 128], F32) for s in range(2)]
                psS = psb(es, "psS", [128, 2, 128], F32)
                HP = 30 if isB else 16
                if isB:
                    cw = sb(es, "cw", [128, 8, 31], F32)
                    cb = sb(es, "cb", [128, 8], F32)
                    ngp = sb(es, "ngp", [128, 8], F32)
                    nbp = sb(es, "nbp", [128, 8], F32)
                    Dg = sb(es, "Dg", [128, 8, 31, 128], BF16)
                    for c in range(8):
                        k.load(cw.ap[:, c, :], I["conv_w"][:, c * 128:(c + 1) * 128].rearrange("j p -> p j"), writes=[cw], allow_slow_non_contiguous=True)
                    k.load(cb.ap[:], I["conv_b"].rearrange("(c p) -> p c", p=128), writes=[cb], allow_slow_non_contiguous=True)
                    k.load(ngp.ap[:], I["ng"].rearrange("(c p) -> p c", p=128), writes=[ngp], allow_slow_non_contiguous=True)
                    k.load(nbp.ap[:], I["nb"].rearrange("(c p) -> p c", p=128), writes=[nbp], allow_slow_non_contiguous=True)
                    for c in range(8):
                        for jj in range(31):
                            op(dve, lambda e, c=c, jj=jj: e.tensor_scalar(Dg.ap[:, c, jj, :], identb.ap[:], cw.ap[:, c, jj:jj + 1], None, ALU.mult),
                               reads=[identb, cw], writes=[Dg])
                    sig = sb(es, "sig", [128, 8, 128], F32)
                    u32 = sb(es, "u32", [128, 8, 128], F32)
                    ext = [sb(es, "ext%d" % s, [128, 8, HP + 128], BF16) for s in range(2)]
                    cT = sb(es, "cT", [128, 8, 128], F32)
                    sq = sig
                    mean = sb(es, "mean", [128, 128], F32)
                    msq = sb(es, "msq", [128, 128], F32)
                    var = sb(es, "var", [128, 128], F32)
                    sn = sb(es, "sn", [128, 8, 128], BF16)
                    prevf = sb(es, "prevf", [32, D], F32)
                    prevb = sb(es, "prevb", [32, D], BF16)
                else:
                    wgf = sb(es, "wgf", [128, 2, 256], F32)
                    scb = sb(es, "scb", [128, D], F32)
                    wg = sb(es, "wg", [128, 4, 2, 256], BF16)
                    k.load(scb.ap[:], I["scale_c"].partition_broadcast(128), writes=[scb])
                    for g in range(4):
                        k.load(wgf.ap[:], I["w_grp"][g].rearrange("(c p) d -> p c d", p=128), writes=[wgf])
                        for cc in range(2):
                            op(dve, lambda e, g=g, cc=cc: e.tensor_tensor(wg.ap[:, g, cc, :], wgf.ap[:, cc, :], scb.ap[:, g * 256:(g + 1) * 256], ALU.mult),
                               reads=[wgf, scb], writes=[wg])
                    ext = [sb(es, "ext%d" % s, [128, 8, HP + 128], F32) for s in range(2)]
                    wa = sb(es, "wa", [128, 2, HP + 128], F32)
                    wb2 = sb(es, "wb2", [128, 2, HP + 128], F32)
                    dT = sb(es, "dT", [128, 8, 128], BF16)
                    ftmp = sb(es, "ftmp", [128, 2, 16], F32)
                    prevf = sb(es, "prevf", [32, D], F32)

                def loads(i):
                    r0, nt = tile_rows(i)
                    k.load(xt[i % 2].ap[:nt, :], xsrc(l, i), reads=[k.dbuf(("x", l, i))], writes=[xt[i % 2]])

                def proj(ps2, col0, nt):
                    for fc in range(8):
                        ps = ps2[fc // 4]
                        for kk in range(8):
                            op(pe, lambda e, ps=ps, fc=fc, kk=kk: e.matmul(ps.ap[:, fc % 4, :nt], win.ap[:, kk, col0 + fc * 128:col0 + (fc + 1) * 128], xT.ap[:, kk, :nt],
                                                                         start=(kk == 0), stop=(kk == 7)),
                               reads=[win, xT], writes=[ps] if (fc % 4 == 0 and kk == 0) else [])
                        if fc % 4 == 3:
                            ps.w = (pe.sem, pe.sem.n)

                def state_out(src32, nt, nrows, dst):
                    for c in range(8):
                        op(pe, lambda e, c=c: e.transpose(psG[c // 4].ap[:nt, c % 4, :], src32[:, c, 0:nt], identf.ap[:, :]),
                           reads=[identf], writes=[psG[c // 4]] if c % 4 == 0 else [], extra=[srcbuf[0].w])
                        if c % 4 == 3:
                            psG[c // 4].w = (pe.sem, pe.sem.n)
                    for hf in range(2):
                        op(act, lambda e, hf=hf: e.activation(ut.ap[:nt, hf * 512:(hf + 1) * 512], psG[hf].ap[:nt, :, :], AF.Copy), reads=[psG[hf]], writes=[ut])
                    k.store(dst, ut.ap[nt - nrows:nt, :], reads=[ut])

                srcbuf = [None]

                def compute(i):
                    r0, nt = tile_rows(i)
                    x_ = xt[i % 2]
                    e_cur = ext[i % 2]
                    e_prev = ext[(i + 1) % 2]
                    make_xT(x_, xb, pstr, xT, nt)
                    if i == 0:
                        op(dve, lambda e: e.memset(e_cur.ap[:, :, 0:HP], 0.0), writes=[e_cur])
                    elif i < NTP:
                        op(dve, lambda e: e.tensor_copy(e_cur.ap[:, :, 0:HP], e_prev.ap[:, :, 128:128 + HP]), reads=[e_prev], writes=[e_cur])
                    else:
                        s = i - NTP
                        if isB:
                            k.load(prevf.ap[0:30, :], I["sconv"][s], writes=[prevf])
                            op(act, lambda e: e.activation(prevb.ap[0:30, :], prevf.ap[0:30, :], AF.Copy), reads=[prevf], writes=[prevb])
                            for c in range(8):
                                op(pe, lambda e, c=c: e.transpose(pstr.ap[:, c, 0:30], prevb.ap[0:30, c * 128:(c + 1) * 128], identb.ap[0:30, 0:30]),
                                   reads=[prevb, identb], writes=[pstr] if c == 0 else [])
                            pstr.w = (pe.sem, pe.sem.n)
                            op(dve, lambda e: e.tensor_copy(e_cur.ap[:, :, 0:30], pstr.ap[:, :, 0:30]), reads=[pstr], writes=[e_cur])
                        else:
                            k.load(prevf.ap[0:15, :], I["spool"][s], writes=[prevf])
                            for c in range(8):
                                op(pe, lambda e, c=c: e.transpose(psG[c // 4].ap[:, c % 4, 0:15], prevf.ap[0:15, c * 128:(c + 1) * 128], identf.ap[0:15, 0:15]),
                                   reads=[prevf, identf], writes=[psG[c // 4]] if c % 4 == 0 else [])
                                if c % 4 == 3:
                                    psG[c // 4].w = (pe.sem, pe.sem.n)
                            for hf in range(2):
                                op(dve, lambda e, hf=hf: e.tensor_copy(e_cur.ap[:, hf * 4:hf * 4 + 4, 1:16], psG[hf].ap[:, :, 0:15]), reads=[psG[hf]], writes=[e_cur])
                    if isB:
                        proj(psA, 0, nt)
                        proj(psB, D, nt)
                        proj(psG, 2 * D, nt)
                        for hf in range(2):
                            op(act, lambda e, hf=hf: e.activation(sig.ap[:, hf * 4:hf * 4 + 4, :nt], psB[hf].ap[:, :, :nt], AF.Sigmoid), reads=[psB[hf]], writes=[sig])
                            op(act, lambda e, hf=hf: e.activation(gs.ap[:, hf * 4:hf * 4 + 4, :nt], psG[hf].ap[:, :, :nt], AF.Silu), reads=[psG[hf]], writes=[gs])
                            op(dve, lambda e, hf=hf: e.tensor_tensor(u32.ap[:, hf * 4:hf * 4 + 4, :nt], psA[hf].ap[:, :, :nt], sig.ap[:, hf * 4:hf * 4 + 4, :nt], ALU.mult),
                               reads=[psA[hf], sig], writes=[u32])
                        op(act, lambda e: e.activation(e_cur.ap[:, :, HP:HP + nt], u32.ap[:, :, :nt], AF.Copy), reads=[u32], writes=[e_cur])
                        for c in range(8):
                            ps = psA[c // 4]
                            for jj in range(31):
                                op(pe, lambda e, ps=ps, c=c, jj=jj: e.matmul(ps.ap[:, c % 4, :nt], Dg.ap[:, c, jj, :], e_cur.ap[:, c, jj:jj + nt], start=(jj == 0), stop=(jj == 30)),
                                   reads=[Dg, e_cur], writes=[ps] if (c % 4 == 0 and jj == 0) else [])
                            if c % 4 == 3:
                                ps.w = (pe.sem, pe.sem.n)
                        for c in range(8):
                            op(act, lambda e, c=c: e.activation(cT.ap[:, c, :nt], psA[c // 4].ap[:, c % 4, :nt], AF.Identity, bias=cb.ap[:, c:c + 1], scale=1.0),
                               reads=[psA[c // 4], cb], writes=[cT])
                        op(act, lambda e: e.activation(sq.ap[:, :, :nt], cT.ap[:, :, :nt], AF.Square), reads=[cT], writes=[sq])
                        for which, src in ((0, cT), (1, sq)):
                            for c in range(8):
                                op(pe, lambda e, which=which, src=src, c=c: e.matmul(psS.ap[:, which, :nt], onesf.ap[:, :], src.ap[:, c, :nt], start=(c == 0), stop=(c == 7)),
                                   reads=[onesf, src], writes=[psS] if (which == 0 and c == 0) else [])
                        psS.w = (pe.sem, pe.sem.n)
                        op(dve, lambda e: e.tensor_scalar(mean.ap[:, :nt], psS.ap[:, 0, :nt], 1.0 / D, None, ALU.mult), reads=[psS], writes=[mean])
                        op(dve, lambda e: e.tensor_tensor(msq.ap[:, :nt], mean.ap[:, :nt], mean.ap[:, :nt], ALU.mult), reads=[mean], writes=[msq])
                        op(dve, lambda e: e.scalar_tensor_tensor(var.ap[:, :nt], psS.ap[:, 1, :nt], 1.0 / D, msq.ap[:, :nt], ALU.mult, ALU.subtract),
                           reads=[psS, msq], writes=[var])
                        op(act, lambda e: e.activation(var.ap[:, :nt], var.ap[:, :nt], AF.Sqrt, bias=epsb.ap[:, :], scale=1.0), reads=[epsb], writes=[var])
                        op(dve, lambda e: e.reciprocal(var.ap[:, :nt], var.ap[:, :nt]), writes=[var])
                        mbc = mean.ap[:, :nt].unsqueeze(1).to_broadcast([128, 8, nt])
                        vbc = var.ap[:, :nt].unsqueeze(1).to_broadcast([128, 8, nt])
                        op(dve, lambda e: e.tensor_tensor(cT.ap[:, :, :nt], cT.ap[:, :, :nt], mbc, ALU.subtract), reads=[mean], writes=[cT])
                        op(dve, lambda e: e.tensor_tensor(cT.ap[:, :, :nt], cT.ap[:, :, :nt], vbc, ALU.mult), reads=[var], writes=[cT])
                        for c in range(8):
                            op(act, lambda e, c=c: e.activation(sn.ap[:, c, :nt], cT.ap[:, c, :nt], AF.Silu, bias=nbp.ap[:, c:c + 1], scale=ngp.ap[:, c:c + 1]),
                               reads=[cT, ngp, nbp], writes=[sn])
                        op(dve, lambda e: e.tensor_tensor(aT.ap[:, :, :nt], sn.ap[:, :, :nt], gs.ap[:, :, :nt], ALU.mult), reads=[sn, gs], writes=[aT])
                        if i == NTP - 1 or i >= NTP:
                            srcbuf[0] = u32
                            dst = O["ncp"][:, :] if i < NTP else O["ncs"][i - NTP]
                            state_out(u32.ap, nt, 30, dst)
                    else:
                        proj(psA, 0, nt)
                        proj(psG, D, nt)
                        for hf in range(2):
                            op(act, lambda e, hf=hf: e.activation(gs.ap[:, hf * 4:hf * 4 + 4, :nt], psG[hf].ap[:, :, :nt], AF.Silu), reads=[psG[hf]], writes=[gs])
                            op(act, lambda e, hf=hf: e.activation(e_cur.ap[:, hf * 4:hf * 4 + 4, HP:HP + nt], psA[hf].ap[:, :, :nt], AF.Copy), reads=[psA[hf]], writes=[e_cur])
                        L = HP + nt
                        for g in range(4):
                            E = e_cur.ap[:, 2 * g:2 * g + 2, :]
                            cur = None
                            bufs = [wa, wb2]
                            for lv in range(g + 1):
                                sh = 2 ** lv
                                lo = 2 ** (lv + 1)
                                dstb = bufs[lv % 2]
                                if lv == 0:
                                    op(dve, lambda e, dstb=dstb, E=E, lo=lo, sh=sh: e.tensor_tensor(dstb.ap[:, :, lo:L], E[:, :, lo:L], E[:, :, lo - sh:L - sh], ALU.add),
                                       reads=[e_cur], writes=[dstb])
                                else:
                                    srcb = bufs[(lv + 1) % 2]
                                    op(dve, lambda e, dstb=dstb, srcb=srcb, lo=lo, sh=sh: e.tensor_tensor(dstb.ap[:, :, lo:L], srcb.ap[:, :, lo:L], srcb.ap[:, :, lo - sh:L - sh], ALU.add),
                                       reads=[srcb], writes=[dstb])
                                cur = dstb
                            w = 2 ** (g + 1)
                            op(dve, lambda e, cur=cur, E=E, g=g, w=w: e.scalar_tensor_tensor(dT.ap[:, 2 * g:2 * g + 2, :nt], cur.ap[:, :, HP:HP + nt], 1.0 / w, E[:, :, HP:HP + nt], ALU.mult, ALU.subtract),
                               reads=[cur, e_cur], writes=[dT])
                            if i == 0:
                                ic = invc.ap[:, g, :].unsqueeze(1).to_broadcast([128, 2, 16])
                                op(dve, lambda e, cur=cur, ic=ic: e.tensor_tensor(ftmp.ap[:, :, :], cur.ap[:, :, HP:HP + 16], ic, ALU.mult), reads=[cur, invc], writes=[ftmp])
                                op(dve, lambda e, E=E, g=g: e.tensor_tensor(dT.ap[:, 2 * g:2 * g + 2, 0:16], ftmp.ap[:, :, :], E[:, :, HP:HP + 16], ALU.subtract),
                                   reads=[ftmp, e_cur], writes=[dT])
                        for g in range(4):
                            for dc in range(2):
                                oc = 2 * g + dc
                                ps = psA[oc // 4]
                                for cc in range(2):
                                    op(pe, lambda e, ps=ps, g=g, dc=dc, cc=cc, oc=oc: e.matmul(ps.ap[:, oc % 4, :nt], wg.ap[:, g, cc, dc * 128:(dc + 1) * 128], dT.ap[:, 2 * g + cc, :nt],
                                                                                          start=(cc == 0), stop=(cc == 1)),
                                       reads=[wg, dT], writes=[ps] if (oc % 4 == 0 and cc == 0) else [])
                                if oc % 4 == 3:
                                    ps.w = (pe.sem, pe.sem.n)
                        for hf in range(2):
                            op(dve, lambda e, hf=hf: e.tensor_tensor(aT.ap[:, hf * 4:hf * 4 + 4, :nt], psA[hf].ap[:, :, :nt], gs.ap[:, hf * 4:hf * 4 + 4, :nt], ALU.mult),
                               reads=[psA[hf], gs], writes=[aT])
                        if i == NTP - 1 or i >= NTP:
                            srcbuf[0] = e_cur
                            dst = O["npp"][:, :] if i < NTP else O["nps"][i - NTP]
                            state_out(e_cur.ap[:, :, HP:HP + 128], nt, 15, dst)
                    post(l, i, aT, wout, x_, psB, pb["z"], pb["stats"], pb["mv"], pb["rstd"], pb["xn"], gbc, bbc)

                for i in range(NT + 1):
                    if i < NT:
                        loads(i)
                    if i > 0:
                        compute(i - 1)
                phase_barrier()
                k.flush()

        for spec_ in getattr(cfg, "layers", [("A", 0, 0), ("B", 1), ("C", 2), ("A", 3, 1)]):
            if spec_[0] == "A":
                layer_A(spec_[1], spec_[2])
            else:
                layer_BC(spec_[1], spec_[0])
        toks = k.barrier_tokens()
        k.sp.add(lambda e: e.nop(), toks)
        k.flush()
    return nc


def rope_table(cfg):
    def tab(pos, r):
        half = r // 2
        inv = (ROPE_THETA ** (-np.arange(half, dtype=np.float32) * 2.0 / r)).astype(np.float32)
        ang = pos.astype(np.float32)[:, None] * inv[None, :]
        return np.cos(ang).astype(np.float32), np.sin(ang).astype(np.float32)
    posp = np.arange(cfg.SEQ)
    poss = cfg.PAST + np.arange(DS)
    rows = []
    for pos in (posp, poss):
        c16, s16 = tab(pos, 32)
        c8, s8 = tab(pos, 16)
        rows.append(np.concatenate([c16, s16, c8, s8], axis=1))
    return np.concatenate([rows[0]] + [rows[1]] * cfg.NS, axis=0).astype(np.float32)


def make_in_maps(cfg, inp, ncores, nb):
    f = lambda a: np.ascontiguousarray(np.asarray(a, dtype=np.float32))
    rope = rope_table(cfg)
    maps = []
    NS = cfg.NS
    for c in range(ncores):
        b = c % nb
        ss = slice(c * NS, (c + 1) * NS)
        maps.append(dict(
            xp=f(inp["x_prompt"][b]), xs=f(inp["x_sample"][ss]).reshape(NS * DS, D),
            ck=f(inp["cache_k"][:, ss]).reshape(2, NS, cfg.PAST, D), cv=f(inp["cache_v"][:, ss]).reshape(2, NS, cfg.PAST, D),
            cki=f(inp["cache_kidx"][:, ss]), sconv=f(inp["state_conv"][0, ss]), spool=f(inp["state_pool"][0, ss]),
            w_in_a=f(inp["w_in_a"]), w_out_a=f(inp["w_out_a"]), w_in_b=f(inp["w_in_b"][0]), conv_w=f(inp["conv_w_b"][0]),
            conv_b=f(inp["conv_bias_b"][0]), ng=f(inp["norm_g_b"][0]), nb=f(inp["norm_b_b"][0]), w_out_b=f(inp["w_out_b"][0]),
            w_in_c=f(inp["w_in_c"][0]), w_grp=f(inp["w_grp_c"][0]), scale_c=f(inp["scale_c"][0]), w_out_c=f(inp["w_out_c"][0]),
            ln_g=f(inp["ln_g"]), ln_b=f(inp["ln_b"]), rope=rope,
        ))
    return maps


def assemble(cfg, res, ncores, nb):
    NS = cfg.NS
    R = res
    cat = lambda key, cores: np.stack([R[c][key] for c in cores])
    pc = list(range(nb))
    ac = list(range(ncores))
    yp = cat("yp", pc)
    ys = np.concatenate([R[c]["ys"].reshape(NS, DS, D) for c in ac])
    nkp = np.stack([R[c]["nkp"] for c in pc], axis=1).reshape(2, nb, cfg.SEQ, NH, 128)
    nvp = np.stack([R[c]["nvp"] for c in pc], axis=1).reshape(2, nb, cfg.SEQ, NH, 128)
    nkip = np.stack([R[c]["nkip"] for c in pc], axis=1)
    ncp = cat("ncp", pc)[None]
    npp = cat("npp", pc)[None]
    nks = np.concatenate([R[c]["nks"].reshape(2, NS, DS, NH, 128) for c in ac], axis=1)
    nvs = np.concatenate([R[c]["nvs"].reshape(2, NS, DS, NH, 128) for c in ac], axis=1)
    nkis = np.concatenate([R[c]["nkis"].reshape(2, NS, DS, 64) for c in ac], axis=1)
    ncs = np.concatenate([R[c]["ncs"] for c in ac])[None]
    nps = np.concatenate([R[c]["nps"] for c in ac])[None]
    return tuple(np.ascontiguousarray(a, dtype=np.float32) for a in (yp, ys, nkp, nvp, nkip, ncp, npp, nks, nvs, nkis, ncs, nps))


def kernel(**inputs):
    cfg = Cfg()
    nc = build(cfg)
    maps = make_in_maps(cfg, inputs, 8, 4)
    res = run_bass_kernel_spmd(nc, maps, core_ids=list(range(8)))
    return assemble(cfg, res.results, 8, 4)
```

```python
import numpy as np
from contextlib import ExitStack
import concourse.bass as bass
import concourse.mybir as mybir
from concourse.bass_utils import run_bass_kernel_spmd

F32 = mybir.dt.float32
BF16 = mybir.dt.bfloat16
U8 = mybir.dt.uint8
ALU = mybir.AluOpType
AF = mybir.ActivationFunctionType

D = 1024
NH = 8
A_IN = 4420
DS = 64
ALPHA = 8.0 ** 0.25
EPS = 1e-5
NEG = -1.0e30
MBIG = -30000.0
SCALE = 128.0 ** -0.5
ROPE_THETA = 500000.0


class Cfg:
    def __init__(self, SEQ=8192, NS=4, PAST=1024, topk_p=256, topk_s=256, ksteps=19):
        self.SEQ, self.NS, self.PAST = SEQ, NS, PAST
        self.topk_p, self.topk_s, self.ksteps = topk_p, topk_s, ksteps
        self.NTP = SEQ // 128
        self.NCT = PAST // 128
        self.TT = SEQ + NS * DS
        self.NKT = self.NTP + NS * (self.NCT + 1)
        self.KSW = PAST + DS


class SemC:
    def __init__(self, h):
        self.h = h
        self.n = 0


class Stream:
    def __init__(self, name, semc, serial):
        self.name, self.sem, self.serial = name, semc, serial
        self.ops = []
        self.seen = {}
        self.last = None

    def add(self, fn, waits=(), inc=None):
        ws = [w for w in waits if w is not None]
        if self.serial and self.last is not None:
            ws.append(self.last)
        if inc is None:
            self.sem.n += 1
            tok = (self.sem, self.sem.n)
            spec = (self.sem, 1)
            if self.serial:
                self.last = tok
        else:
            semc, amt = inc
            semc.n += amt
            tok = (semc, semc.n)
            spec = (semc, amt)
        self.ops.append((fn, ws, spec))
        return tok

    def emit(self, eng):
        for fn, ws, spec in self.ops:
            best = {}
            for (s, v) in ws:
                if self.name == "pe" and s is self.sem:
                    continue
                if best.get(s, 0) < v:
                    best[s] = v
            for s, v in best.items():
                if self.seen.get(s, 0) >= v:
                    continue
                self.seen[s] = v
                eng.wait_ge(s.h, v)
            inst = fn(eng)
            inst.then_inc(spec[0].h, spec[1])
        self.ops = []


class Buf:
    def __init__(self, ap=None):
        self.ap = ap
        self.w = None
        self.rs = {}

    def __getitem__(self, k):
        return self.ap[k]


class K:
    def __init__(self, nc, es):
        self.nc, self.es = nc, es
        mk = lambda n: SemC(es.enter_context(nc.semaphore(n)))
        self.pe = Stream("pe", mk("s_pe"), False)
        self.act = Stream("act", mk("s_act"), True)
        self.dve = Stream("dve", mk("s_dve"), True)
        self.pool = Stream("pool", mk("s_pool"), True)
        self.sp = Stream("sp", mk("s_sp"), False)
        self.stq = self.sp
        self.ldsem = [mk("ld%d" % i) for i in range(16)]
        self.stsem = [mk("st%d" % i) for i in range(16)]
        self.ldi = 0
        self.sti = 0
        self.lasttok = {}
        self.dram = {}

    def op(self, stream, fn, reads=(), writes=(), extra=(), inc=None):
        waits = list(extra)
        for b in reads:
            waits.append(b.w)
        for b in writes:
            waits.append(b.w)
            waits.extend(b.rs.items())
        tok = stream.add(fn, waits, inc)
        for b in reads:
            s, v = tok
            if b.rs.get(s, 0) < v:
                b.rs[s] = v
        for b in writes:
            b.w = tok
            b.rs = {}
        return tok

    def dbuf(self, key):
        if key not in self.dram:
            self.dram[key] = Buf()
        return self.dram[key]

    def load(self, out_ap, in_ap, reads=(), writes=(), **kw):
        semc = self.ldsem[self.ldi % len(self.ldsem)]
        self.ldi += 1
        prev = self.lasttok.get(semc)
        tok = self.op(self.sp, lambda e: e.dma_start(out=out_ap, in_=in_ap, **kw), reads, writes,
                      extra=[prev], inc=(semc, 16))
        self.lasttok[semc] = tok
        return tok

    def store(self, out_ap, in_ap, reads=(), writes=(), **kw):
        semc = self.stsem[self.sti % len(self.stsem)]
        self.sti += 1
        prev = self.lasttok.get(semc)
        tok = self.op(self.stq, lambda e: e.dma_start(out=out_ap, in_=in_ap, **kw), reads, writes,
                      extra=[prev], inc=(semc, 16))
        self.lasttok[semc] = tok
        return tok

    def flush(self):
        nc = self.nc
        with nc.Block() as block:
            @block.sync
            def _(e):
                self.sp.emit(e)

            @block.gpsimd
            def _(e):
                self.pool.emit(e)

            @block.scalar
            def _(e):
                self.act.emit(e)

            @block.vector
            def _(e):
                self.dve.emit(e)

            @block.tensor
            def _(e):
                self.pe.emit(e)

    def barrier_tokens(self):
        toks = []
        for s in (self.pe, self.act, self.dve, self.pool):
            if s.sem.n:
                toks.append((s.sem, s.sem.n))
        for semc in self.ldsem + self.stsem:
            if semc.n:
                toks.append((semc, semc.n))
        return toks


def build(cfg):
    nc = bass.Bass("TRN2", target_bir_lowering=False)
    SEQ, NS, PAST, NTP, NCT, TT, NKT = cfg.SEQ, cfg.NS, cfg.PAST, cfg.NTP, cfg.NCT, cfg.TT, cfg.NKT
    NSR = NS * DS

    def din(name, shape, dt=F32):
        return nc.dram_tensor(name, list(shape), dt, kind="ExternalInput").ap()

    def dout(name, shape):
        return nc.dram_tensor(name, list(shape), F32, kind="ExternalOutput").ap()

    def dscr(name, shape, dt):
        if getattr(cfg, "debug", False) and dt == F32:
            return nc.dram_tensor(name, list(shape), dt, kind="ExternalOutput").ap()
        return nc.dram_tensor(name, list(shape), dt).ap()

    I = dict(
        xp=din("xp", [SEQ, D]), xs=din("xs", [NSR, D]),
        ck=din("ck", [2, NS, PAST, D]), cv=din("cv", [2, NS, PAST, D]), cki=din("cki", [2, NS, PAST, 64]),
        sconv=din("sconv", [NS, 30, D]), spool=din("spool", [NS, 15, D]),
        w_in_a=din("w_in_a", [2, D, A_IN]), w_out_a=din("w_out_a", [2, D, D]),
        w_in_b=din("w_in_b", [D, 3 * D]), conv_w=din("conv_w", [31, D]), conv_b=din("conv_b", [D]),
        ng=din("ng", [D]), nb=din("nb", [D]), w_out_b=din("w_out_b", [D, D]),
        w_in_c=din("w_in_c", [D, 2 * D]), w_grp=din("w_grp", [4, 256, 256]), scale_c=din("scale_c", [D]),
        w_out_c=din("w_out_c", [D, D]), ln_g=din("ln_g", [4, D]), ln_b=din("ln_b", [4, D]),
        rope=din("rope", [TT, 48]),
    )
    O = dict(
        yp=dout("yp", [SEQ, D]), ys=dout("ys", [NSR, D]),
        nkp=dout("nkp", [2, SEQ, D]), nvp=dout("nvp", [2, SEQ, D]), nkip=dout("nkip", [2, SEQ, 64]),
        ncp=dout("ncp", [30, D]), npp=dout("npp", [15, D]),
        nks=dout("nks", [2, NSR, D]), nvs=dout("nvs", [2, NSR, D]), nkis=dout("nkis", [2, NSR, 64]),
        ncs=dout("ncs", [NS, 30, D]), nps=dout("nps", [NS, 15, D]),
    )
    NT = NTP + NS
    SC = dict(
        xres=[None] + [dscr("xres%d" % l, [TT, D], F32) for l in (1, 2, 3)],
        QT=dscr("QT", [NT, 128, 1024], BF16), GT=dscr("GT", [NT, 128, 1024], BF16),
        QIT=dscr("QIT", [NT, 128, 256], BF16), SG=dscr("SG", [TT, 4], F32),
        KT=dscr("KT", [NKT, 128, 1024], BF16), VV=dscr("VV", [NKT, 128, 1024], BF16),
        KIP=dscr("KIP", [64, SEQ], BF16), KIS=dscr("KIS", [NS, 64, PAST + 128], BF16),
    )

    def tile_rows(i):
        if i < NTP:
            return i * 128, 128
        return SEQ + (i - NTP) * DS, DS

    def xsrc(l, i):
        r0, nt = tile_rows(i)
        if l == 0:
            if i < NTP:
                return I["xp"][r0:r0 + nt, :]
            return I["xs"][r0 - SEQ:r0 - SEQ + nt, :]
        return SC["xres"][l][r0:r0 + nt, :]

    def xdst(l, i):
        r0, nt = tile_rows(i)
        if l == 3:
            if i < NTP:
                return O["yp"][r0:r0 + nt, :]
            return O["ys"][r0 - SEQ:r0 - SEQ + nt, :]
        return SC["xres"][l + 1][r0:r0 + nt, :]

    with ExitStack() as es0:
        k = K(nc, es0)
        op, pe, act, dve, pool = k.op, k.pe, k.act, k.dve, k.pool

        uid = [0]

        def sb(es, name, shape, dt):
            uid[0] += 1
            return Buf(es.enter_context(nc.sbuf_tensor("%s_%d" % (name, uid[0]), list(shape), dt)))

        def psb(es, name, shape, dt):
            uid[0] += 1
            return Buf(es.enter_context(nc.psum_tensor("%s_%d" % (name, uid[0]), list(shape), dt)))

        identb = sb(es0, "identb", [128, 128], BF16)
        identf = sb(es0, "identf", [128, 128], F32)
        onesb = sb(es0, "onesb", [128, 128], BF16)
        onesf = sb(es0, "onesf", [128, 128], F32)
        invc = sb(es0, "invc", [128, 4, 16], F32)
        for ib in (identb, identf):
            op(pool, lambda e, ib=ib: e.memset(ib.ap[:], 1.0), writes=[ib])
            op(pool, lambda e, ib=ib: e.affine_select(ib.ap[:], ib.ap[:], [[-1, 128]], ALU.is_equal, 0.0,
                                                        base=0, channel_multiplier=1), writes=[ib])
        op(pool, lambda e: e.memset(onesb.ap[:], 1.0), writes=[onesb])
        op(pool, lambda e: e.memset(onesf.ap[:], 1.0), writes=[onesf])
        iot = sb(es0, "iot", [128, 16], F32)
        op(pool, lambda e: e.iota(iot.ap[:], [[1, 16]], base=1, channel_multiplier=0,
                                  allow_small_or_imprecise_dtypes=True), writes=[iot])
        for g in range(4):
            op(dve, lambda e, g=g: e.tensor_scalar(invc.ap[:, g, :], iot.ap[:], float(2 ** (g + 1)), None, ALU.min),
               reads=[iot], writes=[invc])
        op(dve, lambda e: e.reciprocal(invc.ap[:], invc.ap[:]), writes=[invc])

        def phase_barrier():
            toks = k.barrier_tokens()
            for s in (k.pe, k.act, k.dve, k.pool, k.sp):
                s.add(lambda e: e.nop(), toks)

        def load_weight_bf16(es, wtile, src_ap, ncols, stage, pieces=4):
            cw = (ncols + pieces - 1) // pieces
            cnt = 0
            for kk in range(8):
                for pc in range(pieces):
                    c0 = pc * cw
                    c1 = min(ncols, c0 + cw)
                    if c0 >= c1:
                        continue
                    st = stage[cnt % len(stage)]
                    cnt += 1
                    k.load(st.ap[:, 0:c1 - c0], src_ap[kk * 128:(kk + 1) * 128, c0:c1], writes=[st])
                    eng = act if cnt % 2 else dve
                    if eng is act:
                        op(act, lambda e, st=st, kk=kk, c0=c0, c1=c1: e.activation(wtile.ap[:, kk, c0:c1], st.ap[:, 0:c1 - c0], AF.Copy),
                           reads=[st], writes=[wtile])
                    else:
                        op(dve, lambda e, st=st, kk=kk, c0=c0, c1=c1: e.tensor_copy(wtile.ap[:, kk, c0:c1], st.ap[:, 0:c1 - c0]),
                           reads=[st], writes=[wtile])

        def make_xT(xt, xb, pstr, xT, nt):
            op(act, lambda e: e.activation(xb.ap[:nt, :], xt.ap[:nt, :], AF.Copy), reads=[xt], writes=[xb])
            for kk in range(8):
                op(pe, lambda e, kk=kk: e.transpose(pstr.ap[:, kk, :nt], xb.ap[:nt, kk * 128:(kk + 1) * 128], identb.ap[:nt, :nt]),
                   reads=[xb, identb], writes=[pstr] if kk == 0 else [], extra=[pstr.w] if kk else [])
            pstr.w = (pe.sem, pe.sem.n)
            op(dve, lambda e: e.tensor_copy(xT.ap[:, :, :nt], pstr.ap[:, :, :nt]), reads=[pstr], writes=[xT])

        def post(l, i, aT, wout, xt, ps_y, z, stats, mv, rstd, xn, gbc, bbc):
            r0, nt = tile_rows(i)
            for half in range(2):
                for kk in range(8):
                    op(pe, lambda e, half=half, kk=kk: e.matmul(ps_y[half].ap[:nt, :], aT.ap[:, kk, :nt],
                                                              wout.ap[:, kk, half * 512:(half + 1) * 512],
                                                              start=(kk == 0), stop=(kk == 7)),
                       reads=[aT, wout], writes=[ps_y[half]] if kk == 0 else [])
                ps_y[half].w = (pe.sem, pe.sem.n)
                op(dve, lambda e, half=half: e.scalar_tensor_tensor(z.ap[:nt, half * 512:(half + 1) * 512],
                                                                    xt.ap[:nt, half * 512:(half + 1) * 512], ALPHA,
                                                                    ps_y[half].ap[:nt, :], ALU.mult, ALU.add),
                   reads=[xt, ps_y[half]], writes=[z])
                op(dve, lambda e, half=half: e.bn_stats(stats.ap[:nt, half, :], z.ap[:nt, half * 512:(half + 1) * 512]),
                   reads=[z], writes=[stats])
            op(dve, lambda e: e.bn_aggr(mv.ap[:nt, :], stats.ap[:nt, :, :]), reads=[stats], writes=[mv])
            op(act, lambda e: e.activation(rstd.ap[:nt, :], mv.ap[:nt, 1:2], AF.Sqrt, bias=epsb.ap[:nt, :], scale=1.0),
               reads=[mv, epsb], writes=[rstd])
            op(dve, lambda e: e.reciprocal(rstd.ap[:nt, :], rstd.ap[:nt, :]), writes=[rstd])
            op(dve, lambda e: e.tensor_scalar(xn.ap[:nt, :], z.ap[:nt, :], mv.ap[:nt, 0:1], rstd.ap[:nt, 0:1],
                                              ALU.subtract, ALU.mult), reads=[z, mv, rstd], writes=[xn])
            op(dve, lambda e: e.tensor_tensor(xn.ap[:nt, :], xn.ap[:nt, :], gbc.ap[:nt, :], ALU.mult), reads=[gbc], writes=[xn])
            op(dve, lambda e: e.tensor_tensor(xn.ap[:nt, :], xn.ap[:nt, :], bbc.ap[:nt, :], ALU.add), reads=[bbc], writes=[xn])
            db = k.dbuf(("x", l + 1, i))
            k.store(xdst(l, i), xn.ap[:nt, :], reads=[xn], writes=[db])

        epsb = sb(es0, "epsb", [128, 1], F32)
        op(pool, lambda e: e.memset(epsb.ap[:], EPS), writes=[epsb])

        def load_ln(es, l):
            gbc = sb(es, "gbc", [128, D], F32)
            bbc = sb(es, "bbc", [128, D], F32)
            k.load(gbc.ap[:], I["ln_g"][l, :].partition_broadcast(128), writes=[gbc])
            k.load(bbc.ap[:], I["ln_b"][l, :].partition_broadcast(128), writes=[bbc])
            return gbc, bbc

        def post_bufs(es):
            return dict(
                z=sb(es, "z", [128, D], F32), stats=sb(es, "stats", [128, 2, 6], F32), mv=sb(es, "mv", [128, 2], F32),
                rstd=sb(es, "rstd", [128, 1], F32), xn=sb(es, "xn", [128, D], F32))

        def layer_A(l, j):
            topk_of = lambda i: cfg.topk_p if i < NTP else cfg.topk_s
            with ExitStack() as es:
                win = sb(es, "win", [128, 8, A_IN], BF16)
                stage = [sb(es, "wst%d" % s, [128, 1105], F32) for s in range(3)]
                load_weight_bf16(es, win, I["w_in_a"][j], A_IN, stage, pieces=4)
                xt = [sb(es, "xt%d" % s, [128, D], F32) for s in range(2)]
                rp = [sb(es, "rp%d" % s, [128, 48], F32) for s in range(2)]
                xb = sb(es, "xb", [128, D], BF16)
                xT = sb(es, "xT", [128, 8, 128], BF16)
                hbuf = [sb(es, "h%d" % s, [128, A_IN], F32) for s in range(2)]
                tmp = [sb(es, "rt%d" % s, [128, 16, 16], F32) for s in range(4)]
                hqk = sb(es, "hqk", [128, 2048], BF16)
                vb = sb(es, "vb", [128, D], BF16)
                gs = sb(es, "gs", [128, D], BF16)
                qib = sb(es, "qib", [128, 320], BF16)
                aw = sb(es, "aw", [128, 4], F32)
                sg = sb(es, "sg", [128, 4], F32)
                qT = sb(es, "qT", [128, 8, 128], BF16)
                kT = sb(es, "kT", [128, 8, 128], BF16)
                gT = sb(es, "gT", [128, 8, 128], BF16)
                qiT = sb(es, "qiT", [128, 2, 128], BF16)
                kiT = sb(es, "kiT", [64, 128], BF16)
                ckf = [sb(es, "ckf%d" % s, [128, D], F32) for s in range(2)]
                cvf = [sb(es, "cvf%d" % s, [128, D], F32) for s in range(2)]
                ckif = [sb(es, "ckif%d" % s, [128, 64], F32) for s in range(2)]
                ckb = sb(es, "ckb", [128, D], BF16)
                ckib = sb(es, "ckib", [128, 64], BF16)
                psp = [psb(es, "psp%d" % s, [128, 512], F32) for s in range(5)]
                pstr = psb(es, "pstr", [128, 8, 128], BF16)
                pst2 = [psb(es, "pst2%d" % s, [128, 8, 128], BF16) for s in range(2)]
                chunks = [(c * 512, min(A_IN, (c + 1) * 512)) for c in range(9)]

                def rope_block(hv, c, s, nt, nh, half, rb, h):
                    x1 = hv[:, :, 0:half]
                    x2 = hv[:, :, half:2 * half]
                    cb_ = c.unsqueeze(1).to_broadcast([nt, nh, half])
                    sb_ = s.unsqueeze(1).to_broadcast([nt, nh, half])
                    t = [tt.ap[:nt, 0:nh, 0:half] for tt in tmp]
                    op(dve, lambda e: e.tensor_tensor(t[0], x1, cb_, ALU.mult), reads=[h, rb], writes=[tmp[0]])
                    op(dve, lambda e: e.tensor_tensor(t[1], x2, sb_, ALU.mult), reads=[h, rb], writes=[tmp[1]])
                    op(dve, lambda e: e.tensor_tensor(t[2], x1, sb_, ALU.mult), reads=[h, rb], writes=[tmp[2]])
                    op(dve, lambda e: e.tensor_tensor(t[3], x2, cb_, ALU.mult), reads=[h, rb], writes=[tmp[3]])
                    op(dve, lambda e: e.tensor_tensor(x1, t[0], t[1], ALU.subtract), reads=[tmp[0], tmp[1]], writes=[h])
                    op(dve, lambda e: e.tensor_tensor(x2, t[3], t[2], ALU.add), reads=[tmp[2], tmp[3]], writes=[h])

                def loads(i):
                    r0, nt = tile_rows(i)
                    k.load(xt[i % 2].ap[:nt, :], xsrc(l, i), reads=[k.dbuf(("x", l, i))], writes=[xt[i % 2]])
                    k.load(rp[i % 2].ap[:nt, :], I["rope"][r0:r0 + nt, :], writes=[rp[i % 2]])

                def computeP1(i):
                    r0, nt = tile_rows(i)
                    x_, rp_ = xt[i % 2], rp[i % 2]
                    h = hbuf[i % 2]
                    make_xT(x_, xb, pstr, xT, nt)
                    for ci, (c0, c1) in enumerate(chunks):
                        ps = psp[ci % 5]
                        for kk in range(8):
                            op(pe, lambda e, ps=ps, kk=kk, c0=c0, c1=c1: e.matmul(ps.ap[:nt, 0:c1 - c0], xT.ap[:, kk, :nt], win.ap[:, kk, c0:c1],
                                                                                  start=(kk == 0), stop=(kk == 7)),
                               reads=[xT, win], writes=[ps] if kk == 0 else [])
                        ps.w = (pe.sem, pe.sem.n)
                        if ci % 2 == 0:
                            op(act, lambda e, ps=ps, c0=c0, c1=c1: e.activation(h.ap[:nt, c0:c1], ps.ap[:nt, 0:c1 - c0], AF.Copy),
                               reads=[ps], writes=[h])
                        else:
                            op(dve, lambda e, ps=ps, c0=c0, c1=c1: e.tensor_copy(h.ap[:nt, c0:c1], ps.ap[:nt, 0:c1 - c0]),
                               reads=[ps], writes=[h])

                def computeP2(i):
                    r0, nt = tile_rows(i)
                    x_, rp_ = xt[i % 2], rp[i % 2]
                    h = hbuf[i % 2]
                    hv = h.ap[:nt, 0:2048].rearrange("p (h d) -> p h d", d=128)
                    rope_block(hv, rp_.ap[:nt, 0:16], rp_.ap[:nt, 16:32], nt, 16, 16, rp_, h)
                    hv2 = h.ap[:nt, 4096:4416].rearrange("p (h d) -> p h d", d=64)
                    rope_block(hv2, rp_.ap[:nt, 32:40], rp_.ap[:nt, 40:48], nt, 5, 8, rp_, h)
                    if i < NTP:
                        ko, vo, kio = O["nkp"][j, r0:r0 + nt, :], O["nvp"][j, r0:r0 + nt, :], O["nkip"][j, r0:r0 + nt, :]
                    else:
                        q0 = r0 - SEQ
                        ko, vo, kio = O["nks"][j, q0:q0 + nt, :], O["nvs"][j, q0:q0 + nt, :], O["nkis"][j, q0:q0 + nt, :]
                    k.store(ko, h.ap[:nt, 1024:2048], reads=[h])
                    k.store(vo, h.ap[:nt, 2048:3072], reads=[h])
                    k.store(kio, h.ap[:nt, 4352:4416], reads=[h])
                    op(dve, lambda e: e.tensor_copy(hqk.ap[:nt, :], h.ap[:nt, 0:2048]), reads=[h], writes=[hqk])
                    op(act, lambda e: e.activation(vb.ap[:nt, :], h.ap[:nt, 2048:3072], AF.Copy), reads=[h], writes=[vb])
                    op(act, lambda e: e.activation(gs.ap[:nt, :], h.ap[:nt, 3072:4096], AF.Silu), reads=[h], writes=[gs])
                    op(dve, lambda e: e.tensor_scalar(sg.ap[:nt, :], h.ap[:nt, 4416:4420], 0.0, 2.0, ALU.is_ge, ALU.mult),
                       reads=[h], writes=[sg])
                    op(dve, lambda e: e.tensor_scalar(sg.ap[:nt, :], sg.ap[:nt, :], -1.0, None, ALU.add), writes=[sg])
                    op(dve, lambda e: e.scalar_tensor_tensor(aw.ap[:nt, :], h.ap[:nt, 4416:4420], 0.0625, sg.ap[:nt, :], ALU.mult, ALU.mult),
                       reads=[h, sg], writes=[aw])
                    for hh in range(4):
                        op(dve, lambda e, hh=hh: e.tensor_scalar(qib.ap[:nt, hh * 64:(hh + 1) * 64], h.ap[:nt, 4096 + hh * 64:4096 + (hh + 1) * 64],
                                                                 aw.ap[:nt, hh:hh + 1], None, ALU.mult), reads=[h, aw], writes=[qib])
                    op(dve, lambda e: e.tensor_copy(qib.ap[:nt, 256:320], h.ap[:nt, 4352:4416]), reads=[h], writes=[qib])
                    def trn(ps, src, ncol, dst, dsl, evac):
                        nblk = (ncol + 127) // 128
                        for b in range(nblk):
                            w = min(128, ncol - b * 128)
                            op(pe, lambda e, b=b, w=w: e.transpose(ps.ap[0:w, b, :nt], src.ap[:nt, b * 128:b * 128 + w], identb.ap[:nt, :nt]),
                               reads=[src, identb], writes=[ps] if b == 0 else [])
                        ps.w = (pe.sem, pe.sem.n)
                    trn(pst2[0], hqk, 1024, None, None, None)
                    op(act, lambda e: e.activation(qT.ap[:, :, :nt], pst2[0].ap[:, :, :nt], AF.Copy), reads=[pst2[0]], writes=[qT])
                    for b in range(8):
                        op(pe, lambda e, b=b: e.transpose(pst2[1].ap[:, b, :nt], hqk.ap[:nt, 1024 + b * 128:1024 + (b + 1) * 128], identb.ap[:nt, :nt]),
                           reads=[hqk, identb], writes=[pst2[1]] if b == 0 else [])
                    pst2[1].w = (pe.sem, pe.sem.n)
                    op(dve, lambda e: e.tensor_copy(kT.ap[:, :, :nt], pst2[1].ap[:, :, :nt]), reads=[pst2[1]], writes=[kT])
                    trn(pst2[0], gs, 1024, None, None, None)
                    op(act, lambda e: e.activation(gT.ap[:, :, :nt], pst2[0].ap[:, :, :nt], AF.Copy), reads=[pst2[0]], writes=[gT])
                    trn(pst2[1], qib, 320, None, None, None)
                    op(dve, lambda e: e.tensor_copy(qiT.ap[:, :, :nt], pst2[1].ap[:, 0:2, :nt]), reads=[pst2[1]], writes=[qiT])
                    op(dve, lambda e: e.tensor_copy(kiT.ap[:, :nt], pst2[1].ap[0:64, 2, :nt]), reads=[pst2[1]], writes=[kiT])
                    kidx = i if i < NTP else NTP + (i - NTP) * (NCT + 1) + NCT
                    k.store(SC["QT"][i].rearrange("p (h t) -> p h t", t=128)[:, :, :nt], qT.ap[:, :, :nt], reads=[qT], writes=[k.dbuf(("QT", i))])
                    k.store(SC["GT"][i].rearrange("p (h t) -> p h t", t=128)[:, :, :nt], gT.ap[:, :, :nt], reads=[gT], writes=[k.dbuf(("GT", i))])
                    k.store(SC["QIT"][i].rearrange("p (h t) -> p h t", t=128)[:, :, :nt], qiT.ap[:, :, :nt], reads=[qiT], writes=[k.dbuf(("QIT", i))])
                    k.store(SC["KT"][kidx].rearrange("p (h t) -> p h t", t=128)[:, :, :nt], kT.ap[:, :, :nt], reads=[kT], writes=[k.dbuf(("KT", kidx))])
                    k.store(SC["VV"][kidx][:nt, :], vb.ap[:nt, :], reads=[vb], writes=[k.dbuf(("VV", kidx))])
                    k.store(SC["SG"][r0:r0 + nt, :], sg.ap[:nt, :], reads=[sg], writes=[k.dbuf(("SG", i))])
                    if i < NTP:
                        k.store(SC["KIP"][:, r0:r0 + nt], kiT.ap[:, :nt], reads=[kiT], writes=[k.dbuf(("KIP",))])
                    else:
                        k.store(SC["KIS"][i - NTP][:, PAST:PAST + nt], kiT.ap[:, :nt], reads=[kiT], writes=[k.dbuf(("KIS", i - NTP))])

                def cache_tile(s, c):
                    n = s * NCT + c
                    kf, vf, kif = ckf[n % 2], cvf[n % 2], ckif[n % 2]
                    k.load(kf.ap[:], I["ck"][j, s, c * 128:(c + 1) * 128, :], writes=[kf])
                    k.load(vf.ap[:], I["cv"][j, s, c * 128:(c + 1) * 128, :], writes=[vf])
                    k.load(kif.ap[:], I["cki"][j, s, c * 128:(c + 1) * 128, :], writes=[kif])
                    kidx = NTP + s * (NCT + 1) + c
                    op(act, lambda e: e.activation(ckb.ap[:], kf.ap[:], AF.Copy), reads=[kf], writes=[ckb])
                    for b in range(8):
                        op(pe, lambda e, b=b: e.transpose(pst2[0].ap[:, b, :], ckb.ap[:, b * 128:(b + 1) * 128], identb.ap[:]),
                           reads=[ckb, identb], writes=[pst2[0]] if b == 0 else [])
                    pst2[0].w = (pe.sem, pe.sem.n)
                    op(dve, lambda e: e.tensor_copy(kT.ap[:], pst2[0].ap[:]), reads=[pst2[0]], writes=[kT])
                    k.store(SC["KT"][kidx].rearrange("p (h t) -> p h t", t=128), kT.ap[:], reads=[kT], writes=[k.dbuf(("KT", kidx))])
                    op(act, lambda e: e.activation(vb.ap[:], vf.ap[:], AF.Copy), reads=[vf], writes=[vb])
                    k.store(SC["VV"][kidx], vb.ap[:], reads=[vb], writes=[k.dbuf(("VV", kidx))])
                    op(act, lambda e: e.activation(ckib.ap[:], kif.ap[:], AF.Copy), reads=[kif], writes=[ckib])
                    op(pe, lambda e: e.transpose(pst2[1].ap[0:64, 0, :], ckib.ap[:, :], identb.ap[:]), reads=[ckib, identb], writes=[pst2[1]])
                    op(dve, lambda e: e.tensor_copy(kiT.ap[:, :], pst2[1].ap[0:64, 0, :]), reads=[pst2[1]], writes=[kiT])
                    k.store(SC["KIS"][s][:, c * 128:(c + 1) * 128], kiT.ap[:, :], reads=[kiT], writes=[k.dbuf(("KIS", s))])

                loads(0)
                if NT > 1:
                    loads(1)
                computeP1(0)
                for i in range(NT):
                    if i + 1 < NT:
                        computeP1(i + 1)
                    computeP2(i)
                    if i + 2 < NT:
                        loads(i + 2)
                for s in range(NS):
                    for c in range(NCT):
                        cache_tile(s, c)
                phase_barrier()
                k.flush()

            if getattr(cfg, "skip_p2", False):
                return
            with ExitStack() as es:
                NMAX = max(SEQ, PAST + 128)
                wout = sb(es, "wout", [128, 8, D], BF16)
                stage = [sb(es, "wst%d" % s, [128, D], F32) for s in range(2)]
                load_weight_bf16(es, wout, I["w_out_a"][j], D, stage, pieces=1)
                gbc, bbc = load_ln(es, l)
                pb = post_bufs(es)
                ki2 = sb(es, "ki2", [128, NMAX], BF16)
                kis = sb(es, "kis", [128, PAST + 128], BF16)
                isc = sb(es, "isc", [128, NMAX], F32)
                junk = sb(es, "junk", [128, NMAX], U8)
                mb = [sb(es, "mb%d" % s, [128, 512], F32) for s in range(2)]
                maskT = sb(es, "maskT", [128, NMAX // 128 + 1, 128], BF16)
                rl = [sb(es, "rl%d" % s, [128, 512], F32) for s in range(2)]
                qT = [sb(es, "qT%d" % s, [128, 8, 128], BF16) for s in range(2)]
                gT = [sb(es, "gT%d" % s, [128, 8, 128], BF16) for s in range(2)]
                qiT = [sb(es, "qiT%d" % s, [128, 2, 128], BF16) for s in range(2)]
                sg = [sb(es, "sg%d" % s, [128, 4], F32) for s in range(2)]
                xt = [sb(es, "xt%d" % s, [128, D], F32) for s in range(2)]
                NKB = 4
                ktb = [sb(es, "ktb%d" % s, [128, 8, 128], BF16) for s in range(NKB)]
                vtb = [sb(es, "vtb%d" % s, [128, D], BF16) for s in range(NKB)]
                pT = [sb(es, "pT%d" % s, [128, 4, 128], BF16) for s in range(4)]
                oT = sb(es, "oT", [128, 8, 128], BF16)
                rcp = sb(es, "rcp", [128, 4, 128], F32)
                otmp = sb(es, "otmp", [128, 4, 128], F32)
                probe = sb(es, "probe", [128, 1], F32)
                cnt = sb(es, "cnt", [128, 1], F32)
                inc = sb(es, "inc", [128, 1], F32)
                thr = sb(es, "thr", [128, 1], F32)
                ps_d = [psb(es, "ps_d%d" % s, [128, 512], F32) for s in range(2)]
                ps_s = [psb(es, "ps_s%d" % s, [128, 4, 128], F32) for s in range(2)]
                ps_o = [psb(es, "ps_o%d" % s, [128, 4, 128], F32) for s in range(2)]
                ps_l = [psb(es, "ps_l%d" % s, [128, 4, 128], F32) for s in range(2)]
                sbank = [ps_s[0], ps_s[1], ps_d[0], ps_d[1]]
                sview = [ps_s[0].ap, ps_s[1].ap, ps_d[0].ap.rearrange("p (h q) -> p h q", q=128), ps_d[1].ap.rearrange("p (h q) -> p h q", q=128)]
                kipb = k.dbuf(("KIP",))
                k.load(ki2.ap[0:64, 0:SEQ], SC["KIP"][:, :], reads=[kipb], writes=[ki2])
                k.load(ki2.ap[64:128, 0:SEQ], SC["KIP"][:, :], reads=[kipb], writes=[ki2])
                kvcount = [0]

                def loads(i):
                    r0, nt = tile_rows(i)
                    s2 = i % 2
                    k.load(qT[s2].ap[:, :, :nt], SC["QT"][i].rearrange("p (h t) -> p h t", t=128)[:, :, :nt], reads=[k.dbuf(("QT", i))], writes=[qT[s2]])
                    k.load(gT[s2].ap[:, :, :nt], SC["GT"][i].rearrange("p (h t) -> p h t", t=128)[:, :, :nt], reads=[k.dbuf(("GT", i))], writes=[gT[s2]])
                    k.load(qiT[s2].ap[:, :, :nt], SC["QIT"][i].rearrange("p (h t) -> p h t", t=128)[:, :, :nt], reads=[k.dbuf(("QIT", i))], writes=[qiT[s2]])
                    k.load(sg[s2].ap[:nt, :], SC["SG"][r0:r0 + nt, :], reads=[k.dbuf(("SG", i))], writes=[sg[s2]])
                    k.load(xt[s2].ap[:nt, :], xsrc(l, i), reads=[k.dbuf(("x", l, i))], writes=[xt[s2]])

                def tinfo(i):
                    r0, nq = tile_rows(i)
                    if i < NTP:
                        n = 128 * (i + 1)
                        ktiles = [(t, 128) for t in range(i + 1)]
                        kib = ki2
                    else:
                        s = i - NTP
                        n = PAST + DS
                        base = NTP + s * (NCT + 1)
                        ktiles = [(base + c, 128) for c in range(NCT)] + [(base + NCT, DS)]
                        kib = kis
                    return r0, nq, i % 2, topk_of(i), n, ktiles, kib

                def stageA1(i):
                    r0, nq, s2, topk, n, ktiles, kib = tinfo(i)
                    if i >= NTP:
                        s = i - NTP
                        ksb = k.dbuf(("KIS", s))
                        k.load(kis.ap[0:64, 0:n], SC["KIS"][s][:, 0:n], reads=[ksb], writes=[kis])
                        k.load(kis.ap[64:128, 0:n], SC["KIS"][s][:, 0:n], reads=[ksb], writes=[kis])
                    for c0 in range(0, n, 512):
                        wk = min(512, n - c0)
                        for hh in range(4):
                            b0 = 64 * (hh % 2)
                            ps = ps_d[hh % 2]
                            op(pe, lambda e, ps=ps, hh=hh, b0=b0, c0=c0, wk=wk: e.matmul(ps.ap[:nq, 0:wk], qiT[s2].ap[b0:b0 + 64, hh // 2, :nq],
                                                                                          kib.ap[b0:b0 + 64, c0:c0 + wk], start=True, stop=True),
                               reads=[qiT[s2], kib], writes=[ps])
                            r_ = rl[hh % 2]
                            op(act, lambda e, ps=ps, r_=r_, wk=wk: e.activation(r_.ap[:nq, 0:wk], ps.ap[:nq, 0:wk], AF.Relu), reads=[ps], writes=[r_])
                            if hh == 0:
                                op(dve, lambda e, r_=r_, c0=c0, wk=wk: e.tensor_scalar(isc.ap[:nq, c0:c0 + wk], r_.ap[:nq, 0:wk], sg[s2].ap[:nq, 0:1], None, ALU.mult),
                                   reads=[r_, sg[s2]], writes=[isc])
                            else:
                                op(dve, lambda e, r_=r_, c0=c0, wk=wk, hh=hh: e.scalar_tensor_tensor(isc.ap[:nq, c0:c0 + wk], r_.ap[:nq, 0:wk], sg[s2].ap[:nq, hh:hh + 1],
                                                                                                    isc.ap[:nq, c0:c0 + wk], ALU.mult, ALU.add),
                                   reads=[r_, sg[s2]], writes=[isc])
                    if i < NTP:
                        op(dve, lambda e: e.memset(isc.ap[0:64, n - 64:n], NEG), writes=[isc])
                    if i < NTP and n <= topk:
                        op(dve, lambda e: e.memset(thr.ap[:], -1.0e29), writes=[thr])
                        return
                    op(dve, lambda e: e.memset(probe.ap[:], 0.0), writes=[probe])
                    hw = 4.0
                    for st in range(cfg.ksteps):
                        op(dve, lambda e: e.tensor_scalar(junk.ap[:nq, 0:n], isc.ap[:nq, 0:n], probe.ap[:nq, 0:1], None, ALU.is_ge, ALU.add,
                                                          accum_out=cnt.ap[:nq, :]), reads=[isc, probe], writes=[junk, cnt])
                        op(dve, lambda e, hw=hw: e.tensor_scalar(inc.ap[:nq, :], cnt.ap[:nq, :], float(topk), hw, ALU.is_ge, ALU.mult),
                           reads=[cnt], writes=[inc])
                        op(dve, lambda e, hw=hw: e.scalar_tensor_tensor(probe.ap[:nq, :], inc.ap[:nq, :], -hw / 2, probe.ap[:nq, :], ALU.add, ALU.add),
                           reads=[inc], writes=[probe])
                        hw = hw / 2
                    op(dve, lambda e, hw=hw: e.tensor_scalar(thr.ap[:nq, :], probe.ap[:nq, :], -hw, None, ALU.add), reads=[probe], writes=[thr])

                def stageA2(i):
                    r0, nq, s2, topk, n, ktiles, kib = tinfo(i)
                    for c0 in range(0, n, 512):
                        wk = min(512, n - c0)
                        m_ = mb[(c0 // 512) % 2]
                        op(dve, lambda e, m_=m_, c0=c0, wk=wk: e.tensor_scalar(m_.ap[:nq, 0:wk], isc.ap[:nq, c0:c0 + wk], thr.ap[:nq, 0:1], MBIG, ALU.is_lt, ALU.mult),
                           reads=[isc, thr], writes=[m_])
                        pm = ps_s[(c0 // 512) % 2]
                        nb_ = (wk + 127) // 128
                        for b in range(nb_):
                            w = min(128, wk - b * 128)
                            op(pe, lambda e, pm=pm, m_=m_, b=b, w=w: e.transpose(pm.ap[0:w, b, :nq], m_.ap[:nq, b * 128:b * 128 + w], identf.ap[:nq, :nq]),
                               reads=[m_, identf], writes=[pm] if b == 0 else [])
                        pm.w = (pe.sem, pe.sem.n)
                        kt0 = c0 // 128
                        if wk % 128 == 0:
                            op(act, lambda e, pm=pm, kt0=kt0, nb_=nb_: e.activation(maskT.ap[:, kt0:kt0 + nb_, :nq], pm.ap[:, 0:nb_, :nq], AF.Copy),
                               reads=[pm], writes=[maskT])
                        else:
                            nf = wk // 128
                            if nf:
                                op(act, lambda e, pm=pm, kt0=kt0, nf=nf: e.activation(maskT.ap[:, kt0:kt0 + nf, :nq], pm.ap[:, 0:nf, :nq], AF.Copy),
                                   reads=[pm], writes=[maskT])
                            w = wk - nf * 128
                            op(act, lambda e, pm=pm, kt0=kt0, nf=nf, w=w: e.activation(maskT.ap[0:w, kt0 + nf, :nq], pm.ap[0:w, nf, :nq], AF.Copy),
                               reads=[pm], writes=[maskT])

                def stageB(i):
                    r0, nq, s2, topk, n, ktiles, kib = tinfo(i)
                    nk = len(ktiles)
                    kvs = {}

                    def emitS(ti):
                        kidx, ns = ktiles[ti]
                        slot = kvcount[0] % NKB
                        kvcount[0] += 1
                        kb, vbf = ktb[slot], vtb[slot]
                        kvs[ti] = (kb, vbf)
                        k.load(kb.ap[:, :, :ns], SC["KT"][kidx].rearrange("p (h t) -> p h t", t=128)[:, :, :ns], reads=[k.dbuf(("KT", kidx))], writes=[kb])
                        k.load(vbf.ap[:ns, :], SC["VV"][kidx][:ns, :], reads=[k.dbuf(("VV", kidx))], writes=[vbf])
                        for hg in range(2):
                            bi = (2 * ti + hg) % 4
                            pss = sbank[bi]
                            pv = sview[bi]
                            mbc_ = maskT.ap[:ns, ti, :nq].unsqueeze(1).to_broadcast([ns, 4, nq])
                            op(pe, lambda e, pv=pv, mbc_=mbc_, ns=ns: e.matmul(pv[:ns, :, :nq], identb.ap[:ns, :ns], mbc_, start=True, stop=False, skip_group_check=True),
                               reads=[maskT, identb], writes=[pss])
                            for h4 in range(4):
                                hh = hg * 4 + h4
                                op(pe, lambda e, pv=pv, hh=hh, h4=h4, kb=kb, ns=ns: e.matmul(pv[:ns, h4, :nq], kb.ap[:, hh, :ns], qT[s2].ap[:, hh, :nq], start=False, stop=(h4 == 3), skip_group_check=True),
                                   reads=[kb, qT[s2]])
                            pss.w = (pe.sem, pe.sem.n)
                            p_ = pT[bi]
                            op(act, lambda e, pv=pv, p_=p_, ns=ns: e.activation(p_.ap[:ns, :, :nq], pv[:ns, :, :nq], AF.Exp, scale=SCALE),
                               reads=[pss], writes=[p_])

                    def emitPV(ti):
                        kidx, ns = ktiles[ti]
                        kb, vbf = kvs.pop(ti)
                        for hg in range(2):
                            p_ = pT[(2 * ti + hg) % 4]
                            for h4 in range(4):
                                hh = hg * 4 + h4
                                op(pe, lambda e, hg=hg, hh=hh, h4=h4, vbf=vbf, p_=p_, ns=ns, ti=ti: e.matmul(ps_o[hg].ap[:, h4, :nq], vbf.ap[:ns, hh * 128:(hh + 1) * 128], p_.ap[:ns, h4, :nq],
                                                                                                         start=(ti == 0 and h4 == 0), stop=(ti == nk - 1), skip_group_check=True),
                                   reads=[vbf, p_], writes=[ps_o[hg]] if (ti == 0 and h4 == 0) else [])
                            op(pe, lambda e, hg=hg, p_=p_, ns=ns, ti=ti: e.matmul(ps_l[hg].ap[:, :, :nq], onesb.ap[:ns, :], p_.ap[:ns, :, :nq],
                                                                                  start=(ti == 0), stop=(ti == nk - 1), skip_group_check=True),
                               reads=[p_, onesb], writes=[ps_l[hg]] if ti == 0 else [])

                    emitS(0)
                    for ti in range(nk):
                        if ti + 1 < nk:
                            emitS(ti + 1)
                        emitPV(ti)
                    for hg in range(2):
                        ps_o[hg].w = (pe.sem, pe.sem.n)
                        ps_l[hg].w = (pe.sem, pe.sem.n)
                        op(dve, lambda e, hg=hg: e.reciprocal(rcp.ap[:, :, :nq], ps_l[hg].ap[:, :, :nq]), reads=[ps_l[hg]], writes=[rcp])
                        op(dve, lambda e, hg=hg: e.tensor_tensor(otmp.ap[:, :, :nq], ps_o[hg].ap[:, :, :nq], rcp.ap[:, :, :nq], ALU.mult),
                           reads=[ps_o[hg], rcp], writes=[otmp])
                        op(dve, lambda e, hg=hg: e.tensor_tensor(oT.ap[:, hg * 4:hg * 4 + 4, :nq], otmp.ap[:, :, :nq], gT[s2].ap[:, hg * 4:hg * 4 + 4, :nq], ALU.mult),
                           reads=[otmp, gT[s2]], writes=[oT])
                    post(l, i, oT, wout, xt[s2], ps_d, pb["z"], pb["stats"], pb["mv"], pb["rstd"], pb["xn"], gbc, bbc)

                loads(0)
                stageA1(0)
                stageA2(0)
                for i in range(NT):
                    if i + 1 < NT:
                        loads(i + 1)
                        stageA1(i + 1)
                    stageB(i)
                    if i + 1 < NT:
                        stageA2(i + 1)
                phase_barrier()
                k.flush()

        def layer_BC(l, kind):
            with ExitStack() as es:
                isB = kind == "B"
                NW = 3 * D if isB else 2 * D
                win = sb(es, "win", [128, 8, NW], BF16)
                wout = sb(es, "wout", [128, 8, D], BF16)
                stage = [sb(es, "wst%d" % s, [128, D], F32) for s in range(2)]
                load_weight_bf16(es, win, I["w_in_b"] if isB else I["w_in_c"], NW, stage, pieces=NW // D)
                load_weight_bf16(es, wout, I["w_out_b"] if isB else I["w_out_c"], D, stage, pieces=1)
                gbc, bbc = load_ln(es, l)
                pb = post_bufs(es)
                xt = [sb(es, "xt%d" % s, [128, D], F32) for s in range(2)]
                xbs = [sb(es, "xb%d" % q, [128, D], BF16) for q in range(2)]
                xTs = [sb(es, "xT%d" % q, [128, 8, 128], BF16) for q in range(2)]
                gs = sb(es, "gs", [128, 8, 128], BF16)
                aT = sb(es, "aT", [128, 8, 128], BF16)
                ut = pb["xn"]
                pstr = psb(es, "pstr", [128, 8, 128], BF16)
                psA = [psb(es, "psA%d" % s, [128, 4, 128], F32) for s in range(2)]
                psB = [psb(es, "psB%d" % s, [128, 4, 128], F32) for s in range(2)]
                psG = [psb(es, "psG%d" % s, [128, 4, 128], F32) for s in range(2)]
                psS = psb(es, "psS", [128, 2, 128], F32)
                HP = 30 if isB else 16
                if isB:
                    cw = sb(es, "cw", [128, 8, 31], F32)
                    cb = sb(es, "cb", [128, 8], F32)
                    ngp = sb(es, "ngp", [128, 8], F32)
                    nbp = sb(es, "nbp", [128, 8], F32)
                    Dg = sb(es, "Dg", [128, 8, 31, 128], BF16)
                    for c in range(8):
                        k.load(cw.ap[:, c, :], I["conv_w"][:, c * 128:(c + 1) * 128].rearrange("j p -> p j"), writes=[cw], allow_slow_non_contiguous=True)
                    k.load(cb.ap[:], I["conv_b"].rearrange("(c p) -> p c", p=128), writes=[cb], allow_slow_non_contiguous=True)
                    k.load(ngp.ap[:], I["ng"].rearrange("(c p) -> p c", p=128), writes=[ngp], allow_slow_non_contiguous=True)
                    k.load(nbp.ap[:], I["nb"].rearrange("(c p) -> p c", p=128), writes=[nbp], allow_slow_non_contiguous=True)
                    for c in range(8):
                        for jj in range(31):
                            op(dve, lambda e, c=c, jj=jj: e.tensor_scalar(Dg.ap[:, c, jj, :], identb.ap[:], cw.ap[:, c, jj:jj + 1], None, ALU.mult),
                               reads=[identb, cw], writes=[Dg])
                    sig = sb(es, "sig", [128, 8, 128], F32)
                    u32 = sb(es, "u32", [128, 8, 128], F32)
                    ext = [sb(es, "ext%d" % s, [128, 8, HP + 128], BF16) for s in range(2)]
                    cT = sb(es, "cT", [128, 8, 128], F32)
                    sq = sig
                    mean = sb(es, "mean", [128, 128], F32)
                    msq = sb(es, "msq", [128, 128], F32)
                    var = sb(es, "var", [128, 128], F32)
                    sn = sb(es, "sn", [128, 8, 128], BF16)
                    prevf = sb(es, "prevf", [32, D], F32)
                    prevb = sb(es, "prevb", [32, D], BF16)
                else:
                    wgf = sb(es, "wgf", [128, 2, 256], F32)
                    scb = sb(es, "scb", [128, D], F32)
                    wg = sb(es, "wg", [128, 4, 2, 256], BF16)
                    k.load(scb.ap[:], I["scale_c"].partition_broadcast(128), writes=[scb])
                    for g in range(4):
                        k.load(wgf.ap[:], I["w_grp"][g].rearrange("(c p) d -> p c d", p=128), writes=[wgf])
                        for cc in range(2):
                            op(dve, lambda e, g=g, cc=cc: e.tensor_tensor(wg.ap[:, g, cc, :], wgf.ap[:, cc, :], scb.ap[:, g * 256:(g + 1) * 256], ALU.mult),
                               reads=[wgf, scb], writes=[wg])
                    ext = [sb(es, "ext%d" % s, [128, 8, HP + 128], F32) for s in range(2)]
                    wa = sb(es, "wa", [128, 2, HP + 128], F32)
                    wb2 = sb(es, "wb2", [128, 2, HP + 128], F32)
                    dT = sb(es, "dT", [128, 8, 128], BF16)
                    ftmp = sb(es, "ftmp", [128, 2, 16], F32)
                    prevf = sb(es, "prevf", [32, D], F32)

                def loads(i):
                    r0, nt = tile_rows(i)
                    k.load(xt[i % 2].ap[:nt, :], xsrc(l, i), reads=[k.dbuf(("x", l, i))], writes=[xt[i % 2]])

                def mk_next(i):
                    r0, nt = tile_rows(i)
                    make_xT(xt[i % 2], xbs[i % 2], pstr, xTs[i % 2], nt)

                def proj(ps2, col0, nt, xT):
                    for fc in range(8):
                        ps = ps2[fc // 4]
                        for kk in range(8):
                            op(pe, lambda e, ps=ps, fc=fc, kk=kk: e.matmul(ps.ap[:, fc % 4, :nt], win.ap[:, kk, col0 + fc * 128:col0 + (fc + 1) * 128], xT.ap[:, kk, :nt],
                                                                         start=(kk == 0), stop=(kk == 7)),
                               reads=[win, xT], writes=[ps] if (fc % 4 == 0 and kk == 0) else [])
                        if fc % 4 == 3:
                            ps.w = (pe.sem, pe.sem.n)

                def state_out(src32, nt, nrows, dst):
                    for c in range(8):
                        op(pe, lambda e, c=c: e.transpose(psG[c // 4].ap[:nt, c % 4, :], src32[:, c, 0:nt], identf.ap[:, :]),
                           reads=[identf], writes=[psG[c // 4]] if c % 4 == 0 else [], extra=[srcbuf[0].w])
                        if c % 4 == 3:
                            psG[c // 4].w = (pe.sem, pe.sem.n)
                    for hf in range(2):
                        op(act, lambda e, hf=hf: e.activation(ut.ap[:nt, hf * 512:(hf + 1) * 512], psG[hf].ap[:nt, :, :], AF.Copy), reads=[psG[hf]], writes=[ut])
                    k.store(dst, ut.ap[nt - nrows:nt, :], reads=[ut])

                srcbuf = [None]

                def compute(i):
                    r0, nt = tile_rows(i)
                    x_ = xt[i % 2]
                    e_cur = ext[i % 2]
                    e_prev = ext[(i + 1) % 2]
                    xT = xTs[i % 2]
                    if i == 0:
                        op(dve, lambda e: e.memset(e_cur.ap[:, :, 0:HP], 0.0), writes=[e_cur])
                    elif i < NTP:
                        op(dve, lambda e: e.tensor_copy(e_cur.ap[:, :, 0:HP], e_prev.ap[:, :, 128:128 + HP]), reads=[e_prev], writes=[e_cur])
                    else:
                        s = i - NTP
                        if isB:
                            k.load(prevf.ap[0:30, :], I["sconv"][s], writes=[prevf])
                            op(act, lambda e: e.activation(prevb.ap[0:30, :], prevf.ap[0:30, :], AF.Copy), reads=[prevf], writes=[prevb])
                            for c in range(8):
                                op(pe, lambda e, c=c: e.transpose(pstr.ap[:, c, 0:30], prevb.ap[0:30, c * 128:(c + 1) * 128], identb.ap[0:30, 0:30]),
                                   reads=[prevb, identb], writes=[pstr] if c == 0 else [])
                            pstr.w = (pe.sem, pe.sem.n)
                            op(dve, lambda e: e.tensor_copy(e_cur.ap[:, :, 0:30], pstr.ap[:, :, 0:30]), reads=[pstr], writes=[e_cur])
                        else:
                            k.load(prevf.ap[0:15, :], I["spool"][s], writes=[prevf])
                            for c in range(8):
                                op(pe, lambda e, c=c: e.transpose(psG[c // 4].ap[:, c % 4, 0:15], prevf.ap[0:15, c * 128:(c + 1) * 128], identf.ap[0:15, 0:15]),
                                   reads=[prevf, identf], writes=[psG[c // 4]] if c % 4 == 0 else [])
                                if c % 4 == 3:
                                    psG[c // 4].w = (pe.sem, pe.sem.n)
                            for hf in range(2):
                                op(dve, lambda e, hf=hf: e.tensor_copy(e_cur.ap[:, hf * 4:hf * 4 + 4, 1:16], psG[hf].ap[:, :, 0:15]), reads=[psG[hf]], writes=[e_cur])
                    if isB:
                        proj(psA, 0, nt, xT)
                        proj(psB, D, nt, xT)
                        proj(psG, 2 * D, nt, xT)
                        if i + 1 < NT:
                            mk_next(i + 1)
                        for hf in range(2):
                            op(act, lambda e, hf=hf: e.activation(sig.ap[:, hf * 4:hf * 4 + 4, :nt], psB[hf].ap[:, :, :nt], AF.Sigmoid), reads=[psB[hf]], writes=[sig])
                            op(act, lambda e, hf=hf: e.activation(gs.ap[:, hf * 4:hf * 4 + 4, :nt], psG[hf].ap[:, :, :nt], AF.Silu), reads=[psG[hf]], writes=[gs])
                            op(dve, lambda e, hf=hf: e.tensor_tensor(u32.ap[:, hf * 4:hf * 4 + 4, :nt], psA[hf].ap[:, :, :nt], sig.ap[:, hf * 4:hf * 4 + 4, :nt], ALU.mult),
                               reads=[psA[hf], sig], writes=[u32])
                        op(act, lambda e: e.activation(e_cur.ap[:, :, HP:HP + nt], u32.ap[:, :, :nt], AF.Copy), reads=[u32], writes=[e_cur])
                        for c in range(8):
                            ps = psA[c // 4]
                            for jj in range(31):
                                op(pe, lambda e, ps=ps, c=c, jj=jj: e.matmul(ps.ap[:, c % 4, :nt], Dg.ap[:, c, jj, :], e_cur.ap[:, c, jj:jj + nt], start=(jj == 0), stop=(jj == 30)),
                                   reads=[Dg, e_cur], writes=[ps] if (c % 4 == 0 and jj == 0) else [])
                            if c % 4 == 3:
                                ps.w = (pe.sem, pe.sem.n)
                        for c in range(8):
                            op(act, lambda e, c=c: e.activation(cT.ap[:, c, :nt], psA[c // 4].ap[:, c % 4, :nt], AF.Identity, bias=cb.ap[:, c:c + 1], scale=1.0),
                               reads=[psA[c // 4], cb], writes=[cT])
                        op(act, lambda e: e.activation(sq.ap[:, :, :nt], cT.ap[:, :, :nt], AF.Square), reads=[cT], writes=[sq])
                        for which, src in ((0, cT), (1, sq)):
                            for c in range(8):
                                op(pe, lambda e, which=which, src=src, c=c: e.matmul(psS.ap[:, which, :nt], onesf.ap[:, :], src.ap[:, c, :nt], start=(c == 0), stop=(c == 7)),
                                   reads=[onesf, src], writes=[psS] if (which == 0 and c == 0) else [])
                        psS.w = (pe.sem, pe.sem.n)
                        op(dve, lambda e: e.tensor_scalar(mean.ap[:, :nt], psS.ap[:, 0, :nt], 1.0 / D, None, ALU.mult), reads=[psS], writes=[mean])
                        op(dve, lambda e: e.tensor_tensor(msq.ap[:, :nt], mean.ap[:, :nt], mean.ap[:, :nt], ALU.mult), reads=[mean], writes=[msq])
                        op(dve, lambda e: e.scalar_tensor_tensor(var.ap[:, :nt], psS.ap[:, 1, :nt], 1.0 / D, msq.ap[:, :nt], ALU.mult, ALU.subtract),
                           reads=[psS, msq], writes=[var])
                        op(act, lambda e: e.activation(var.ap[:, :nt], var.ap[:, :nt], AF.Sqrt, bias=epsb.ap[:, :], scale=1.0), reads=[epsb], writes=[var])
                        op(dve, lambda e: e.reciprocal(var.ap[:, :nt], var.ap[:, :nt]), writes=[var])
                        mbc = mean.ap[:, :nt].unsqueeze(1).to_broadcast([128, 8, nt])
                        vbc = var.ap[:, :nt].unsqueeze(1).to_broadcast([128, 8, nt])
                        op(dve, lambda e: e.tensor_tensor(cT.ap[:, :, :nt], cT.ap[:, :, :nt], mbc, ALU.subtract), reads=[mean], writes=[cT])
                        op(dve, lambda e: e.tensor_tensor(cT.ap[:, :, :nt], cT.ap[:, :, :nt], vbc, ALU.mult), reads=[var], writes=[cT])
                        for c in range(8):
                            op(act, lambda e, c=c: e.activation(sn.ap[:, c, :nt], cT.ap[:, c, :nt], AF.Silu, bias=nbp.ap[:, c:c + 1], scale=ngp.ap[:, c:c + 1]),
                               reads=[cT, ngp, nbp], writes=[sn])
                        op(dve, lambda e: e.tensor_tensor(aT.ap[:, :, :nt], sn.ap[:, :, :nt], gs.ap[:, :, :nt], ALU.mult), reads=[sn, gs], writes=[aT])
                        if i == NTP - 1 or i >= NTP:
                            srcbuf[0] = u32
                            dst = O["ncp"][:, :] if i < NTP else O["ncs"][i - NTP]
                            state_out(u32.ap, nt, 30, dst)
                    else:
                        proj(psA, 0, nt, xT)
                        proj(psG, D, nt, xT)
                        if i + 1 < NT:
                            mk_next(i + 1)
                        for hf in range(2):
                            op(act, lambda e, hf=hf: e.activation(gs.ap[:, hf * 4:hf * 4 + 4, :nt], psG[hf].ap[:, :, :nt], AF.Silu), reads=[psG[hf]], writes=[gs])
                            op(act, lambda e, hf=hf: e.activation(e_cur.ap[:, hf * 4:hf * 4 + 4, HP:HP + nt], psA[hf].ap[:, :, :nt], AF.Copy), reads=[psA[hf]], writes=[e_cur])
                        L = HP + nt
                        for g in range(4):
                            E = e_cur.ap[:, 2 * g:2 * g + 2, :]
                            cur = None
                            bufs = [wa, wb2]
                            for lv in range(g + 1):
                                sh = 2 ** lv
                                lo = 2 ** (lv + 1)
                                dstb = bufs[lv % 2]
                                if lv == 0:
                                    op(dve, lambda e, dstb=dstb, E=E, lo=lo, sh=sh: e.tensor_tensor(dstb.ap[:, :, lo:L], E[:, :, lo:L], E[:, :, lo - sh:L - sh], ALU.add),
                                       reads=[e_cur], writes=[dstb])
                                else:
                                    srcb = bufs[(lv + 1) % 2]
                                    op(dve, lambda e, dstb=dstb, srcb=srcb, lo=lo, sh=sh: e.tensor_tensor(dstb.ap[:, :, lo:L], srcb.ap[:, :, lo:L], srcb.ap[:, :, lo - sh:L - sh], ALU.add),
                                       reads=[srcb], writes=[dstb])
                                cur = dstb
                            w = 2 ** (g + 1)
                            op(dve, lambda e, cur=cur, E=E, g=g, w=w: e.scalar_tensor_tensor(dT.ap[:, 2 * g:2 * g + 2, :nt], cur.ap[:, :, HP:HP + nt], 1.0 / w, E[:, :, HP:HP + nt], ALU.mult, ALU.subtract),
                               reads=[cur, e_cur], writes=[dT])
                            if i == 0:
                                ic = invc.ap[:, g, :].unsqueeze(1).to_broadcast([128, 2, 16])
                                op(dve, lambda e, cur=cur, ic=ic: e.tensor_tensor(ftmp.ap[:, :, :], cur.ap[:, :, HP:HP + 16], ic, ALU.mult), reads=[cur, invc], writes=[ftmp])
                                op(dve, lambda e, E=E, g=g: e.tensor_tensor(dT.ap[:, 2 * g:2 * g + 2, 0:16], ftmp.ap[:, :, :], E[:, :, HP:HP + 16], ALU.subtract),
                                   reads=[ftmp, e_cur], writes=[dT])
                        for g in range(4):
                            for dc in range(2):
                                oc = 2 * g + dc
                                ps = psA[oc // 4]
                                for cc in range(2):
                                    op(pe, lambda e, ps=ps, g=g, dc=dc, cc=cc, oc=oc: e.matmul(ps.ap[:, oc % 4, :nt], wg.ap[:, g, cc, dc * 128:(dc + 1) * 128], dT.ap[:, 2 * g + cc, :nt],
                                                                                          start=(cc == 0), stop=(cc == 1)),
                                       reads=[wg, dT], writes=[ps] if (oc % 4 == 0 and cc == 0) else [])
                                if oc % 4 == 3:
                                    ps.w = (pe.sem, pe.sem.n)
                        for hf in range(2):
                            op(dve, lambda e, hf=hf: e.tensor_tensor(aT.ap[:, hf * 4:hf * 4 + 4, :nt], psA[hf].ap[:, :, :nt], gs.ap[:, hf * 4:hf * 4 + 4, :nt], ALU.mult),
                               reads=[psA[hf], gs], writes=[aT])
                        if i == NTP - 1 or i >= NTP:
                            srcbuf[0] = e_cur
                            dst = O["npp"][:, :] if i < NTP else O["nps"][i - NTP]
                            state_out(e_cur.ap[:, :, HP:HP + 128], nt, 15, dst)
                    post(l, i, aT, wout, x_, psB, pb["z"], pb["stats"], pb["mv"], pb["rstd"], pb["xn"], gbc, bbc)

                loads(0)
                mk_next(0)
                for i in range(NT):
                    if i + 1 < NT:
                        loads(i + 1)
                    compute(i)
                phase_barrier()
                k.flush()

        for spec_ in getattr(cfg, "layers", [("A", 0, 0), ("B", 1), ("C", 2), ("A", 3, 1)]):
            if spec_[0] == "A":
                layer_A(spec_[1], spec_[2])
            else:
                layer_BC(spec_[1], spec_[0])
        toks = k.barrier_tokens()
        k.sp.add(lambda e: e.nop(), toks)
        k.flush()
    return nc


def rope_table(cfg):
    def tab(pos, r):
        half = r // 2
        inv = (ROPE_THETA ** (-np.arange(half, dtype=np.float32) * 2.0 / r)).astype(np.float32)
        ang = pos.astype(np.float32)[:, None] * inv[None, :]
        return np.cos(ang).astype(np.float32), np.sin(ang).astype(np.float32)
    posp = np.arange(cfg.SEQ)
    poss = cfg.PAST + np.arange(DS)
    rows = []
    for pos in (posp, poss):
        c16, s16 = tab(pos, 32)
        c8, s8 = tab(pos, 16)
        rows.append(np.concatenate([c16, s16, c8, s8], axis=1))
    return np.concatenate([rows[0]] + [rows[1]] * cfg.NS, axis=0).astype(np.float32)


def make_in_maps(cfg, inp, ncores, nb):
    f = lambda a: np.ascontiguousarray(np.asarray(a, dtype=np.float32))
    rope = rope_table(cfg)
    maps = []
    NS = cfg.NS
    for c in range(ncores):
        b = c % nb
        ss = slice(c * NS, (c + 1) * NS)
        maps.append(dict(
            xp=f(inp["x_prompt"][b]), xs=f(inp["x_sample"][ss]).reshape(NS * DS, D),
            ck=f(inp["cache_k"][:, ss]).reshape(2, NS, cfg.PAST, D), cv=f(inp["cache_v"][:, ss]).reshape(2, NS, cfg.PAST, D),
            cki=f(inp["cache_kidx"][:, ss]), sconv=f(inp["state_conv"][0, ss]), spool=f(inp["state_pool"][0, ss]),
            w_in_a=f(inp["w_in_a"]), w_out_a=f(inp["w_out_a"]), w_in_b=f(inp["w_in_b"][0]), conv_w=f(inp["conv_w_b"][0]),
            conv_b=f(inp["conv_bias_b"][0]), ng=f(inp["norm_g_b"][0]), nb=f(inp["norm_b_b"][0]), w_out_b=f(inp["w_out_b"][0]),
            w_in_c=f(inp["w_in_c"][0]), w_grp=f(inp["w_grp_c"][0]), scale_c=f(inp["scale_c"][0]), w_out_c=f(inp["w_out_c"][0]),
            ln_g=f(inp["ln_g"]), ln_b=f(inp["ln_b"]), rope=rope,
        ))
    return maps


def assemble(cfg, res, ncores, nb):
    NS = cfg.NS
    R = res
    cat = lambda key, cores: np.stack([R[c][key] for c in cores])
    pc = list(range(nb))
    ac = list(range(ncores))
    yp = cat("yp", pc)
    ys = np.concatenate([R[c]["ys"].reshape(NS, DS, D) for c in ac])
    nkp = np.stack([R[c]["nkp"] for c in pc], axis=1).reshape(2, nb, cfg.SEQ, NH, 128)
    nvp = np.stack([R[c]["nvp"] for c in pc], axis=1).reshape(2, nb, cfg.SEQ, NH, 128)
    nkip = np.stack([R[c]["nkip"] for c in pc], axis=1)
    ncp = cat("ncp", pc)[None]
    npp = cat("npp", pc)[None]
    nks = np.concatenate([R[c]["nks"].reshape(2, NS, DS, NH, 128) for c in ac], axis=1)
    nvs = np.concatenate([R[c]["nvs"].reshape(2, NS, DS, NH, 128) for c in ac], axis=1)
    nkis = np.concatenate([R[c]["nkis"].reshape(2, NS, DS, 64) for c in ac], axis=1)
    ncs = np.concatenate([R[c]["ncs"] for c in ac])[None]
    nps = np.concatenate([R[c]["nps"] for c in ac])[None]
    return tuple(np.ascontiguousarray(a, dtype=np.float32) for a in (yp, ys, nkp, nvp, nkip, ncp, npp, nks, nvs, nkis, ncs, nps))


def kernel(**inputs):
    cfg = Cfg()
    nc = build(cfg)
    maps = make_in_maps(cfg, inputs, 8, 4)
    res = run_bass_kernel_spmd(nc, maps, core_ids=list(range(8)))
    return assemble(cfg, res.results, 8, 4)
```

```python
import numpy as np
from contextlib import ExitStack
import concourse.bass as bass
import concourse.mybir as mybir
from concourse.bass_utils import run_bass_kernel_spmd

F32 = mybir.dt.float32
BF16 = mybir.dt.bfloat16
U8 = mybir.dt.uint8
ALU = mybir.AluOpType
AF = mybir.ActivationFunctionType

D = 1024
NH = 8
A_IN = 4420
DS = 64
ALPHA = 8.0 ** 0.25
EPS = 1e-5
NEG = -1.0e30
MBIG = -30000.0
SCALE = 128.0 ** -0.5
ROPE_THETA = 500000.0


class Cfg:
    def __init__(self, SEQ=8192, NS=4, PAST=1024, topk_p=256, topk_s=256, ksteps=19):
        self.SEQ, self.NS, self.PAST = SEQ, NS, PAST
        self.topk_p, self.topk_s, self.ksteps = topk_p, topk_s, ksteps
        self.NTP = SEQ // 128
        self.NCT = PAST // 128
        self.TT = SEQ + NS * DS
        self.NKT = self.NTP + NS * (self.NCT + 1)
        self.KSW = PAST + DS


class SemC:
    def __init__(self, h):
        self.h = h
        self.n = 0


class Stream:
    def __init__(self, name, semc, serial):
        self.name, self.sem, self.serial = name, semc, serial
        self.ops = []
        self.seen = {}
        self.last = None

    def add(self, fn, waits=(), inc=None):
        ws = [w for w in waits if w is not None]
        if self.serial and self.last is not None:
            ws.append(self.last)
        if inc is None:
            self.sem.n += 1
            tok = (self.sem, self.sem.n)
            spec = (self.sem, 1)
            if self.serial:
                self.last = tok
        else:
            semc, amt = inc
            semc.n += amt
            tok = (semc, semc.n)
            spec = (semc, amt)
        self.ops.append((fn, ws, spec))
        return tok

    def emit(self, eng):
        for fn, ws, spec in self.ops:
            best = {}
            for (s, v) in ws:
                if self.name == "pe" and s is self.sem:
                    continue
                if best.get(s, 0) < v:
                    best[s] = v
            for s, v in best.items():
                if self.seen.get(s, 0) >= v:
                    continue
                self.seen[s] = v
                eng.wait_ge(s.h, v)
            inst = fn(eng)
            inst.then_inc(spec[0].h, spec[1])
        self.ops = []


class Buf:
    def __init__(self, ap=None):
        self.ap = ap
        self.w = None
        self.rs = {}

    def __getitem__(self, k):
        return self.ap[k]


class K:
    def __init__(self, nc, es):
        self.nc, self.es = nc, es
        mk = lambda n: SemC(es.enter_context(nc.semaphore(n)))
        self.pe = Stream("pe", mk("s_pe"), False)
        self.act = Stream("act", mk("s_act"), True)
        self.dve = Stream("dve", mk("s_dve"), True)
        self.pool = Stream("pool", mk("s_pool"), True)
        self.sp = Stream("sp", mk("s_sp"), False)
        self.stq = self.sp
        self.ldsem = [mk("ld%d" % i) for i in range(16)]
        self.stsem = [mk("st%d" % i) for i in range(16)]
        self.ldi = 0
        self.sti = 0
        self.lasttok = {}
        self.dram = {}

    def op(self, stream, fn, reads=(), writes=(), extra=(), inc=None):
        waits = list(extra)
        for b in reads:
            waits.append(b.w)
        for b in writes:
            waits.append(b.w)
            waits.extend(b.rs.items())
        tok = stream.add(fn, waits, inc)
        for b in reads:
            s, v = tok
            if b.rs.get(s, 0) < v:
                b.rs[s] = v
        for b in writes:
            b.w = tok
            b.rs = {}
        return tok

    def dbuf(self, key):
        if key not in self.dram:
            self.dram[key] = Buf()
        return self.dram[key]

    def load(self, out_ap, in_ap, reads=(), writes=(), **kw):
        semc = self.ldsem[self.ldi % len(self.ldsem)]
        self.ldi += 1
        prev = self.lasttok.get(semc)
        tok = self.op(self.sp, lambda e: e.dma_start(out=out_ap, in_=in_ap, **kw), reads, writes,
                      extra=[prev], inc=(semc, 16))
        self.lasttok[semc] = tok
        return tok

    def store(self, out_ap, in_ap, reads=(), writes=(), **kw):
        semc = self.stsem[self.sti % len(self.stsem)]
        self.sti += 1
        prev = self.lasttok.get(semc)
        tok = self.op(self.stq, lambda e: e.dma_start(out=out_ap, in_=in_ap, **kw), reads, writes,
                      extra=[prev], inc=(semc, 16))
        self.lasttok[semc] = tok
        return tok

    def flush(self):
        nc = self.nc
        with nc.Block() as block:
            @block.sync
            def _(e):
                self.sp.emit(e)

            @block.gpsimd
            def _(e):
                self.pool.emit(e)

            @block.scalar
            def _(e):
                self.act.emit(e)

            @block.vector
            def _(e):
                self.dve.emit(e)

            @block.tensor
            def _(e):
                self.pe.emit(e)

    def barrier_tokens(self):
        toks = []
        for s in (self.pe, self.act, self.dve, self.pool):
            if s.sem.n:
                toks.append((s.sem, s.sem.n))
        for semc in self.ldsem + self.stsem:
            if semc.n:
                toks.append((semc, semc.n))
        return toks


def build(cfg):
    nc = bass.Bass("TRN2", target_bir_lowering=False)
    SEQ, NS, PAST, NTP, NCT, TT, NKT = cfg.SEQ, cfg.NS, cfg.PAST, cfg.NTP, cfg.NCT, cfg.TT, cfg.NKT
    NSR = NS * DS

    def din(name, shape, dt=F32):
        return nc.dram_tensor(name, list(shape), dt, kind="ExternalInput").ap()

    def dout(name, shape):
        return nc.dram_tensor(name, list(shape), F32, kind="ExternalOutput").ap()

    def dscr(name, shape, dt):
        if getattr(cfg, "debug", False) and dt == F32:
            return nc.dram_tensor(name, list(shape), dt, kind="ExternalOutput").ap()
        return nc.dram_tensor(name, list(shape), dt).ap()

    I = dict(
        xp=din("xp", [SEQ, D]), xs=din("xs", [NSR, D]),
        ck=din("ck", [2, NS, PAST, D]), cv=din("cv", [2, NS, PAST, D]), cki=din("cki", [2, NS, PAST, 64]),
        sconv=din("sconv", [NS, 30, D]), spool=din("spool", [NS, 15, D]),
        w_in_a=din("w_in_a", [2, D, A_IN]), w_out_a=din("w_out_a", [2, D, D]),
        w_in_b=din("w_in_b", [D, 3 * D]), conv_w=din("conv_w", [31, D]), conv_b=din("conv_b", [D]),
        ng=din("ng", [D]), nb=din("nb", [D]), w_out_b=din("w_out_b", [D, D]),
        w_in_c=din("w_in_c", [D, 2 * D]), w_grp=din("w_grp", [4, 256, 256]), scale_c=din("scale_c", [D]),
        w_out_c=din("w_out_c", [D, D]), ln_g=din("ln_g", [4, D]), ln_b=din("ln_b", [4, D]),
        rope=din("rope", [TT, 48]),
    )
    O = dict(
        yp=dout("yp", [SEQ, D]), ys=dout("ys", [NSR, D]),
        nkp=dout("nkp", [2, SEQ, D]), nvp=dout("nvp", [2, SEQ, D]), nkip=dout("nkip", [2, SEQ, 64]),
        ncp=dout("ncp", [30, D]), npp=dout("npp", [15, D]),
        nks=dout("nks", [2, NSR, D]), nvs=dout("nvs", [2, NSR, D]), nkis=dout("nkis", [2, NSR, 64]),
        ncs=dout("ncs", [NS, 30, D]), nps=dout("nps", [NS, 15, D]),
    )
    NT = NTP + NS
    SC = dict(
        xres=[None] + [dscr("xres%d" % l, [TT, D], F32) for l in (1, 2, 3)],
        QT=dscr("QT", [NT, 128, 1024], BF16), GT=dscr("GT", [NT, 128, 1024], BF16),
        QIT=dscr("QIT", [NT, 128, 256], BF16), SG=dscr("SG", [TT, 4], F32),
        KT=dscr("KT", [NKT, 128, 1024], BF16), VV=dscr("VV", [NKT, 128, 1024], BF16),
        KIP=dscr("KIP", [64, SEQ], BF16), KIS=dscr("KIS", [NS, 64, PAST + 128], BF16),
    )

    def tile_rows(i):
        if i < NTP:
            return i * 128, 128
        return SEQ + (i - NTP) * DS, DS

    def xsrc(l, i):
        r0, nt = tile_rows(i)
        if l == 0:
            if i < NTP:
                return I["xp"][r0:r0 + nt, :]
            return I["xs"][r0 - SEQ:r0 - SEQ + nt, :]
        return SC["xres"][l][r0:r0 + nt, :]

    def xdst(l, i):
        r0, nt = tile_rows(i)
        if l == 3:
            if i < NTP:
                return O["yp"][r0:r0 + nt, :]
            return O["ys"][r0 - SEQ:r0 - SEQ + nt, :]
        return SC["xres"][l + 1][r0:r0 + nt, :]

    with ExitStack() as es0:
        k = K(nc, es0)
        op, pe, act, dve, pool = k.op, k.pe, k.act, k.dve, k.pool

        uid = [0]

        def sb(es, name, shape, dt):
            uid[0] += 1
            return Buf(es.enter_context(nc.sbuf_tensor("%s_%d" % (name, uid[0]), list(shape), dt)))

        def psb(es, name, shape, dt):
            uid[0] += 1
            return Buf(es.enter_context(nc.psum_tensor("%s_%d" % (name, uid[0]), list(shape), dt)))

        identb = sb(es0, "identb", [128, 128], BF16)
        identf = sb(es0, "identf", [128, 128], F32)
        onesb = sb(es0, "onesb", [128, 128], BF16)
        onesf = sb(es0, "onesf", [128, 128], F32)
        invc = sb(es0, "invc", [128, 4, 16], F32)
        for ib in (identb, identf):
            op(pool, lambda e, ib=ib: e.memset(ib.ap[:], 1.0), writes=[ib])
            op(pool, lambda e, ib=ib: e.affine_select(ib.ap[:], ib.ap[:], [[-1, 128]], ALU.is_equal, 0.0,
                                                        base=0, channel_multiplier=1), writes=[ib])
        op(pool, lambda e: e.memset(onesb.ap[:], 1.0), writes=[onesb])
        op(pool, lambda e: e.memset(onesf.ap[:], 1.0), writes=[onesf])
        iot = sb(es0, "iot", [128, 16], F32)
        op(pool, lambda e: e.iota(iot.ap[:], [[1, 16]], base=1, channel_multiplier=0,
                                  allow_small_or_imprecise_dtypes=True), writes=[iot])
        for g in range(4):
            op(dve, lambda e, g=g: e.tensor_scalar(invc.ap[:, g, :], iot.ap[:], float(2 ** (g + 1)), None, ALU.min),
               reads=[iot], writes=[invc])
        op(dve, lambda e: e.reciprocal(invc.ap[:], invc.ap[:]), writes=[invc])

        def phase_barrier():
            toks = k.barrier_tokens()
            for s in (k.pe, k.act, k.dve, k.pool, k.sp):
                s.add(lambda e: e.nop(), toks)

        def load_weight_bf16(es, wtile, src_ap, ncols, stage, pieces=4):
            cw = (ncols + pieces - 1) // pieces
            cnt = 0
            for kk in range(8):
                for pc in range(pieces):
                    c0 = pc * cw
                    c1 = min(ncols, c0 + cw)
                    if c0 >= c1:
                        continue
                    st = stage[cnt % len(stage)]
                    cnt += 1
                    k.load(st.ap[:, 0:c1 - c0], src_ap[kk * 128:(kk + 1) * 128, c0:c1], writes=[st])
                    eng = act if cnt % 2 else dve
                    if eng is act:
                        op(act, lambda e, st=st, kk=kk, c0=c0, c1=c1: e.activation(wtile.ap[:, kk, c0:c1], st.ap[:, 0:c1 - c0], AF.Copy),
                           reads=[st], writes=[wtile])
                    else:
                        op(dve, lambda e, st=st, kk=kk, c0=c0, c1=c1: e.tensor_copy(wtile.ap[:, kk, c0:c1], st.ap[:, 0:c1 - c0]),
                           reads=[st], writes=[wtile])

        def make_xT(xt, xb, pstr, xT, nt):
            op(act, lambda e: e.activation(xb.ap[:nt, :], xt.ap[:nt, :], AF.Copy), reads=[xt], writes=[xb])
            for kk in range(8):
                op(pe, lambda e, kk=kk: e.transpose(pstr.ap[:, kk, :nt], xb.ap[:nt, kk * 128:(kk + 1) * 128], identb.ap[:nt, :nt]),
                   reads=[xb, identb], writes=[pstr] if kk == 0 else [], extra=[pstr.w] if kk else [])
            pstr.w = (pe.sem, pe.sem.n)
            op(dve, lambda e: e.tensor_copy(xT.ap[:, :, :nt], pstr.ap[:, :, :nt]), reads=[pstr], writes=[xT])

        def post(l, i, aT, wout, xt, ps_y, z, stats, mv, rstd, xn, gbc, bbc):
            r0, nt = tile_rows(i)
            for half in range(2):
                for kk in range(8):
                    op(pe, lambda e, half=half, kk=kk: e.matmul(ps_y[half].ap[:nt, :], aT.ap[:, kk, :nt],
                                                              wout.ap[:, kk, half * 512:(half + 1) * 512],
                                                              start=(kk == 0), stop=(kk == 7)),
                       reads=[aT, wout], writes=[ps_y[half]] if kk == 0 else [])
                ps_y[half].w = (pe.sem, pe.sem.n)
                op(dve, lambda e, half=half: e.scalar_tensor_tensor(z.ap[:nt, half * 512:(half + 1) * 512],
                                                                    xt.ap[:nt, half * 512:(half + 1) * 512], ALPHA,
                                                                    ps_y[half].ap[:nt, :], ALU.mult, ALU.add),
                   reads=[xt, ps_y[half]], writes=[z])
                op(dve, lambda e, half=half: e.bn_stats(stats.ap[:nt, half, :], z.ap[:nt, half * 512:(half + 1) * 512]),
                   reads=[z], writes=[stats])
            op(dve, lambda e: e.bn_aggr(mv.ap[:nt, :], stats.ap[:nt, :, :]), reads=[stats], writes=[mv])
            op(act, lambda e: e.activation(rstd.ap[:nt, :], mv.ap[:nt, 1:2], AF.Sqrt, bias=epsb.ap[:nt, :], scale=1.0),
               reads=[mv, epsb], writes=[rstd])
            op(dve, lambda e: e.reciprocal(rstd.ap[:nt, :], rstd.ap[:nt, :]), writes=[rstd])
            op(dve, lambda e: e.tensor_scalar(xn.ap[:nt, :], z.ap[:nt, :], mv.ap[:nt, 0:1], rstd.ap[:nt, 0:1],
                                              ALU.subtract, ALU.mult), reads=[z, mv, rstd], writes=[xn])
            op(dve, lambda e: e.tensor_tensor(xn.ap[:nt, :], xn.ap[:nt, :], gbc.ap[:nt, :], ALU.mult), reads=[gbc], writes=[xn])
            op(dve, lambda e: e.tensor_tensor(xn.ap[:nt, :], xn.ap[:nt, :], bbc.ap[:nt, :], ALU.add), reads=[bbc], writes=[xn])
            db = k.dbuf(("x", l + 1, i))
            k.store(xdst(l, i), xn.ap[:nt, :], reads=[xn], writes=[db])

        epsb = sb(es0, "epsb", [128, 1], F32)
        op(pool, lambda e: e.memset(epsb.ap[:], EPS), writes=[epsb])

        def load_ln(es, l):
            gbc = sb(es, "gbc", [128, D], F32)
            bbc = sb(es, "bbc", [128, D], F32)
            k.load(gbc.ap[:], I["ln_g"][l, :].partition_broadcast(128), writes=[gbc])
            k.load(bbc.ap[:], I["ln_b"][l, :].partition_broadcast(128), writes=[bbc])
            return gbc, bbc

        def post_bufs(es):
            return dict(
                z=sb(es, "z", [128, D], F32), stats=sb(es, "stats", [128, 2, 6], F32), mv=sb(es, "mv", [128, 2], F32),
                rstd=sb(es, "rstd", [128, 1], F32), xn=sb(es, "xn", [128, D], F32))

        def layer_A(l, j):
            topk_of = lambda i: cfg.topk_p if i < NTP else cfg.topk_s
            with ExitStack() as es:
                win = sb(es, "win", [128, 8, A_IN], BF16)
                stage = [sb(es, "wst%d" % s, [128, 1105], F32) for s in range(3)]
                load_weight_bf16(es, win, I["w_in_a"][j], A_IN, stage, pieces=4)
                xt = [sb(es, "xt%d" % s, [128, D], F32) for s in range(2)]
                rp = [sb(es, "rp%d" % s, [128, 48], F32) for s in range(2)]
                xb = sb(es, "xb", [128, D], BF16)
                xT = sb(es, "xT", [128, 8, 128], BF16)
                hbuf = [sb(es, "h%d" % s, [128, A_IN], F32) for s in range(2)]
                tmp = [sb(es, "rt%d" % s, [128, 16, 16], F32) for s in range(4)]
                hqk = sb(es, "hqk", [128, 2048], BF16)
                vb = sb(es, "vb", [128, D], BF16)
                gs = sb(es, "gs", [128, D], BF16)
                qib = sb(es, "qib", [128, 320], BF16)
                aw = sb(es, "aw", [128, 4], F32)
                sg = sb(es, "sg", [128, 4], F32)
                qT = sb(es, "qT", [128, 8, 128], BF16)
                kT = sb(es, "kT", [128, 8, 128], BF16)
                gT = sb(es, "gT", [128, 8, 128], BF16)
                qiT = sb(es, "qiT", [128, 2, 128], BF16)
                kiT = sb(es, "kiT", [64, 128], BF16)
                ckf = [sb(es, "ckf%d" % s, [128, D], F32) for s in range(2)]
                cvf = [sb(es, "cvf%d" % s, [128, D], F32) for s in range(2)]
                ckif = [sb(es, "ckif%d" % s, [128, 64], F32) for s in range(2)]
                ckb = sb(es, "ckb", [128, D], BF16)
                ckib = sb(es, "ckib", [128, 64], BF16)
                psp = [psb(es, "psp%d" % s, [128, 512], F32) for s in range(5)]
                pstr = psb(es, "pstr", [128, 8, 128], BF16)
                pst2 = [psb(es, "pst2%d" % s, [128, 8, 128], BF16) for s in range(2)]
                chunks = [(c * 512, min(A_IN, (c + 1) * 512)) for c in range(9)]

                def rope_block(hv, c, s, nt, nh, half, rb, h):
                    x1 = hv[:, :, 0:half]
                    x2 = hv[:, :, half:2 * half]
                    cb_ = c.unsqueeze(1).to_broadcast([nt, nh, half])
                    sb_ = s.unsqueeze(1).to_broadcast([nt, nh, half])
                    t = [tt.ap[:nt, 0:nh, 0:half] for tt in tmp]
                    op(dve, lambda e: e.tensor_tensor(t[0], x1, cb_, ALU.mult), reads=[h, rb], writes=[tmp[0]])
                    op(dve, lambda e: e.tensor_tensor(t[1], x2, sb_, ALU.mult), reads=[h, rb], writes=[tmp[1]])
                    op(dve, lambda e: e.tensor_tensor(t[2], x1, sb_, ALU.mult), reads=[h, rb], writes=[tmp[2]])
                    op(dve, lambda e: e.tensor_tensor(t[3], x2, cb_, ALU.mult), reads=[h, rb], writes=[tmp[3]])
                    op(dve, lambda e: e.tensor_tensor(x1, t[0], t[1], ALU.subtract), reads=[tmp[0], tmp[1]], writes=[h])
                    op(dve, lambda e: e.tensor_tensor(x2, t[3], t[2], ALU.add), reads=[tmp[2], tmp[3]], writes=[h])

                def loads(i):
                    r0, nt = tile_rows(i)
                    k.load(xt[i % 2].ap[:nt, :], xsrc(l, i), reads=[k.dbuf(("x", l, i))], writes=[xt[i % 2]])
                    k.load(rp[i % 2].ap[:nt, :], I["rope"][r0:r0 + nt, :], writes=[rp[i % 2]])

                def computeP1(i):
                    r0, nt = tile_rows(i)
                    x_, rp_ = xt[i % 2], rp[i % 2]
                    h = hbuf[i % 2]
                    make_xT(x_, xb, pstr, xT, nt)
                    for ci, (c0, c1) in enumerate(chunks):
                        ps = psp[ci % 5]
                        for kk in range(8):
                            op(pe, lambda e, ps=ps, kk=kk, c0=c0, c1=c1: e.matmul(ps.ap[:nt, 0:c1 - c0], xT.ap[:, kk, :nt], win.ap[:, kk, c0:c1],
                                                                                  start=(kk == 0), stop=(kk == 7)),
                               reads=[xT, win], writes=[ps] if kk == 0 else [])
                        ps.w = (pe.sem, pe.sem.n)
                        if ci % 2 == 0:
                            op(act, lambda e, ps=ps, c0=c0, c1=c1: e.activation(h.ap[:nt, c0:c1], ps.ap[:nt, 0:c1 - c0], AF.Copy),
                               reads=[ps], writes=[h])
                        else:
                            op(dve, lambda e, ps=ps, c0=c0, c1=c1: e.tensor_copy(h.ap[:nt, c0:c1], ps.ap[:nt, 0:c1 - c0]),
                               reads=[ps], writes=[h])

                def computeP2(i):
                    r0, nt = tile_rows(i)
                    x_, rp_ = xt[i % 2], rp[i % 2]
                    h = hbuf[i % 2]
                    hv = h.ap[:nt, 0:2048].rearrange("p (h d) -> p h d", d=128)
                    rope_block(hv, rp_.ap[:nt, 0:16], rp_.ap[:nt, 16:32], nt, 16, 16, rp_, h)
                    hv2 = h.ap[:nt, 4096:4416].rearrange("p (h d) -> p h d", d=64)
                    rope_block(hv2, rp_.ap[:nt, 32:40], rp_.ap[:nt, 40:48], nt, 5, 8, rp_, h)
                    if i < NTP:
                        ko, vo, kio = O["nkp"][j, r0:r0 + nt, :], O["nvp"][j, r0:r0 + nt, :], O["nkip"][j, r0:r0 + nt, :]
                    else:
                        q0 = r0 - SEQ
                        ko, vo, kio = O["nks"][j, q0:q0 + nt, :], O["nvs"][j, q0:q0 + nt, :], O["nkis"][j, q0:q0 + nt, :]
                    k.store(ko, h.ap[:nt, 1024:2048], reads=[h])
                    k.store(vo, h.ap[:nt, 2048:3072], reads=[h])
                    k.store(kio, h.ap[:nt, 4352:4416], reads=[h])
                    op(dve, lambda e: e.tensor_copy(hqk.ap[:nt, :], h.ap[:nt, 0:2048]), reads=[h], writes=[hqk])
                    op(act, lambda e: e.activation(vb.ap[:nt, :], h.ap[:nt, 2048:3072], AF.Copy), reads=[h], writes=[vb])
                    op(act, lambda e: e.activation(gs.ap[:nt, :], h.ap[:nt, 3072:4096], AF.Silu), reads=[h], writes=[gs])
                    op(dve, lambda e: e.tensor_scalar(sg.ap[:nt, :], h.ap[:nt, 4416:4420], 0.0, 2.0, ALU.is_ge, ALU.mult),
                       reads=[h], writes=[sg])
                    op(dve, lambda e: e.tensor_scalar(sg.ap[:nt, :], sg.ap[:nt, :], -1.0, None, ALU.add), writes=[sg])
                    op(dve, lambda e: e.scalar_tensor_tensor(aw.ap[:nt, :], h.ap[:nt, 4416:4420], 0.0625, sg.ap[:nt, :], ALU.mult, ALU.mult),
                       reads=[h, sg], writes=[aw])
                    for hh in range(4):
                        op(dve, lambda e, hh=hh: e.tensor_scalar(qib.ap[:nt, hh * 64:(hh + 1) * 64], h.ap[:nt, 4096 + hh * 64:4096 + (hh + 1) * 64],
                                                                 aw.ap[:nt, hh:hh + 1], None, ALU.mult), reads=[h, aw], writes=[qib])
                    op(dve, lambda e: e.tensor_copy(qib.ap[:nt, 256:320], h.ap[:nt, 4352:4416]), reads=[h], writes=[qib])
                    def trn(ps, src, ncol, dst, dsl, evac):
                        nblk = (ncol + 127) // 128
                        for b in range(nblk):
                            w = min(128, ncol - b * 128)
                            op(pe, lambda e, b=b, w=w: e.transpose(ps.ap[0:w, b, :nt], src.ap[:nt, b * 128:b * 128 + w], identb.ap[:nt, :nt]),
                               reads=[src, identb], writes=[ps] if b == 0 else [])
                        ps.w = (pe.sem, pe.sem.n)
                    trn(pst2[0], hqk, 1024, None, None, None)
                    op(act, lambda e: e.activation(qT.ap[:, :, :nt], pst2[0].ap[:, :, :nt], AF.Copy), reads=[pst2[0]], writes=[qT])
                    for b in range(8):
                        op(pe, lambda e, b=b: e.transpose(pst2[1].ap[:, b, :nt], hqk.ap[:nt, 1024 + b * 128:1024 + (b + 1) * 128], identb.ap[:nt, :nt]),
                           reads=[hqk, identb], writes=[pst2[1]] if b == 0 else [])
                    pst2[1].w = (pe.sem, pe.sem.n)
                    op(dve, lambda e: e.tensor_copy(kT.ap[:, :, :nt], pst2[1].ap[:, :, :nt]), reads=[pst2[1]], writes=[kT])
                    trn(pst2[0], gs, 1024, None, None, None)
                    op(act, lambda e: e.activation(gT.ap[:, :, :nt], pst2[0].ap[:, :, :nt], AF.Copy), reads=[pst2[0]], writes=[gT])
                    trn(pst2[1], qib, 320, None, None, None)
                    op(dve, lambda e: e.tensor_copy(qiT.ap[:, :, :nt], pst2[1].ap[:, 0:2, :nt]), reads=[pst2[1]], writes=[qiT])
                    op(dve, lambda e: e.tensor_copy(kiT.ap[:, :nt], pst2[1].ap[0:64, 2, :nt]), reads=[pst2[1]], writes=[kiT])
                    kidx = i if i < NTP else NTP + (i - NTP) * (NCT + 1) + NCT
                    k.store(SC["QT"][i].rearrange("p (h t) -> p h t", t=128)[:, :, :nt], qT.ap[:, :, :nt], reads=[qT], writes=[k.dbuf(("QT", i))])
                    k.store(SC["GT"][i].rearrange("p (h t) -> p h t", t=128)[:, :, :nt], gT.ap[:, :, :nt], reads=[gT], writes=[k.dbuf(("GT", i))])
                    k.store(SC["QIT"][i].rearrange("p (h t) -> p h t", t=128)[:, :, :nt], qiT.ap[:, :, :nt], reads=[qiT], writes=[k.dbuf(("QIT", i))])
                    k.store(SC["KT"][kidx].rearrange("p (h t) -> p h t", t=128)[:, :, :nt], kT.ap[:, :, :nt], reads=[kT], writes=[k.dbuf(("KT", kidx))])
                    k.store(SC["VV"][kidx][:nt, :], vb.ap[:nt, :], reads=[vb], writes=[k.dbuf(("VV", kidx))])
                    k.store(SC["SG"][r0:r0 + nt, :], sg.ap[:nt, :], reads=[sg], writes=[k.dbuf(("SG", i))])
                    if i < NTP:
                        k.store(SC["KIP"][:, r0:r0 + nt], kiT.ap[:, :nt], reads=[kiT], writes=[k.dbuf(("KIP",))])
                    else:
                        k.store(SC["KIS"][i - NTP][:, PAST:PAST + nt], kiT.ap[:, :nt], reads=[kiT], writes=[k.dbuf(("KIS", i - NTP))])

                def cache_tile(s, c):
                    n = s * NCT + c
                    kf, vf, kif = ckf[n % 2], cvf[n % 2], ckif[n % 2]
                    k.load(kf.ap[:], I["ck"][j, s, c * 128:(c + 1) * 128, :], writes=[kf])
                    k.load(vf.ap[:], I["cv"][j, s, c * 128:(c + 1) * 128, :], writes=[vf])
                    k.load(kif.ap[:], I["cki"][j, s, c * 128:(c + 1) * 128, :], writes=[kif])
                    kidx = NTP + s * (NCT + 1) + c
                    op(act, lambda e: e.activation(ckb.ap[:], kf.ap[:], AF.Copy), reads=[kf], writes=[ckb])
                    for b in range(8):
                        op(pe, lambda e, b=b: e.transpose(pst2[0].ap[:, b, :], ckb.ap[:, b * 128:(b + 1) * 128], identb.ap[:]),
                           reads=[ckb, identb], writes=[pst2[0]] if b == 0 else [])
                    pst2[0].w = (pe.sem, pe.sem.n)
                    op(dve, lambda e: e.tensor_copy(kT.ap[:], pst2[0].ap[:]), reads=[pst2[0]], writes=[kT])
                    k.store(SC["KT"][kidx].rearrange("p (h t) -> p h t", t=128), kT.ap[:], reads=[kT], writes=[k.dbuf(("KT", kidx))])
                    op(act, lambda e: e.activation(vb.ap[:], vf.ap[:], AF.Copy), reads=[vf], writes=[vb])
                    k.store(SC["VV"][kidx], vb.ap[:], reads=[vb], writes=[k.dbuf(("VV", kidx))])
                    op(act, lambda e: e.activation(ckib.ap[:], kif.ap[:], AF.Copy), reads=[kif], writes=[ckib])
                    op(pe, lambda e: e.transpose(pst2[1].ap[0:64, 0, :], ckib.ap[:, :], identb.ap[:]), reads=[ckib, identb], writes=[pst2[1]])
                    op(dve, lambda e: e.tensor_copy(kiT.ap[:, :], pst2[1].ap[0:64, 0, :]), reads=[pst2[1]], writes=[kiT])
                    k.store(SC["KIS"][s][:, c * 128:(c + 1) * 128], kiT.ap[:, :], reads=[kiT], writes=[k.dbuf(("KIS", s))])

                loads(0)
                if NT > 1:
                    loads(1)
                computeP1(0)
                for i in range(NT):
                    if i + 1 < NT:
                        computeP1(i + 1)
                    computeP2(i)
                    if i + 2 < NT:
                        loads(i + 2)
                for s in range(NS):
                    for c in range(NCT):
                        cache_tile(s, c)
                phase_barrier()
                k.flush()

            if getattr(cfg, "skip_p2", False):
                return
            with ExitStack() as es:
                NMAX = max(SEQ, PAST + 128)
                wout = sb(es, "wout", [128, 8, D], BF16)
                stage = [sb(es, "wst%d" % s, [128, D], F32) for s in range(2)]
                load_weight_bf16(es, wout, I["w_out_a"][j], D, stage, pieces=1)
                gbc, bbc = load_ln(es, l)
                pb = post_bufs(es)
                ki2 = sb(es, "ki2", [128, NMAX], BF16)
                kis = sb(es, "kis", [128, PAST + 128], BF16)
                isc = sb(es, "isc", [128, NMAX], F32)
                junk = sb(es, "junk", [128, NMAX], U8)
                mb = [sb(es, "mb%d" % s, [128, 512], F32) for s in range(2)]
                maskT = sb(es, "maskT", [128, NMAX // 128 + 1, 128], BF16)
                rl = [sb(es, "rl%d" % s, [128, 512], F32) for s in range(2)]
                qT = [sb(es, "qT%d" % s, [128, 8, 128], BF16) for s in range(2)]
                gT = [sb(es, "gT%d" % s, [128, 8, 128], BF16) for s in range(2)]
                qiT = [sb(es, "qiT%d" % s, [128, 2, 128], BF16) for s in range(2)]
                sg = [sb(es, "sg%d" % s, [128, 4], F32) for s in range(2)]
                xt = [sb(es, "xt%d" % s, [128, D], F32) for s in range(2)]
                NKB = 4
                ktb = [sb(es, "ktb%d" % s, [128, 8, 128], BF16) for s in range(NKB)]
                vtb = [sb(es, "vtb%d" % s, [128, D], BF16) for s in range(NKB)]
                pT = [sb(es, "pT%d" % s, [128, 4, 128], BF16) for s in range(4)]
                oT = sb(es, "oT", [128, 8, 128], BF16)
                rcp = sb(es, "rcp", [128, 4, 128], F32)
                otmp = sb(es, "otmp", [128, 4, 128], F32)
                probe = sb(es, "probe", [128, 1], F32)
                cnt = sb(es, "cnt", [128, 1], F32)
                inc = sb(es, "inc", [128, 1], F32)
                thr = sb(es, "thr", [128, 1], F32)
                accA = sb(es, "accA", [128, 1], F32)
                tsum = sb(es, "tsum", [128, 1], F32)
                junk2 = sb(es, "junk2", [128, NMAX], U8)
                ps_d = [psb(es, "ps_d%d" % s, [128, 512], F32) for s in range(2)]
                ps_s = [psb(es, "ps_s%d" % s, [128, 4, 128], F32) for s in range(2)]
                ps_o = [psb(es, "ps_o%d" % s, [128, 4, 128], F32) for s in range(2)]
                ps_l = [psb(es, "ps_l%d" % s, [128, 4, 128], F32) for s in range(2)]
                sbank = [ps_s[0], ps_s[1], ps_d[0], ps_d[1]]
                sview = [ps_s[0].ap, ps_s[1].ap, ps_d[0].ap.rearrange("p (h q) -> p h q", q=128), ps_d[1].ap.rearrange("p (h q) -> p h q", q=128)]
                kipb = k.dbuf(("KIP",))
                k.load(ki2.ap[0:64, 0:SEQ], SC["KIP"][:, :], reads=[kipb], writes=[ki2])
                k.load(ki2.ap[64:128, 0:SEQ], SC["KIP"][:, :], reads=[kipb], writes=[ki2])
                kvcount = [0]

                def loads(i):
                    r0, nt = tile_rows(i)
                    s2 = i % 2
                    k.load(qT[s2].ap[:, :, :nt], SC["QT"][i].rearrange("p (h t) -> p h t", t=128)[:, :, :nt], reads=[k.dbuf(("QT", i))], writes=[qT[s2]])
                    k.load(gT[s2].ap[:, :, :nt], SC["GT"][i].rearrange("p (h t) -> p h t", t=128)[:, :, :nt], reads=[k.dbuf(("GT", i))], writes=[gT[s2]])
                    k.load(qiT[s2].ap[:, :, :nt], SC["QIT"][i].rearrange("p (h t) -> p h t", t=128)[:, :, :nt], reads=[k.dbuf(("QIT", i))], writes=[qiT[s2]])
                    k.load(sg[s2].ap[:nt, :], SC["SG"][r0:r0 + nt, :], reads=[k.dbuf(("SG", i))], writes=[sg[s2]])
                    k.load(xt[s2].ap[:nt, :], xsrc(l, i), reads=[k.dbuf(("x", l, i))], writes=[xt[s2]])

                def tinfo(i):
                    r0, nq = tile_rows(i)
                    if i < NTP:
                        n = 128 * (i + 1)
                        ktiles = [(t, 128) for t in range(i + 1)]
                        kib = ki2
                    else:
                        s = i - NTP
                        n = PAST + DS
                        base = NTP + s * (NCT + 1)
                        ktiles = [(base + c, 128) for c in range(NCT)] + [(base + NCT, DS)]
                        kib = kis
                    return r0, nq, i % 2, topk_of(i), n, ktiles, kib

                def stageA1(i):
                    r0, nq, s2, topk, n, ktiles, kib = tinfo(i)
                    if i >= NTP:
                        s = i - NTP
                        ksb = k.dbuf(("KIS", s))
                        k.load(kis.ap[0:64, 0:n], SC["KIS"][s][:, 0:n], reads=[ksb], writes=[kis])
                        k.load(kis.ap[64:128, 0:n], SC["KIS"][s][:, 0:n], reads=[ksb], writes=[kis])
                    for c0 in range(0, n, 512):
                        wk = min(512, n - c0)
                        for hh in range(4):
                            b0 = 64 * (hh % 2)
                            ps = ps_d[hh % 2]
                            op(pe, lambda e, ps=ps, hh=hh, b0=b0, c0=c0, wk=wk: e.matmul(ps.ap[:nq, 0:wk], qiT[s2].ap[b0:b0 + 64, hh // 2, :nq],
                                                                                          kib.ap[b0:b0 + 64, c0:c0 + wk], start=True, stop=True),
                               reads=[qiT[s2], kib], writes=[ps])
                            r_ = rl[hh % 2]
                            op(act, lambda e, ps=ps, r_=r_, wk=wk: e.activation(r_.ap[:nq, 0:wk], ps.ap[:nq, 0:wk], AF.Relu), reads=[ps], writes=[r_])
                            if hh == 0:
                                op(dve, lambda e, r_=r_, c0=c0, wk=wk: e.tensor_scalar(isc.ap[:nq, c0:c0 + wk], r_.ap[:nq, 0:wk], sg[s2].ap[:nq, 0:1], None, ALU.mult),
                                   reads=[r_, sg[s2]], writes=[isc])
                            else:
                                op(dve, lambda e, r_=r_, c0=c0, wk=wk, hh=hh: e.scalar_tensor_tensor(isc.ap[:nq, c0:c0 + wk], r_.ap[:nq, 0:wk], sg[s2].ap[:nq, hh:hh + 1],
                                                                                                    isc.ap[:nq, c0:c0 + wk], ALU.mult, ALU.add),
                                   reads=[r_, sg[s2]], writes=[isc])
                    if i < NTP:
                        op(dve, lambda e: e.memset(isc.ap[0:64, n - 64:n], NEG), writes=[isc])
                    if i < NTP and n <= topk:
                        op(dve, lambda e: e.memset(thr.ap[:], -1.0e29), writes=[thr])
                        return []
                    steps = []
                    op(dve, lambda e: e.memset(probe.ap[:], 0.0), writes=[probe])
                    n1 = min(n, max(128, int(round(n * getattr(cfg, 'split_frac', 0.5) / 128.0)) * 128))
                    if not getattr(cfg, "split", True):
                        n1 = n
                    na = n - n1

                    def mk_full(hw):
                        def f():
                            op(dve, lambda e: e.tensor_scalar(junk.ap[:nq, 0:n], isc.ap[:nq, 0:n], probe.ap[:nq, 0:1], None, ALU.is_ge, ALU.add,
                                                              accum_out=cnt.ap[:nq, :]), reads=[isc, probe], writes=[junk, cnt])
                            op(dve, lambda e: e.tensor_scalar(inc.ap[:nq, :], cnt.ap[:nq, :], float(topk), hw, ALU.is_ge, ALU.mult),
                               reads=[cnt], writes=[inc])
                            op(dve, lambda e: e.scalar_tensor_tensor(probe.ap[:nq, :], inc.ap[:nq, :], -hw / 2, probe.ap[:nq, :], ALU.add, ALU.add),
                               reads=[inc], writes=[probe])
                        return f

                    def mk_split(hw):
                        def f():
                            op(dve, lambda e: e.tensor_scalar(junk.ap[:nq, 0:n1], isc.ap[:nq, 0:n1], probe.ap[:nq, 0:1], None, ALU.is_ge, ALU.add,
                                                              accum_out=cnt.ap[:nq, :]), reads=[isc, probe], writes=[junk, cnt])
                            op(act, lambda e: e.activation(junk2.ap[:nq, 0:na], isc.ap[:nq, n1:n], AF.Sign, bias=probe.ap[:nq, 0:1], scale=-1.0,
                                                           accum_out=accA.ap[:nq, :]), reads=[isc, probe], writes=[junk2, accA])
                            op(dve, lambda e: e.scalar_tensor_tensor(tsum.ap[:nq, :], cnt.ap[:nq, :], 2.0, accA.ap[:nq, :], ALU.mult, ALU.subtract),
                               reads=[cnt, accA], writes=[tsum])
                            op(dve, lambda e: e.tensor_scalar(inc.ap[:nq, :], tsum.ap[:nq, :], float(2 * topk - na), hw, ALU.is_ge, ALU.mult),
                               reads=[tsum], writes=[inc])
                            op(dve, lambda e: e.scalar_tensor_tensor(probe.ap[:nq, :], inc.ap[:nq, :], -hw / 2, probe.ap[:nq, :], ALU.add, ALU.add),
                               reads=[inc], writes=[probe])
                        return f

                    hw = 4.0
                    for st in range(cfg.ksteps):
                        steps.append(mk_full(hw) if (st == 0 or na <= 0) else mk_split(hw))
                        hw = hw / 2
                    hwf = hw
                    steps.append(lambda: op(dve, lambda e: e.tensor_scalar(thr.ap[:nq, :], probe.ap[:nq, :], -hwf, None, ALU.add), reads=[probe], writes=[thr]))
                    return steps

                def stageA2(i):
                    r0, nq, s2, topk, n, ktiles, kib = tinfo(i)
                    for c0 in range(0, n, 512):
                        wk = min(512, n - c0)
                        m_ = mb[(c0 // 512) % 2]
                        op(dve, lambda e, m_=m_, c0=c0, wk=wk: e.tensor_scalar(m_.ap[:nq, 0:wk], isc.ap[:nq, c0:c0 + wk], thr.ap[:nq, 0:1], MBIG, ALU.is_lt, ALU.mult),
                           reads=[isc, thr], writes=[m_])
                        pm = ps_s[(c0 // 512) % 2]
                        nb_ = (wk + 127) // 128
                        for b in range(nb_):
                            w = min(128, wk - b * 128)
                            op(pe, lambda e, pm=pm, m_=m_, b=b, w=w: e.transpose(pm.ap[0:w, b, :nq], m_.ap[:nq, b * 128:b * 128 + w], identf.ap[:nq, :nq]),
                               reads=[m_, identf], writes=[pm] if b == 0 else [])
                        pm.w = (pe.sem, pe.sem.n)
                        kt0 = c0 // 128
                        if wk % 128 == 0:
                            op(act, lambda e, pm=pm, kt0=kt0, nb_=nb_: e.activation(maskT.ap[:, kt0:kt0 + nb_, :nq], pm.ap[:, 0:nb_, :nq], AF.Copy),
                               reads=[pm], writes=[maskT])
                        else:
                            nf = wk // 128
                            if nf:
                                op(act, lambda e, pm=pm, kt0=kt0, nf=nf: e.activation(maskT.ap[:, kt0:kt0 + nf, :nq], pm.ap[:, 0:nf, :nq], AF.Copy),
                                   reads=[pm], writes=[maskT])
                            w = wk - nf * 128
                            op(act, lambda e, pm=pm, kt0=kt0, nf=nf, w=w: e.activation(maskT.ap[0:w, kt0 + nf, :nq], pm.ap[0:w, nf, :nq], AF.Copy),
                               reads=[pm], writes=[maskT])

                def stageB(i, pending):
                    r0, nq, s2, topk, n, ktiles, kib = tinfo(i)
                    nk = len(ktiles)
                    kvs = {}

                    def emitS(ti):
                        kidx, ns = ktiles[ti]
                        slot = kvcount[0] % NKB
                        kvcount[0] += 1
                        kb, vbf = ktb[slot], vtb[slot]
                        kvs[ti] = (kb, vbf)
                        k.load(kb.ap[:, :, :ns], SC["KT"][kidx].rearrange("p (h t) -> p h t", t=128)[:, :, :ns], reads=[k.dbuf(("KT", kidx))], writes=[kb])
                        k.load(vbf.ap[:ns, :], SC["VV"][kidx][:ns, :], reads=[k.dbuf(("VV", kidx))], writes=[vbf])
                        for hg in range(2):
                            bi = (2 * ti + hg) % 4
                            pss = sbank[bi]
                            pv = sview[bi]
                            mbc_ = maskT.ap[:ns, ti, :nq].unsqueeze(1).to_broadcast([ns, 4, nq])
                            op(pe, lambda e, pv=pv, mbc_=mbc_, ns=ns: e.matmul(pv[:ns, :, :nq], identb.ap[:ns, :ns], mbc_, start=True, stop=False, skip_group_check=True),
                               reads=[maskT, identb], writes=[pss])
                            for h4 in range(4):
                                hh = hg * 4 + h4
                                op(pe, lambda e, pv=pv, hh=hh, h4=h4, kb=kb, ns=ns: e.matmul(pv[:ns, h4, :nq], kb.ap[:, hh, :ns], qT[s2].ap[:, hh, :nq], start=False, stop=(h4 == 3), skip_group_check=True),
                                   reads=[kb, qT[s2]])
                            pss.w = (pe.sem, pe.sem.n)
                            p_ = pT[bi]
                            op(act, lambda e, pv=pv, p_=p_, ns=ns: e.activation(p_.ap[:ns, :, :nq], pv[:ns, :, :nq], AF.Exp, scale=SCALE),
                               reads=[pss], writes=[p_])

                    def emitPV(ti):
                        kidx, ns = ktiles[ti]
                        kb, vbf = kvs.pop(ti)
                        for hg in range(2):
                            p_ = pT[(2 * ti + hg) % 4]
                            for h4 in range(4):
                                hh = hg * 4 + h4
                                op(pe, lambda e, hg=hg, hh=hh, h4=h4, vbf=vbf, p_=p_, ns=ns, ti=ti: e.matmul(ps_o[hg].ap[:, h4, :nq], vbf.ap[:ns, hh * 128:(hh + 1) * 128], p_.ap[:ns, h4, :nq],
                                                                                                         start=(ti == 0 and h4 == 0), stop=(ti == nk - 1), skip_group_check=True),
                                   reads=[vbf, p_], writes=[ps_o[hg]] if (ti == 0 and h4 == 0) else [])
                            op(pe, lambda e, hg=hg, p_=p_, ns=ns, ti=ti: e.matmul(ps_l[hg].ap[:, :, :nq], onesb.ap[:ns, :], p_.ap[:ns, :, :nq],
                                                                                  start=(ti == 0), stop=(ti == nk - 1), skip_group_check=True),
                               reads=[p_, onesb], writes=[ps_l[hg]] if ti == 0 else [])

                    emitS(0)
                    for ti in range(nk):
                        if ti + 1 < nk:
                            emitS(ti + 1)
                        emitPV(ti)
                        if pending:
                            pending.pop(0)()
                    while pending:
                        pending.pop(0)()
                    for hg in range(2):
                        ps_o[hg].w = (pe.sem, pe.sem.n)
                        ps_l[hg].w = (pe.sem, pe.sem.n)
                        op(dve, lambda e, hg=hg: e.reciprocal(rcp.ap[:, :, :nq], ps_l[hg].ap[:, :, :nq]), reads=[ps_l[hg]], writes=[rcp])
                        op(dve, lambda e, hg=hg: e.tensor_tensor(otmp.ap[:, :, :nq], ps_o[hg].ap[:, :, :nq], rcp.ap[:, :, :nq], ALU.mult),
                           reads=[ps_o[hg], rcp], writes=[otmp])
                        op(dve, lambda e, hg=hg: e.tensor_tensor(oT.ap[:, hg * 4:hg * 4 + 4, :nq], otmp.ap[:, :, :nq], gT[s2].ap[:, hg * 4:hg * 4 + 4, :nq], ALU.mult),
                           reads=[otmp, gT[s2]], writes=[oT])
                    post(l, i, oT, wout, xt[s2], ps_d, pb["z"], pb["stats"], pb["mv"], pb["rstd"], pb["xn"], gbc, bbc)

                loads(0)
                for f in stageA1(0):
                    f()
                stageA2(0)
                for i in range(NT):
                    pend = []
                    if i + 1 < NT:
                        loads(i + 1)
                        pend = stageA1(i + 1)
                    stageB(i, pend)
                    if i + 1 < NT:
                        stageA2(i + 1)
                phase_barrier()
                k.flush()

        def layer_BC(l, kind):
            with ExitStack() as es:
                isB = kind == "B"
                NW = 3 * D if isB else 2 * D
                win = sb(es, "win", [128, 8, NW], BF16)
                wout = sb(es, "wout", [128, 8, D], BF16)
                stage = [sb(es, "wst%d" % s, [128, D], F32) for s in range(2)]
                load_weight_bf16(es, win, I["w_in_b"] if isB else I["w_in_c"], NW, stage, pieces=NW // D)
                load_weight_bf16(es, wout, I["w_out_b"] if isB else I["w_out_c"], D, stage, pieces=1)
                gbc, bbc = load_ln(es, l)
                pb = post_bufs(es)
                xt = [sb(es, "xt%d" % s, [128, D], F32) for s in range(2)]
                xbs = [sb(es, "xb%d" % q, [128, D], BF16) for q in range(2)]
                xTs = [sb(es, "xT%d" % q, [128, 8, 128], BF16) for q in range(2)]
                gs = sb(es, "gs", [128, 8, 128], BF16)
                aT = sb(es, "aT", [128, 8, 128], BF16)
                ut = pb["xn"]
                pstr = psb(es, "pstr", [128, 8, 128], BF16)
                psA = [psb(es, "psA%d" % s, [128, 4, 128], F32) for s in range(2)]
                psB = [psb(es, "psB%d" % s, [128, 4, 128], F32) for s in range(2)]
                psG = [psb(es, "psG%d" % s, [128, 4, 128], F32) for s in range(2)]
                psS = psb(es, "psS", [128, 2, 128], F32)
                HP = 30 if isB else 16
                if isB:
                    cw = sb(es, "cw", [128, 8, 31], F32)
                    cb = sb(es, "cb", [128, 8], F32)
                    ngp = sb(es, "ngp", [128, 8], F32)
                    nbp = sb(es, "nbp", [128, 8], F32)
                    Dg = sb(es, "Dg", [128, 8, 31, 128], BF16)
                    for c in range(8):
                        k.load(cw.ap[:, c, :], I["conv_w"][:, c * 128:(c + 1) * 128].rearrange("j p -> p j"), writes=[cw], allow_slow_non_contiguous=True)
                    k.load(cb.ap[:], I["conv_b"].rearrange("(c p) -> p c", p=128), writes=[cb], allow_slow_non_contiguous=True)
                    k.load(ngp.ap[:], I["ng"].rearrange("(c p) -> p c", p=128), writes=[ngp], allow_slow_non_contiguous=True)
                    k.load(nbp.ap[:], I["nb"].rearrange("(c p) -> p c", p=128), writes=[nbp], allow_slow_non_contiguous=True)
                    for c in range(8):
                        for jj in range(31):
                            op(dve, lambda e, c=c, jj=jj: e.tensor_scalar(Dg.ap[:, c, jj, :], identb.ap[:], cw.ap[:, c, jj:jj + 1], None, ALU.mult),
                               reads=[identb, cw], writes=[Dg])
                    sig = sb(es, "sig", [128, 8, 128], F32)
                    u32 = sb(es, "u32", [128, 8, 128], F32)
                    ext = [sb(es, "ext%d" % s, [128, 8, HP + 128], BF16) for s in range(2)]
                    cT = sb(es, "cT", [128, 8, 128], F32)
                    sq = sig
                    mean = sb(es, "mean", [128, 128], F32)
                    msq = sb(es, "msq", [128, 128], F32)
                    var = sb(es, "var", [128, 128], F32)
                    sn = sb(es, "sn", [128, 8, 128], BF16)
                    prevf = sb(es, "prevf", [32, D], F32)
                    prevb = sb(es, "prevb", [32, D], BF16)
                else:
                    wgf = sb(es, "wgf", [128, 2, 256], F32)
                    scb = sb(es, "scb", [128, D], F32)
                    wg = sb(es, "wg", [128, 4, 2, 256], BF16)
                    k.load(scb.ap[:], I["scale_c"].partition_broadcast(128), writes=[scb])
                    for g in range(4):
                        k.load(wgf.ap[:], I["w_grp"][g].rearrange("(c p) d -> p c d", p=128), writes=[wgf])
                        for cc in range(2):
                            op(dve, lambda e, g=g, cc=cc: e.tensor_tensor(wg.ap[:, g, cc, :], wgf.ap[:, cc, :], scb.ap[:, g * 256:(g + 1) * 256], ALU.mult),
                               reads=[wgf, scb], writes=[wg])
                    ext = [sb(es, "ext%d" % s, [128, 8, HP + 128], F32) for s in range(2)]
                    wa = sb(es, "wa", [128, 2, HP + 128], F32)
                    wb2 = sb(es, "wb2", [128, 2, HP + 128], F32)
                    dT = sb(es, "dT", [128, 8, 128], BF16)
                    ftmp = sb(es, "ftmp", [128, 2, 16], F32)
                    prevf = sb(es, "prevf", [32, D], F32)

                def loads(i):
                    r0, nt = tile_rows(i)
                    k.load(xt[i % 2].ap[:nt, :], xsrc(l, i), reads=[k.dbuf(("x", l, i))], writes=[xt[i % 2]])

                def mk_next(i):
                    r0, nt = tile_rows(i)
                    make_xT(xt[i % 2], xbs[i % 2], pstr, xTs[i % 2], nt)

                def proj(ps2, col0, nt, xT):
                    for fc in range(8):
                        ps = ps2[fc // 4]
                        for kk in range(8):
                            op(pe, lambda e, ps=ps, fc=fc, kk=kk: e.matmul(ps.ap[:, fc % 4, :nt], win.ap[:, kk, col0 + fc * 128:col0 + (fc + 1) * 128], xT.ap[:, kk, :nt],
                                                                         start=(kk == 0), stop=(kk == 7)),
                               reads=[win, xT], writes=[ps] if (fc % 4 == 0 and kk == 0) else [])
                        if fc % 4 == 3:
                            ps.w = (pe.sem, pe.sem.n)

                def state_out(src32, nt, nrows, dst):
                    for c in range(8):
                        op(pe, lambda e, c=c: e.transpose(psG[c // 4].ap[:nt, c % 4, :], src32[:, c, 0:nt], identf.ap[:, :]),
                           reads=[identf], writes=[psG[c // 4]] if c % 4 == 0 else [], extra=[srcbuf[0].w])
                        if c % 4 == 3:
                            psG[c // 4].w = (pe.sem, pe.sem.n)
                    for hf in range(2):
                        op(act, lambda e, hf=hf: e.activation(ut.ap[:nt, hf * 512:(hf + 1) * 512], psG[hf].ap[:nt, :, :], AF.Copy), reads=[psG[hf]], writes=[ut])
                    k.store(dst, ut.ap[nt - nrows:nt, :], reads=[ut])

                srcbuf = [None]

                def compute(i):
                    r0, nt = tile_rows(i)
                    x_ = xt[i % 2]
                    e_cur = ext[i % 2]
                    e_prev = ext[(i + 1) % 2]
                    xT = xTs[i % 2]
                    if i == 0:
                        op(dve, lambda e: e.memset(e_cur.ap[:, :, 0:HP], 0.0), writes=[e_cur])
                    elif i < NTP:
                        op(dve, lambda e: e.tensor_copy(e_cur.ap[:, :, 0:HP], e_prev.ap[:, :, 128:128 + HP]), reads=[e_prev], writes=[e_cur])
                    else:
                        s = i - NTP
                        if isB:
                            k.load(prevf.ap[0:30, :], I["sconv"][s], writes=[prevf])
                            op(act, lambda e: e.activation(prevb.ap[0:30, :], prevf.ap[0:30, :], AF.Copy), reads=[prevf], writes=[prevb])
                            for c in range(8):
                                op(pe, lambda e, c=c: e.transpose(pstr.ap[:, c, 0:30], prevb.ap[0:30, c * 128:(c + 1) * 128], identb.ap[0:30, 0:30]),
                                   reads=[prevb, identb], writes=[pstr] if c == 0 else [])
                            pstr.w = (pe.sem, pe.sem.n)
                            op(dve, lambda e: e.tensor_copy(e_cur.ap[:, :, 0:30], pstr.ap[:, :, 0:30]), reads=[pstr], writes=[e_cur])
                        else:
                            k.load(prevf.ap[0:15, :], I["spool"][s], writes=[prevf])
                            for c in range(8):
                                op(pe, lambda e, c=c: e.transpose(psG[c // 4].ap[:, c % 4, 0:15], prevf.ap[0:15, c * 128:(c + 1) * 128], identf.ap[0:15, 0:15]),
                                   reads=[prevf, identf], writes=[psG[c // 4]] if c % 4 == 0 else [])
                                if c % 4 == 3:
                                    psG[c // 4].w = (pe.sem, pe.sem.n)
                            for hf in range(2):
                                op(dve, lambda e, hf=hf: e.tensor_copy(e_cur.ap[:, hf * 4:hf * 4 + 4, 1:16], psG[hf].ap[:, :, 0:15]), reads=[psG[hf]], writes=[e_cur])
                    if isB:
                        proj(psA, 0, nt, xT)
                        proj(psB, D, nt, xT)
                        proj(psG, 2 * D, nt, xT)
                        if i + 1 < NT:
                            mk_next(i + 1)
                        for hf in range(2):
                            op(act, lambda e, hf=hf: e.activation(sig.ap[:, hf * 4:hf * 4 + 4, :nt], psB[hf].ap[:, :, :nt], AF.Sigmoid), reads=[psB[hf]], writes=[sig])
                            op(act, lambda e, hf=hf: e.activation(gs.ap[:, hf * 4:hf * 4 + 4, :nt], psG[hf].ap[:, :, :nt], AF.Silu), reads=[psG[hf]], writes=[gs])
                            op(dve, lambda e, hf=hf: e.tensor_tensor(u32.ap[:, hf * 4:hf * 4 + 4, :nt], psA[hf].ap[:, :, :nt], sig.ap[:, hf * 4:hf * 4 + 4, :nt], ALU.mult),
                               reads=[psA[hf], sig], writes=[u32])
                        op(act, lambda e: e.activation(e_cur.ap[:, :, HP:HP + nt], u32.ap[:, :, :nt], AF.Copy), reads=[u32], writes=[e_cur])
                        for c in range(8):
                            ps = psA[c // 4]
                            for jj in range(31):
                                op(pe, lambda e, ps=ps, c=c, jj=jj: e.matmul(ps.ap[:, c % 4, :nt], Dg.ap[:, c, jj, :], e_cur.ap[:, c, jj:jj + nt], start=(jj == 0), stop=(jj == 30)),
                                   reads=[Dg, e_cur], writes=[ps] if (c % 4 == 0 and jj == 0) else [])
                            if c % 4 == 3:
                                ps.w = (pe.sem, pe.sem.n)
                        for c in range(8):
                            op(act, lambda e, c=c: e.activation(cT.ap[:, c, :nt], psA[c // 4].ap[:, c % 4, :nt], AF.Identity, bias=cb.ap[:, c:c + 1], scale=1.0),
                               reads=[psA[c // 4], cb], writes=[cT])
                        op(act, lambda e: e.activation(sq.ap[:, :, :nt], cT.ap[:, :, :nt], AF.Square), reads=[cT], writes=[sq])
                        for which, src in ((0, cT), (1, sq)):
                            for c in range(8):
                                op(pe, lambda e, which=which, src=src, c=c: e.matmul(psS.ap[:, which, :nt], onesf.ap[:, :], src.ap[:, c, :nt], start=(c == 0), stop=(c == 7)),
                                   reads=[onesf, src], writes=[psS] if (which == 0 and c == 0) else [])
                        psS.w = (pe.sem, pe.sem.n)
                        op(dve, lambda e: e.tensor_scalar(mean.ap[:, :nt], psS.ap[:, 0, :nt], 1.0 / D, None, ALU.mult), reads=[psS], writes=[mean])
                        op(dve, lambda e: e.tensor_tensor(msq.ap[:, :nt], mean.ap[:, :nt], mean.ap[:, :nt], ALU.mult), reads=[mean], writes=[msq])
                        op(dve, lambda e: e.scalar_tensor_tensor(var.ap[:, :nt], psS.ap[:, 1, :nt], 1.0 / D, msq.ap[:, :nt], ALU.mult, ALU.subtract),
                           reads=[psS, msq], writes=[var])
                        op(act, lambda e: e.activation(var.ap[:, :nt], var.ap[:, :nt], AF.Sqrt, bias=epsb.ap[:, :], scale=1.0), reads=[epsb], writes=[var])
                        op(dve, lambda e: e.reciprocal(var.ap[:, :nt], var.ap[:, :nt]), writes=[var])
                        mbc = mean.ap[:, :nt].unsqueeze(1).to_broadcast([128, 8, nt])
                        vbc = var.ap[:, :nt].unsqueeze(1).to_broadcast([128, 8, nt])
                        op(dve, lambda e: e.tensor_tensor(cT.ap[:, :, :nt], cT.ap[:, :, :nt], mbc, ALU.subtract), reads=[mean], writes=[cT])
                        op(dve, lambda e: e.tensor_tensor(cT.ap[:, :, :nt], cT.ap[:, :, :nt], vbc, ALU.mult), reads=[var], writes=[cT])
                        for c in range(8):
                            op(act, lambda e, c=c: e.activation(sn.ap[:, c, :nt], cT.ap[:, c, :nt], AF.Silu, bias=nbp.ap[:, c:c + 1], scale=ngp.ap[:, c:c + 1]),
                               reads=[cT, ngp, nbp], writes=[sn])
                        op(dve, lambda e: e.tensor_tensor(aT.ap[:, :, :nt], sn.ap[:, :, :nt], gs.ap[:, :, :nt], ALU.mult), reads=[sn, gs], writes=[aT])
                        if i == NTP - 1 or i >= NTP:
                            srcbuf[0] = u32
                            dst = O["ncp"][:, :] if i < NTP else O["ncs"][i - NTP]
                            state_out(u32.ap, nt, 30, dst)
                    else:
                        proj(psA, 0, nt, xT)
                        proj(psG, D, nt, xT)
                        if i + 1 < NT:
                            mk_next(i + 1)
                        for hf in range(2):
                            op(act, lambda e, hf=hf: e.activation(gs.ap[:, hf * 4:hf * 4 + 4, :nt], psG[hf].ap[:, :, :nt], AF.Silu), reads=[psG[hf]], writes=[gs])
                            op(act, lambda e, hf=hf: e.activation(e_cur.ap[:, hf * 4:hf * 4 + 4, HP:HP + nt], psA[hf].ap[:, :, :nt], AF.Copy), reads=[psA[hf]], writes=[e_cur])
                        L = HP + nt
                        for g in range(4):
                            E = e_cur.ap[:, 2 * g:2 * g + 2, :]
                            cur = None
                            bufs = [wa, wb2]
                            for lv in range(g + 1):
                                sh = 2 ** lv
                                lo = 2 ** (lv + 1)
                                dstb = bufs[lv % 2]
                                if lv == 0:
                                    op(dve, lambda e, dstb=dstb, E=E, lo=lo, sh=sh: e.tensor_tensor(dstb.ap[:, :, lo:L], E[:, :, lo:L], E[:, :, lo - sh:L - sh], ALU.add),
                                       reads=[e_cur], writes=[dstb])
                                else:
                                    srcb = bufs[(lv + 1) % 2]
                                    op(dve, lambda e, dstb=dstb, srcb=srcb, lo=lo, sh=sh: e.tensor_tensor(dstb.ap[:, :, lo:L], srcb.ap[:, :, lo:L], srcb.ap[:, :, lo - sh:L - sh], ALU.add),
                                       reads=[srcb], writes=[dstb])
                                cur = dstb
                            w = 2 ** (g + 1)
                            op(dve, lambda e, cur=cur, E=E, g=g, w=w: e.scalar_tensor_tensor(dT.ap[:, 2 * g:2 * g + 2, :nt], cur.ap[:, :, HP:HP + nt], 1.0 / w, E[:, :, HP:HP + nt], ALU.mult, ALU.subtract),
                               reads=[cur, e_cur], writes=[dT])
                            if i == 0:
                                ic = invc.ap[:, g, :].unsqueeze(1).to_broadcast([128, 2, 16])
                                op(dve, lambda e, cur=cur, ic=ic: e.tensor_tensor(ftmp.ap[:, :, :], cur.ap[:, :, HP:HP + 16], ic, ALU.mult), reads=[cur, invc], writes=[ftmp])
                                op(dve, lambda e, E=E, g=g: e.tensor_tensor(dT.ap[:, 2 * g:2 * g + 2, 0:16], ftmp.ap[:, :, :], E[:, :, HP:HP + 16], ALU.subtract),
                                   reads=[ftmp, e_cur], writes=[dT])
                        for g in range(4):
                            for dc in range(2):
                                oc = 2 * g + dc
                                ps = psA[oc // 4]
                                for cc in range(2):
                                    op(pe, lambda e, ps=ps, g=g, dc=dc, cc=cc, oc=oc: e.matmul(ps.ap[:, oc % 4, :nt], wg.ap[:, g, cc, dc * 128:(dc + 1) * 128], dT.ap[:, 2 * g + cc, :nt],
                                                                                          start=(cc == 0), stop=(cc == 1)),
                                       reads=[wg, dT], writes=[ps] if (oc % 4 == 0 and cc == 0) else [])
                                if oc % 4 == 3:
                                    ps.w = (pe.sem, pe.sem.n)
                        for hf in range(2):
                            op(dve, lambda e, hf=hf: e.tensor_tensor(aT.ap[:, hf * 4:hf * 4 + 4, :nt], psA[hf].ap[:, :, :nt], gs.ap[:, hf * 4:hf * 4 + 4, :nt], ALU.mult),
                               reads=[psA[hf], gs], writes=[aT])
                        if i == NTP - 1 or i >= NTP:
                            srcbuf[0] = e_cur
                            dst = O["npp"][:, :] if i < NTP else O["nps"][i - NTP]
                            state_out(e_cur.ap[:, :, HP:HP + 128], nt, 15, dst)
                    post(l, i, aT, wout, x_, psB, pb["z"], pb["stats"], pb["mv"], pb["rstd"], pb["xn"], gbc, bbc)

                loads(0)
                mk_next(0)
                for i in range(NT):
                    if i + 1 < NT:
                        loads(i + 1)
                    compute(i)
                phase_barrier()
                k.flush()

        for spec_ in getattr(cfg, "layers", [("A", 0, 0), ("B", 1), ("C", 2), ("A", 3, 1)]):
            if spec_[0] == "A":
                layer_A(spec_[1], spec_[2])
            else:
                layer_BC(spec_[1], spec_[0])
        toks = k.barrier_tokens()
        k.sp.add(lambda e: e.nop(), toks)
        k.flush()
    return nc


def rope_table(cfg):
    def tab(pos, r):
        half = r // 2
        inv = (ROPE_THETA ** (-np.arange(half, dtype=np.float32) * 2.0 / r)).astype(np.float32)
        ang = pos.astype(np.float32)[:, None] * inv[None, :]
        return np.cos(ang).astype(np.float32), np.sin(ang).astype(np.float32)
    posp = np.arange(cfg.SEQ)
    poss = cfg.PAST + np.arange(DS)
    rows = []
    for pos in (posp, poss):
        c16, s16 = tab(pos, 32)
        c8, s8 = tab(pos, 16)
        rows.append(np.concatenate([c16, s16, c8, s8], axis=1))
    return np.concatenate([rows[0]] + [rows[1]] * cfg.NS, axis=0).astype(np.float32)


def make_in_maps(cfg, inp, ncores, nb):
    f = lambda a: np.ascontiguousarray(np.asarray(a, dtype=np.float32))
    rope = rope_table(cfg)
    maps = []
    NS = cfg.NS
    for c in range(ncores):
        b = c % nb
        ss = slice(c * NS, (c + 1) * NS)
        maps.append(dict(
            xp=f(inp["x_prompt"][b]), xs=f(inp["x_sample"][ss]).reshape(NS * DS, D),
            ck=f(inp["cache_k"][:, ss]).reshape(2, NS, cfg.PAST, D), cv=f(inp["cache_v"][:, ss]).reshape(2, NS, cfg.PAST, D),
            cki=f(inp["cache_kidx"][:, ss]), sconv=f(inp["state_conv"][0, ss]), spool=f(inp["state_pool"][0, ss]),
            w_in_a=f(inp["w_in_a"]), w_out_a=f(inp["w_out_a"]), w_in_b=f(inp["w_in_b"][0]), conv_w=f(inp["conv_w_b"][0]),
            conv_b=f(inp["conv_bias_b"][0]), ng=f(inp["norm_g_b"][0]), nb=f(inp["norm_b_b"][0]), w_out_b=f(inp["w_out_b"][0]),
            w_in_c=f(inp["w_in_c"][0]), w_grp=f(inp["w_grp_c"][0]), scale_c=f(inp["scale_c"][0]), w_out_c=f(inp["w_out_c"][0]),
            ln_g=f(inp["ln_g"]), ln_b=f(inp["ln_b"]), rope=rope,
        ))
    return maps


def assemble(cfg, res, ncores, nb):
    NS = cfg.NS
    R = res
    cat = lambda key, cores: np.stack([R[c][key] for c in cores])
    pc = list(range(nb))
    ac = list(range(ncores))
    yp = cat("yp", pc)
    ys = np.concatenate([R[c]["ys"].reshape(NS, DS, D) for c in ac])
    nkp = np.stack([R[c]["nkp"] for c in pc], axis=1).reshape(2, nb, cfg.SEQ, NH, 128)
    nvp = np.stack([R[c]["nvp"] for c in pc], axis=1).reshape(2, nb, cfg.SEQ, NH, 128)
    nkip = np.stack([R[c]["nkip"] for c in pc], axis=1)
    ncp = cat("ncp", pc)[None]
    npp = cat("npp", pc)[None]
    nks = np.concatenate([R[c]["nks"].reshape(2, NS, DS, NH, 128) for c in ac], axis=1)
    nvs = np.concatenate([R[c]["nvs"].reshape(2, NS, DS, NH, 128) for c in ac], axis=1)
    nkis = np.concatenate([R[c]["nkis"].reshape(2, NS, DS, 64) for c in ac], axis=1)
    ncs = np.concatenate([R[c]["ncs"] for c in ac])[None]
    nps = np.concatenate([R[c]["nps"] for c in ac])[None]
    return tuple(np.ascontiguousarray(a, dtype=np.float32) for a in (yp, ys, nkp, nvp, nkip, ncp, npp, nks, nvs, nkis, ncs, nps))


def kernel(**inputs):
    cfg = Cfg()
    nc = build(cfg)
    maps = make_in_maps(cfg, inputs, 8, 4)
    res = run_bass_kernel_spmd(nc, maps, core_ids=list(range(8)))
    return assemble(cfg, res.results, 8, 4)
```

```python
import numpy as np
from contextlib import ExitStack
import concourse.bass as bass
import concourse.mybir as mybir
from concourse.bass_utils import run_bass_kernel_spmd

F32 = mybir.dt.float32
BF16 = mybir.dt.bfloat16
U8 = mybir.dt.uint8
ALU = mybir.AluOpType
AF = mybir.ActivationFunctionType

D = 1024
NH = 8
A_IN = 4420
DS = 64
ALPHA = 8.0 ** 0.25
EPS = 1e-5
NEG = -1.0e30
MBIG = -30000.0
SCALE = 128.0 ** -0.5
ROPE_THETA = 500000.0


class Cfg:
    def __init__(self, SEQ=8192, NS=4, PAST=1024, topk_p=256, topk_s=256, ksteps=19):
        self.SEQ, self.NS, self.PAST = SEQ, NS, PAST
        self.topk_p, self.topk_s, self.ksteps = topk_p, topk_s, ksteps
        self.NTP = SEQ // 128
        self.NCT = PAST // 128
        self.TT = SEQ + NS * DS
        self.NKT = self.NTP + NS * (self.NCT + 1)
        self.KSW = PAST + DS


class SemC:
    def __init__(self, h):
        self.h = h
        self.n = 0


class Stream:
    def __init__(self, name, semc, serial):
        self.name, self.sem, self.serial = name, semc, serial
        self.ops = []
        self.seen = {}
        self.last = None

    def add(self, fn, waits=(), inc=None):
        ws = [w for w in waits if w is not None]
        if self.serial and self.last is not None:
            ws.append(self.last)
        if inc is None:
            self.sem.n += 1
            tok = (self.sem, self.sem.n)
            spec = (self.sem, 1)
            if self.serial:
                self.last = tok
        else:
            semc, amt = inc
            semc.n += amt
            tok = (semc, semc.n)
            spec = (semc, amt)
        self.ops.append((fn, ws, spec))
        return tok

    def emit(self, eng):
        for fn, ws, spec in self.ops:
            best = {}
            for (s, v) in ws:
                if self.name == "pe" and s is self.sem:
                    continue
                if best.get(s, 0) < v:
                    best[s] = v
            for s, v in best.items():
                if self.seen.get(s, 0) >= v:
                    continue
                self.seen[s] = v
                eng.wait_ge(s.h, v)
            inst = fn(eng)
            inst.then_inc(spec[0].h, spec[1])
        self.ops = []


class Buf:
    def __init__(self, ap=None):
        self.ap = ap
        self.w = None
        self.rs = {}

    def __getitem__(self, k):
        return self.ap[k]


class K:
    def __init__(self, nc, es):
        self.nc, self.es = nc, es
        mk = lambda n: SemC(es.enter_context(nc.semaphore(n)))
        self.pe = Stream("pe", mk("s_pe"), False)
        self.act = Stream("act", mk("s_act"), True)
        self.dve = Stream("dve", mk("s_dve"), True)
        self.pool = Stream("pool", mk("s_pool"), True)
        self.sp = Stream("sp", mk("s_sp"), False)
        self.stq = self.sp
        self.ldsem = [mk("ld%d" % i) for i in range(16)]
        self.stsem = [mk("st%d" % i) for i in range(16)]
        self.ldi = 0
        self.sti = 0
        self.lasttok = {}
        self.dram = {}

    def op(self, stream, fn, reads=(), writes=(), extra=(), inc=None):
        waits = list(extra)
        for b in reads:
            waits.append(b.w)
        for b in writes:
            waits.append(b.w)
            waits.extend(b.rs.items())
        tok = stream.add(fn, waits, inc)
        for b in reads:
            s, v = tok
            if b.rs.get(s, 0) < v:
                b.rs[s] = v
        for b in writes:
            b.w = tok
            b.rs = {}
        return tok

    def dbuf(self, key):
        if key not in self.dram:
            self.dram[key] = Buf()
        return self.dram[key]

    def load(self, out_ap, in_ap, reads=(), writes=(), **kw):
        semc = self.ldsem[self.ldi % len(self.ldsem)]
        self.ldi += 1
        prev = self.lasttok.get(semc)
        tok = self.op(self.sp, lambda e: e.dma_start(out=out_ap, in_=in_ap, **kw), reads, writes,
                      extra=[prev], inc=(semc, 16))
        self.lasttok[semc] = tok
        return tok

    def store(self, out_ap, in_ap, reads=(), writes=(), **kw):
        semc = self.stsem[self.sti % len(self.stsem)]
        self.sti += 1
        prev = self.lasttok.get(semc)
        tok = self.op(self.stq, lambda e: e.dma_start(out=out_ap, in_=in_ap, **kw), reads, writes,
                      extra=[prev], inc=(semc, 16))
        self.lasttok[semc] = tok
        return tok

    def flush(self):
        nc = self.nc
        with nc.Block() as block:
            @block.sync
            def _(e):
                self.sp.emit(e)

            @block.gpsimd
            def _(e):
                self.pool.emit(e)

            @block.scalar
            def _(e):
                self.act.emit(e)

            @block.vector
            def _(e):
                self.dve.emit(e)

            @block.tensor
            def _(e):
                self.pe.emit(e)

    def barrier_tokens(self):
        toks = []
        for s in (self.pe, self.act, self.dve, self.pool):
            if s.sem.n:
                toks.append((s.sem, s.sem.n))
        for semc in self.ldsem + self.stsem:
            if semc.n:
                toks.append((semc, semc.n))
        return toks


def build(cfg):
    nc = bass.Bass("TRN2", target_bir_lowering=False)
    SEQ, NS, PAST, NTP, NCT, TT, NKT = cfg.SEQ, cfg.NS, cfg.PAST, cfg.NTP, cfg.NCT, cfg.TT, cfg.NKT
    NSR = NS * DS

    def din(name, shape, dt=F32):
        return nc.dram_tensor(name, list(shape), dt, kind="ExternalInput").ap()

    def dout(name, shape):
        return nc.dram_tensor(name, list(shape), F32, kind="ExternalOutput").ap()

    def dscr(name, shape, dt):
        if getattr(cfg, "debug", False) and dt == F32:
            return nc.dram_tensor(name, list(shape), dt, kind="ExternalOutput").ap()
        return nc.dram_tensor(name, list(shape), dt).ap()

    I = dict(
        xp=din("xp", [SEQ, D]), xs=din("xs", [NSR, D]),
        ck=din("ck", [2, NS, PAST, D]), cv=din("cv", [2, NS, PAST, D]), cki=din("cki", [2, NS, PAST, 64]),
        sconv=din("sconv", [NS, 30, D]), spool=din("spool", [NS, 15, D]),
        w_in_a=din("w_in_a", [2, D, A_IN]), w_out_a=din("w_out_a", [2, D, D]),
        w_in_b=din("w_in_b", [D, 3 * D]), conv_w=din("conv_w", [31, D]), conv_b=din("conv_b", [D]),
        ng=din("ng", [D]), nb=din("nb", [D]), w_out_b=din("w_out_b", [D, D]),
        w_in_c=din("w_in_c", [D, 2 * D]), w_grp=din("w_grp", [4, 256, 256]), scale_c=din("scale_c", [D]),
        w_out_c=din("w_out_c", [D, D]), ln_g=din("ln_g", [4, D]), ln_b=din("ln_b", [4, D]),
        rope=din("rope", [TT, 48]),
    )
    O = dict(
        yp=dout("yp", [SEQ, D]), ys=dout("ys", [NSR, D]),
        nkp=dout("nkp", [2, SEQ, D]), nvp=dout("nvp", [2, SEQ, D]), nkip=dout("nkip", [2, SEQ, 64]),
        ncp=dout("ncp", [30, D]), npp=dout("npp", [15, D]),
        nks=dout("nks", [2, NSR, D]), nvs=dout("nvs", [2, NSR, D]), nkis=dout("nkis", [2, NSR, 64]),
        ncs=dout("ncs", [NS, 30, D]), nps=dout("nps", [NS, 15, D]),
    )
    NT = NTP + NS
    SC = dict(
        xres=[None] + [dscr("xres%d" % l, [TT, D], F32) for l in (1, 2, 3)],
        QT=dscr("QT", [NT, 128, 1024], BF16), GT=dscr("GT", [NT, 128, 1024], BF16),
        QIT=dscr("QIT", [NT, 128, 256], BF16), SG=dscr("SG", [TT, 4], F32),
        KT=dscr("KT", [NKT, 128, 1024], BF16), VV=dscr("VV", [NKT, 128, 1024], BF16),
        KIP=dscr("KIP", [64, SEQ], BF16), KIS=dscr("KIS", [NS, 64, PAST + 128], BF16),
    )

    def tile_rows(i):
        if i < NTP:
            return i * 128, 128
        return SEQ + (i - NTP) * DS, DS

    def xsrc(l, i):
        r0, nt = tile_rows(i)
        if l == 0:
            if i < NTP:
                return I["xp"][r0:r0 + nt, :]
            return I["xs"][r0 - SEQ:r0 - SEQ + nt, :]
        return SC["xres"][l][r0:r0 + nt, :]

    def xdst(l, i):
        r0, nt = tile_rows(i)
        if l == 3:
            if i < NTP:
                return O["yp"][r0:r0 + nt, :]
            return O["ys"][r0 - SEQ:r0 - SEQ + nt, :]
        return SC["xres"][l + 1][r0:r0 + nt, :]

    with ExitStack() as es0:
        k = K(nc, es0)
        op, pe, act, dve, pool = k.op, k.pe, k.act, k.dve, k.pool

        uid = [0]

        def sb(es, name, shape, dt):
            uid[0] += 1
            return Buf(es.enter_context(nc.sbuf_tensor("%s_%d" % (name, uid[0]), list(shape), dt)))

        def psb(es, name, shape, dt):
            uid[0] += 1
            return Buf(es.enter_context(nc.psum_tensor("%s_%d" % (name, uid[0]), list(shape), dt)))

        identb = sb(es0, "identb", [128, 128], BF16)
        identf = sb(es0, "identf", [128, 128], F32)
        onesb = sb(es0, "onesb", [128, 128], BF16)
        onesf = sb(es0, "onesf", [128, 128], F32)
        invc = sb(es0, "invc", [128, 4, 16], F32)
        for ib in (identb, identf):
            op(pool, lambda e, ib=ib: e.memset(ib.ap[:], 1.0), writes=[ib])
            op(pool, lambda e, ib=ib: e.affine_select(ib.ap[:], ib.ap[:], [[-1, 128]], ALU.is_equal, 0.0,
                                                        base=0, channel_multiplier=1), writes=[ib])
        op(pool, lambda e: e.memset(onesb.ap[:], 1.0), writes=[onesb])
        op(pool, lambda e: e.memset(onesf.ap[:], 1.0), writes=[onesf])
        iot = sb(es0, "iot", [128, 16], F32)
        op(pool, lambda e: e.iota(iot.ap[:], [[1, 16]], base=1, channel_multiplier=0,
                                  allow_small_or_imprecise_dtypes=True), writes=[iot])
        for g in range(4):
            op(dve, lambda e, g=g: e.tensor_scalar(invc.ap[:, g, :], iot.ap[:], float(2 ** (g + 1)), None, ALU.min),
               reads=[iot], writes=[invc])
        op(dve, lambda e: e.reciprocal(invc.ap[:], invc.ap[:]), writes=[invc])

        def phase_barrier():
            toks = k.barrier_tokens()
            for s in (k.pe, k.act, k.dve, k.pool, k.sp):
                s.add(lambda e: e.nop(), toks)

        def load_weight_bf16(es, wtile, src_ap, ncols, stage, pieces=4):
            cw = (ncols + pieces - 1) // pieces
            cnt = 0
            for kk in range(8):
                for pc in range(pieces):
                    c0 = pc * cw
                    c1 = min(ncols, c0 + cw)
                    if c0 >= c1:
                        continue
                    st = stage[cnt % len(stage)]
                    cnt += 1
                    k.load(st.ap[:, 0:c1 - c0], src_ap[kk * 128:(kk + 1) * 128, c0:c1], writes=[st])
                    eng = act if cnt % 2 else dve
                    if eng is act:
                        op(act, lambda e, st=st, kk=kk, c0=c0, c1=c1: e.activation(wtile.ap[:, kk, c0:c1], st.ap[:, 0:c1 - c0], AF.Copy),
                           reads=[st], writes=[wtile])
                    else:
                        op(dve, lambda e, st=st, kk=kk, c0=c0, c1=c1: e.tensor_copy(wtile.ap[:, kk, c0:c1], st.ap[:, 0:c1 - c0]),
                           reads=[st], writes=[wtile])

        def make_xT(xt, xb, pstr, xT, nt):
            op(act, lambda e: e.activation(xb.ap[:nt, :], xt.ap[:nt, :], AF.Copy), reads=[xt], writes=[xb])
            for kk in range(8):
                op(pe, lambda e, kk=kk: e.transpose(pstr.ap[:, kk, :nt], xb.ap[:nt, kk * 128:(kk + 1) * 128], identb.ap[:nt, :nt]),
                   reads=[xb, identb], writes=[pstr] if kk == 0 else [], extra=[pstr.w] if kk else [])
            pstr.w = (pe.sem, pe.sem.n)
            op(dve, lambda e: e.tensor_copy(xT.ap[:, :, :nt], pstr.ap[:, :, :nt]), reads=[pstr], writes=[xT])

        def xcast(xt, xb, nt):
            op(act, lambda e: e.activation(xb.ap[:nt, :], xt.ap[:nt, :], AF.Copy), reads=[xt], writes=[xb])

        def xtrans(xb, pstr, xT, nt):
            for kk in range(8):
                op(pe, lambda e, kk=kk: e.transpose(pstr.ap[:, kk, :nt], xb.ap[:nt, kk * 128:(kk + 1) * 128], identb.ap[:nt, :nt]),
                   reads=[xb, identb], writes=[pstr] if kk == 0 else [])
            pstr.w = (pe.sem, pe.sem.n)
            op(dve, lambda e: e.tensor_copy(xT.ap[:, :, :nt], pstr.ap[:, :, :nt]), reads=[pstr], writes=[xT])

        def post(l, i, aT, wout, xt, ps_y, z, stats, mv, rstd, xn, gbc, bbc):
            r0, nt = tile_rows(i)
            for half in range(2):
                for kk in range(8):
                    op(pe, lambda e, half=half, kk=kk: e.matmul(ps_y[half].ap[:nt, :], aT.ap[:, kk, :nt],
                                                              wout.ap[:, kk, half * 512:(half + 1) * 512],
                                                              start=(kk == 0), stop=(kk == 7)),
                       reads=[aT, wout], writes=[ps_y[half]] if kk == 0 else [])
                ps_y[half].w = (pe.sem, pe.sem.n)
                op(dve, lambda e, half=half: e.scalar_tensor_tensor(z.ap[:nt, half * 512:(half + 1) * 512],
                                                                    xt.ap[:nt, half * 512:(half + 1) * 512], ALPHA,
                                                                    ps_y[half].ap[:nt, :], ALU.mult, ALU.add),
                   reads=[xt, ps_y[half]], writes=[z])
                op(dve, lambda e, half=half: e.bn_stats(stats.ap[:nt, half, :], z.ap[:nt, half * 512:(half + 1) * 512]),
                   reads=[z], writes=[stats])
            op(dve, lambda e: e.bn_aggr(mv.ap[:nt, :], stats.ap[:nt, :, :]), reads=[stats], writes=[mv])
            op(act, lambda e: e.activation(rstd.ap[:nt, :], mv.ap[:nt, 1:2], AF.Sqrt, bias=epsb.ap[:nt, :], scale=1.0),
               reads=[mv, epsb], writes=[rstd])
            op(dve, lambda e: e.reciprocal(rstd.ap[:nt, :], rstd.ap[:nt, :]), writes=[rstd])
            op(dve, lambda e: e.tensor_scalar(xn.ap[:nt, :], z.ap[:nt, :], mv.ap[:nt, 0:1], rstd.ap[:nt, 0:1],
                                              ALU.subtract, ALU.mult), reads=[z, mv, rstd], writes=[xn])
            op(dve, lambda e: e.tensor_tensor(xn.ap[:nt, :], xn.ap[:nt, :], gbc.ap[:nt, :], ALU.mult), reads=[gbc], writes=[xn])
            op(dve, lambda e: e.tensor_tensor(xn.ap[:nt, :], xn.ap[:nt, :], bbc.ap[:nt, :], ALU.add), reads=[bbc], writes=[xn])
            db = k.dbuf(("x", l + 1, i))
            k.store(xdst(l, i), xn.ap[:nt, :], reads=[xn], writes=[db])

        epsb = sb(es0, "epsb", [128, 1], F32)
        op(pool, lambda e: e.memset(epsb.ap[:], EPS), writes=[epsb])

        def load_ln(es, l):
            gbc = sb(es, "gbc", [128, D], F32)
            bbc = sb(es, "bbc", [128, D], F32)
            k.load(gbc.ap[:], I["ln_g"][l, :].partition_broadcast(128), writes=[gbc])
            k.load(bbc.ap[:], I["ln_b"][l, :].partition_broadcast(128), writes=[bbc])
            return gbc, bbc

        def post_bufs(es):
            return dict(
                z=sb(es, "z", [128, D], F32), stats=sb(es, "stats", [128, 2, 6], F32), mv=sb(es, "mv", [128, 2], F32),
                rstd=sb(es, "rstd", [128, 1], F32), xn=sb(es, "xn", [128, D], F32))

        def layer_A(l, j):
            topk_of = lambda i: cfg.topk_p if i < NTP else cfg.topk_s
            with ExitStack() as es:
                win = sb(es, "win", [128, 8, A_IN], BF16)
                stage = [sb(es, "wst%d" % s, [128, 1105], F32) for s in range(3)]
                load_weight_bf16(es, win, I["w_in_a"][j], A_IN, stage, pieces=4)
                xt = [sb(es, "xt%d" % s, [128, D], F32) for s in range(2)]
                rp = [sb(es, "rp%d" % s, [128, 48], F32) for s in range(3)]
                xbs = [sb(es, "xb%d" % s, [128, D], BF16) for s in range(2)]
                xT = sb(es, "xT", [128, 8, 128], BF16)
                hbuf = [sb(es, "h%d" % s, [128, A_IN], F32) for s in range(2)]
                tmp = [sb(es, "rt%d" % s, [128, 16, 16], F32) for s in range(4)]
                hqk = sb(es, "hqk", [128, 2048], BF16)
                vb = sb(es, "vb", [128, D], BF16)
                gs = sb(es, "gs", [128, D], BF16)
                qib = sb(es, "qib", [128, 320], BF16)
                aw = sb(es, "aw", [128, 4], F32)
                sg = sb(es, "sg", [128, 4], F32)
                qT = sb(es, "qT", [128, 8, 128], BF16)
                kT = sb(es, "kT", [128, 8, 128], BF16)
                gT = sb(es, "gT", [128, 8, 128], BF16)
                qiT = sb(es, "qiT", [128, 2, 128], BF16)
                kiT = sb(es, "kiT", [64, 128], BF16)
                ckf = [sb(es, "ckf%d" % s, [128, D], F32) for s in range(2)]
                cvf = [sb(es, "cvf%d" % s, [128, D], F32) for s in range(2)]
                ckif = [sb(es, "ckif%d" % s, [128, 64], F32) for s in range(2)]
                ckb = sb(es, "ckb", [128, D], BF16)
                ckib = sb(es, "ckib", [128, 64], BF16)
                psp = [psb(es, "psp%d" % s, [128, 512], F32) for s in range(5)]
                pstr = psb(es, "pstr", [128, 8, 128], BF16)
                pst2 = [psb(es, "pst2%d" % s, [128, 8, 128], BF16) for s in range(2)]
                chunks = [(c * 512, min(A_IN, (c + 1) * 512)) for c in range(9)]

                def rope_block(hv, c, s, nt, nh, half, rb, h):
                    x1 = hv[:, :, 0:half]
                    x2 = hv[:, :, half:2 * half]
                    cb_ = c.unsqueeze(1).to_broadcast([nt, nh, half])
                    sb_ = s.unsqueeze(1).to_broadcast([nt, nh, half])
                    t = [tt.ap[:nt, 0:nh, 0:half] for tt in tmp]
                    op(dve, lambda e: e.tensor_tensor(t[0], x1, cb_, ALU.mult), reads=[h, rb], writes=[tmp[0]])
                    op(dve, lambda e: e.tensor_tensor(t[1], x2, sb_, ALU.mult), reads=[h, rb], writes=[tmp[1]])
                    op(dve, lambda e: e.tensor_tensor(t[2], x1, sb_, ALU.mult), reads=[h, rb], writes=[tmp[2]])
                    op(dve, lambda e: e.tensor_tensor(t[3], x2, cb_, ALU.mult), reads=[h, rb], writes=[tmp[3]])
                    op(dve, lambda e: e.tensor_tensor(x1, t[0], t[1], ALU.subtract), reads=[tmp[0], tmp[1]], writes=[h])
                    op(dve, lambda e: e.tensor_tensor(x2, t[3], t[2], ALU.add), reads=[tmp[2], tmp[3]], writes=[h])

                def loads(i):
                    r0, nt = tile_rows(i)
                    k.load(xt[i % 2].ap[:nt, :], xsrc(l, i), reads=[k.dbuf(("x", l, i))], writes=[xt[i % 2]])
                    k.load(rp[i % 3].ap[:nt, :], I["rope"][r0:r0 + nt, :], writes=[rp[i % 3]])

                def computeP1(i):
                    r0, nt = tile_rows(i)
                    x_, rp_ = xt[i % 2], rp[i % 3]
                    h = hbuf[i % 2]
                    if i + 1 < NT:
                        xcast(xt[(i + 1) % 2], xbs[(i + 1) % 2], tile_rows(i + 1)[1])
                    xtrans(xbs[i % 2], pstr, xT, nt)
                    for ci, (c0, c1) in enumerate(chunks):
                        ps = psp[ci % 5]
                        for kk in range(8):
                            op(pe, lambda e, ps=ps, kk=kk, c0=c0, c1=c1: e.matmul(ps.ap[:nt, 0:c1 - c0], xT.ap[:, kk, :nt], win.ap[:, kk, c0:c1],
                                                                                  start=(kk == 0), stop=(kk == 7)),
                               reads=[xT, win], writes=[ps] if kk == 0 else [])
                        ps.w = (pe.sem, pe.sem.n)
                        if ci % 2 == 0:
                            op(act, lambda e, ps=ps, c0=c0, c1=c1: e.activation(h.ap[:nt, c0:c1], ps.ap[:nt, 0:c1 - c0], AF.Copy),
                               reads=[ps], writes=[h])
                        else:
                            op(dve, lambda e, ps=ps, c0=c0, c1=c1: e.tensor_copy(h.ap[:nt, c0:c1], ps.ap[:nt, 0:c1 - c0]),
                               reads=[ps], writes=[h])

                def computeP2(i):
                    r0, nt = tile_rows(i)
                    x_, rp_ = xt[i % 2], rp[i % 3]
                    h = hbuf[i % 2]
                    hv = h.ap[:nt, 0:2048].rearrange("p (h d) -> p h d", d=128)
                    rope_block(hv, rp_.ap[:nt, 0:16], rp_.ap[:nt, 16:32], nt, 16, 16, rp_, h)
                    hv2 = h.ap[:nt, 4096:4416].rearrange("p (h d) -> p h d", d=64)
                    rope_block(hv2, rp_.ap[:nt, 32:40], rp_.ap[:nt, 40:48], nt, 5, 8, rp_, h)
                    if i < NTP:
                        ko, vo, kio = O["nkp"][j, r0:r0 + nt, :], O["nvp"][j, r0:r0 + nt, :], O["nkip"][j, r0:r0 + nt, :]
                    else:
                        q0 = r0 - SEQ
                        ko, vo, kio = O["nks"][j, q0:q0 + nt, :], O["nvs"][j, q0:q0 + nt, :], O["nkis"][j, q0:q0 + nt, :]
                    k.store(ko, h.ap[:nt, 1024:2048], reads=[h])
                    k.store(vo, h.ap[:nt, 2048:3072], reads=[h])
                    k.store(kio, h.ap[:nt, 4352:4416], reads=[h])
                    op(dve, lambda e: e.tensor_copy(hqk.ap[:nt, :], h.ap[:nt, 0:2048]), reads=[h], writes=[hqk])
                    op(act, lambda e: e.activation(vb.ap[:nt, :], h.ap[:nt, 2048:3072], AF.Copy), reads=[h], writes=[vb])
                    op(act, lambda e: e.activation(gs.ap[:nt, :], h.ap[:nt, 3072:4096], AF.Silu), reads=[h], writes=[gs])
                    op(dve, lambda e: e.tensor_scalar(sg.ap[:nt, :], h.ap[:nt, 4416:4420], 0.0, 2.0, ALU.is_ge, ALU.mult),
                       reads=[h], writes=[sg])
                    op(dve, lambda e: e.tensor_scalar(sg.ap[:nt, :], sg.ap[:nt, :], -1.0, None, ALU.add), writes=[sg])
                    op(dve, lambda e: e.scalar_tensor_tensor(aw.ap[:nt, :], h.ap[:nt, 4416:4420], 0.0625, sg.ap[:nt, :], ALU.mult, ALU.mult),
                       reads=[h, sg], writes=[aw])
                    for hh in range(4):
                        op(dve, lambda e, hh=hh: e.tensor_scalar(qib.ap[:nt, hh * 64:(hh + 1) * 64], h.ap[:nt, 4096 + hh * 64:4096 + (hh + 1) * 64],
                                                                 aw.ap[:nt, hh:hh + 1], None, ALU.mult), reads=[h, aw], writes=[qib])
                    op(dve, lambda e: e.tensor_copy(qib.ap[:nt, 256:320], h.ap[:nt, 4352:4416]), reads=[h], writes=[qib])
                    def trn(ps, src, ncol, dst, dsl, evac):
                        nblk = (ncol + 127) // 128
                        for b in range(nblk):
                            w = min(128, ncol - b * 128)
                            op(pe, lambda e, b=b, w=w: e.transpose(ps.ap[0:w, b, :nt], src.ap[:nt, b * 128:b * 128 + w], identb.ap[:nt, :nt]),
                               reads=[src, identb], writes=[ps] if b == 0 else [])
                        ps.w = (pe.sem, pe.sem.n)
                    trn(pst2[0], hqk, 1024, None, None, None)
                    op(act, lambda e: e.activation(qT.ap[:, :, :nt], pst2[0].ap[:, :, :nt], AF.Copy), reads=[pst2[0]], writes=[qT])
                    for b in range(8):
                        op(pe, lambda e, b=b: e.transpose(pst2[1].ap[:, b, :nt], hqk.ap[:nt, 1024 + b * 128:1024 + (b + 1) * 128], identb.ap[:nt, :nt]),
                           reads=[hqk, identb], writes=[pst2[1]] if b == 0 else [])
                    pst2[1].w = (pe.sem, pe.sem.n)
                    op(dve, lambda e: e.tensor_copy(kT.ap[:, :, :nt], pst2[1].ap[:, :, :nt]), reads=[pst2[1]], writes=[kT])
                    trn(pst2[0], gs, 1024, None, None, None)
                    op(act, lambda e: e.activation(gT.ap[:, :, :nt], pst2[0].ap[:, :, :nt], AF.Copy), reads=[pst2[0]], writes=[gT])
                    trn(pst2[1], qib, 320, None, None, None)
                    op(dve, lambda e: e.tensor_copy(qiT.ap[:, :, :nt], pst2[1].ap[:, 0:2, :nt]), reads=[pst2[1]], writes=[qiT])
                    op(dve, lambda e: e.tensor_copy(kiT.ap[:, :nt], pst2[1].ap[0:64, 2, :nt]), reads=[pst2[1]], writes=[kiT])
                    kidx = i if i < NTP else NTP + (i - NTP) * (NCT + 1) + NCT
                    k.store(SC["QT"][i].rearrange("p (h t) -> p h t", t=128)[:, :, :nt], qT.ap[:, :, :nt], reads=[qT], writes=[k.dbuf(("QT", i))])
                    k.store(SC["GT"][i].rearrange("p (h t) -> p h t", t=128)[:, :, :nt], gT.ap[:, :, :nt], reads=[gT], writes=[k.dbuf(("GT", i))])
                    k.store(SC["QIT"][i].rearrange("p (h t) -> p h t", t=128)[:, :, :nt], qiT.ap[:, :, :nt], reads=[qiT], writes=[k.dbuf(("QIT", i))])
                    k.store(SC["KT"][kidx].rearrange("p (h t) -> p h t", t=128)[:, :, :nt], kT.ap[:, :, :nt], reads=[kT], writes=[k.dbuf(("KT", kidx))])
                    k.store(SC["VV"][kidx][:nt, :], vb.ap[:nt, :], reads=[vb], writes=[k.dbuf(("VV", kidx))])
                    k.store(SC["SG"][r0:r0 + nt, :], sg.ap[:nt, :], reads=[sg], writes=[k.dbuf(("SG", i))])
                    if i < NTP:
                        k.store(SC["KIP"][:, r0:r0 + nt], kiT.ap[:, :nt], reads=[kiT], writes=[k.dbuf(("KIP",))])
                    else:
                        k.store(SC["KIS"][i - NTP][:, PAST:PAST + nt], kiT.ap[:, :nt], reads=[kiT], writes=[k.dbuf(("KIS", i - NTP))])

                def cache_tile(s, c):
                    n = s * NCT + c
                    kf, vf, kif = ckf[n % 2], cvf[n % 2], ckif[n % 2]
                    k.load(kf.ap[:], I["ck"][j, s, c * 128:(c + 1) * 128, :], writes=[kf])
                    k.load(vf.ap[:], I["cv"][j, s, c * 128:(c + 1) * 128, :], writes=[vf])
                    k.load(kif.ap[:], I["cki"][j, s, c * 128:(c + 1) * 128, :], writes=[kif])
                    kidx = NTP + s * (NCT + 1) + c
                    op(act, lambda e: e.activation(ckb.ap[:], kf.ap[:], AF.Copy), reads=[kf], writes=[ckb])
                    for b in range(8):
                        op(pe, lambda e, b=b: e.transpose(pst2[0].ap[:, b, :], ckb.ap[:, b * 128:(b + 1) * 128], identb.ap[:]),
                           reads=[ckb, identb], writes=[pst2[0]] if b == 0 else [])
                    pst2[0].w = (pe.sem, pe.sem.n)
                    op(dve, lambda e: e.tensor_copy(kT.ap[:], pst2[0].ap[:]), reads=[pst2[0]], writes=[kT])
                    k.store(SC["KT"][kidx].rearrange("p (h t) -> p h t", t=128), kT.ap[:], reads=[kT], writes=[k.dbuf(("KT", kidx))])
                    op(act, lambda e: e.activation(vb.ap[:], vf.ap[:], AF.Copy), reads=[vf], writes=[vb])
                    k.store(SC["VV"][kidx], vb.ap[:], reads=[vb], writes=[k.dbuf(("VV", kidx))])
                    op(act, lambda e: e.activation(ckib.ap[:], kif.ap[:], AF.Copy), reads=[kif], writes=[ckib])
                    op(pe, lambda e: e.transpose(pst2[1].ap[0:64, 0, :], ckib.ap[:, :], identb.ap[:]), reads=[ckib, identb], writes=[pst2[1]])
                    op(dve, lambda e: e.tensor_copy(kiT.ap[:, :], pst2[1].ap[0:64, 0, :]), reads=[pst2[1]], writes=[kiT])
                    k.store(SC["KIS"][s][:, c * 128:(c + 1) * 128], kiT.ap[:, :], reads=[kiT], writes=[k.dbuf(("KIS", s))])

                loads(0)
                if NT > 1:
                    loads(1)
                xcast(xt[0], xbs[0], tile_rows(0)[1])
                computeP1(0)
                for i in range(NT):
                    if i + 2 < NT:
                        loads(i + 2)
                    if i + 1 < NT:
                        computeP1(i + 1)
                    computeP2(i)
                for s in range(NS):
                    for c in range(NCT):
                        cache_tile(s, c)
                phase_barrier()
                k.flush()

            if getattr(cfg, "skip_p2", False):
                return
            with ExitStack() as es:
                NMAX = max(SEQ, PAST + 128)
                wout = sb(es, "wout", [128, 8, D], BF16)
                stage = [sb(es, "wst%d" % s, [128, D], F32) for s in range(2)]
                load_weight_bf16(es, wout, I["w_out_a"][j], D, stage, pieces=1)
                gbc, bbc = load_ln(es, l)
                pb = post_bufs(es)
                ki2 = sb(es, "ki2", [128, NMAX], BF16)
                kis = sb(es, "kis", [128, PAST + 128], BF16)
                isc = sb(es, "isc", [128, NMAX], F32)
                junk = sb(es, "junk", [128, NMAX], U8)
                mb = [sb(es, "mb%d" % s, [128, 512], F32) for s in range(2)]
                maskT = sb(es, "maskT", [128, NMAX // 128 + 1, 128], BF16)
                rl = [sb(es, "rl%d" % s, [128, 512], F32) for s in range(2)]
                qT = [sb(es, "qT%d" % s, [128, 8, 128], BF16) for s in range(2)]
                gT = [sb(es, "gT%d" % s, [128, 8, 128], BF16) for s in range(2)]
                qiT = [sb(es, "qiT%d" % s, [128, 2, 128], BF16) for s in range(2)]
                sg = [sb(es, "sg%d" % s, [128, 4], F32) for s in range(2)]
                xt = [sb(es, "xt%d" % s, [128, D], F32) for s in range(2)]
                NKB = 4
                ktb = [sb(es, "ktb%d" % s, [128, 8, 128], BF16) for s in range(NKB)]
                vtb = [sb(es, "vtb%d" % s, [128, D], BF16) for s in range(NKB)]
                pT = [sb(es, "pT%d" % s, [128, 4, 128], BF16) for s in range(4)]
                oT = sb(es, "oT", [128, 8, 128], BF16)
                rcp = sb(es, "rcp", [128, 4, 128], F32)
                otmp = sb(es, "otmp", [128, 4, 128], F32)
                probe = sb(es, "probe", [128, 1], F32)
                cnt = sb(es, "cnt", [128, 1], F32)
                inc = sb(es, "inc", [128, 1], F32)
                thr = sb(es, "thr", [128, 1], F32)
                accA = sb(es, "accA", [128, 1], F32)
                tsum = sb(es, "tsum", [128, 1], F32)
                junk2 = sb(es, "junk2", [128, NMAX], U8)
                ps_d = [psb(es, "ps_d%d" % s, [128, 512], F32) for s in range(2)]
                ps_s = [psb(es, "ps_s%d" % s, [128, 4, 128], F32) for s in range(2)]
                ps_o = [psb(es, "ps_o%d" % s, [128, 4, 128], F32) for s in range(2)]
                ps_l = [psb(es, "ps_l%d" % s, [128, 4, 128], F32) for s in range(2)]
                sbank = [ps_s[0], ps_s[1], ps_d[0], ps_d[1]]
                sview = [ps_s[0].ap, ps_s[1].ap, ps_d[0].ap.rearrange("p (h q) -> p h q", q=128), ps_d[1].ap.rearrange("p (h q) -> p h q", q=128)]
                kipb = k.dbuf(("KIP",))
                k.load(ki2.ap[0:64, 0:SEQ], SC["KIP"][:, :], reads=[kipb], writes=[ki2])
                k.load(ki2.ap[64:128, 0:SEQ], SC["KIP"][:, :], reads=[kipb], writes=[ki2])
                kvcount = [0]

                def loads(i):
                    r0, nt = tile_rows(i)
                    s2 = i % 2
                    k.load(qT[s2].ap[:, :, :nt], SC["QT"][i].rearrange("p (h t) -> p h t", t=128)[:, :, :nt], reads=[k.dbuf(("QT", i))], writes=[qT[s2]])
                    k.load(gT[s2].ap[:, :, :nt], SC["GT"][i].rearrange("p (h t) -> p h t", t=128)[:, :, :nt], reads=[k.dbuf(("GT", i))], writes=[gT[s2]])
                    k.load(qiT[s2].ap[:, :, :nt], SC["QIT"][i].rearrange("p (h t) -> p h t", t=128)[:, :, :nt], reads=[k.dbuf(("QIT", i))], writes=[qiT[s2]])
                    k.load(sg[s2].ap[:nt, :], SC["SG"][r0:r0 + nt, :], reads=[k.dbuf(("SG", i))], writes=[sg[s2]])
                    k.load(xt[s2].ap[:nt, :], xsrc(l, i), reads=[k.dbuf(("x", l, i))], writes=[xt[s2]])

                def tinfo(i):
                    r0, nq = tile_rows(i)
                    if i < NTP:
                        n = 128 * (i + 1)
                        ktiles = [(t, 128) for t in range(i + 1)]
                        kib = ki2
                    else:
                        s = i - NTP
                        n = PAST + DS
                        base = NTP + s * (NCT + 1)
                        ktiles = [(base + c, 128) for c in range(NCT)] + [(base + NCT, DS)]
                        kib = kis
                    return r0, nq, i % 2, topk_of(i), n, ktiles, kib

                def stageA1(i):
                    r0, nq, s2, topk, n, ktiles, kib = tinfo(i)
                    if i >= NTP:
                        s = i - NTP
                        ksb = k.dbuf(("KIS", s))
                        k.load(kis.ap[0:64, 0:n], SC["KIS"][s][:, 0:n], reads=[ksb], writes=[kis])
                        k.load(kis.ap[64:128, 0:n], SC["KIS"][s][:, 0:n], reads=[ksb], writes=[kis])
                    for c0 in range(0, n, 512):
                        wk = min(512, n - c0)
                        for hh in range(4):
                            b0 = 64 * (hh % 2)
                            ps = ps_d[hh % 2]
                            op(pe, lambda e, ps=ps, hh=hh, b0=b0, c0=c0, wk=wk: e.matmul(ps.ap[:nq, 0:wk], qiT[s2].ap[b0:b0 + 64, hh // 2, :nq],
                                                                                          kib.ap[b0:b0 + 64, c0:c0 + wk], start=True, stop=True),
                               reads=[qiT[s2], kib], writes=[ps])
                            r_ = rl[hh % 2]
                            op(act, lambda e, ps=ps, r_=r_, wk=wk: e.activation(r_.ap[:nq, 0:wk], ps.ap[:nq, 0:wk], AF.Relu), reads=[ps], writes=[r_])
                            if hh == 0:
                                op(dve, lambda e, r_=r_, c0=c0, wk=wk: e.tensor_scalar(isc.ap[:nq, c0:c0 + wk], r_.ap[:nq, 0:wk], sg[s2].ap[:nq, 0:1], None, ALU.mult),
                                   reads=[r_, sg[s2]], writes=[isc])
                            else:
                                op(dve, lambda e, r_=r_, c0=c0, wk=wk, hh=hh: e.scalar_tensor_tensor(isc.ap[:nq, c0:c0 + wk], r_.ap[:nq, 0:wk], sg[s2].ap[:nq, hh:hh + 1],
                                                                                                    isc.ap[:nq, c0:c0 + wk], ALU.mult, ALU.add),
                                   reads=[r_, sg[s2]], writes=[isc])
                    if i < NTP:
                        op(dve, lambda e: e.memset(isc.ap[0:64, n - 64:n], NEG), writes=[isc])
                    if i < NTP and n <= topk:
                        op(dve, lambda e: e.memset(thr.ap[:], -1.0e29), writes=[thr])
                        return []
                    steps = []
                    op(dve, lambda e: e.memset(probe.ap[:], 0.0), writes=[probe])
                    n1 = min(n, max(128, int(round(n * getattr(cfg, 'split_frac', 0.5) / 128.0)) * 128))
                    if not getattr(cfg, "split", True):
                        n1 = n
                    na = n - n1

                    def mk_full(hw):
                        def f():
                            op(dve, lambda e: e.tensor_scalar(junk.ap[:nq, 0:n], isc.ap[:nq, 0:n], probe.ap[:nq, 0:1], None, ALU.is_ge, ALU.add,
                                                              accum_out=cnt.ap[:nq, :]), reads=[isc, probe], writes=[junk, cnt])
                            op(dve, lambda e: e.tensor_scalar(inc.ap[:nq, :], cnt.ap[:nq, :], float(topk), hw, ALU.is_ge, ALU.mult),
                               reads=[cnt], writes=[inc])
                            op(dve, lambda e: e.scalar_tensor_tensor(probe.ap[:nq, :], inc.ap[:nq, :], -hw / 2, probe.ap[:nq, :], ALU.add, ALU.add),
                               reads=[inc], writes=[probe])
                        return f

                    def mk_split(hw):
                        def f():
                            op(dve, lambda e: e.tensor_scalar(junk.ap[:nq, 0:n1], isc.ap[:nq, 0:n1], probe.ap[:nq, 0:1], None, ALU.is_ge, ALU.add,
                                                              accum_out=cnt.ap[:nq, :]), reads=[isc, probe], writes=[junk, cnt])
                            op(act, lambda e: e.activation(junk2.ap[:nq, 0:na], isc.ap[:nq, n1:n], AF.Sign, bias=probe.ap[:nq, 0:1], scale=-1.0,
                                                           accum_out=accA.ap[:nq, :]), reads=[isc, probe], writes=[junk2, accA])
                            op(dve, lambda e: e.scalar_tensor_tensor(tsum.ap[:nq, :], cnt.ap[:nq, :], 2.0, accA.ap[:nq, :], ALU.mult, ALU.subtract),
                               reads=[cnt, accA], writes=[tsum])
                            op(dve, lambda e: e.tensor_scalar(inc.ap[:nq, :], tsum.ap[:nq, :], float(2 * topk - na), hw, ALU.is_ge, ALU.mult),
                               reads=[tsum], writes=[inc])
                            op(dve, lambda e: e.scalar_tensor_tensor(probe.ap[:nq, :], inc.ap[:nq, :], -hw / 2, probe.ap[:nq, :], ALU.add, ALU.add),
                               reads=[inc], writes=[probe])
                        return f

                    hw = 4.0
                    for st in range(cfg.ksteps):
                        steps.append(mk_full(hw) if (st == 0 or na <= 0) else mk_split(hw))
                        hw = hw / 2
                    hwf = hw
                    steps.append(lambda: op(dve, lambda e: e.tensor_scalar(thr.ap[:nq, :], probe.ap[:nq, :], -hwf, None, ALU.add), reads=[probe], writes=[thr]))
                    return steps

                def stageA2(i):
                    r0, nq, s2, topk, n, ktiles, kib = tinfo(i)
                    for c0 in range(0, n, 512):
                        wk = min(512, n - c0)
                        m_ = mb[(c0 // 512) % 2]
                        op(dve, lambda e, m_=m_, c0=c0, wk=wk: e.tensor_scalar(m_.ap[:nq, 0:wk], isc.ap[:nq, c0:c0 + wk], thr.ap[:nq, 0:1], MBIG, ALU.is_lt, ALU.mult),
                           reads=[isc, thr], writes=[m_])
                        pm = ps_s[(c0 // 512) % 2]
                        nb_ = (wk + 127) // 128
                        for b in range(nb_):
                            w = min(128, wk - b * 128)
                            op(pe, lambda e, pm=pm, m_=m_, b=b, w=w: e.transpose(pm.ap[0:w, b, :nq], m_.ap[:nq, b * 128:b * 128 + w], identf.ap[:nq, :nq]),
                               reads=[m_, identf], writes=[pm] if b == 0 else [])
                        pm.w = (pe.sem, pe.sem.n)
                        kt0 = c0 // 128
                        if wk % 128 == 0:
                            op(act, lambda e, pm=pm, kt0=kt0, nb_=nb_: e.activation(maskT.ap[:, kt0:kt0 + nb_, :nq], pm.ap[:, 0:nb_, :nq], AF.Copy),
                               reads=[pm], writes=[maskT])
                        else:
                            nf = wk // 128
                            if nf:
                                op(act, lambda e, pm=pm, kt0=kt0, nf=nf: e.activation(maskT.ap[:, kt0:kt0 + nf, :nq], pm.ap[:, 0:nf, :nq], AF.Copy),
                                   reads=[pm], writes=[maskT])
                            w = wk - nf * 128
                            op(act, lambda e, pm=pm, kt0=kt0, nf=nf, w=w: e.activation(maskT.ap[0:w, kt0 + nf, :nq], pm.ap[0:w, nf, :nq], AF.Copy),
                               reads=[pm], writes=[maskT])

                def stageB(i, pending):
                    r0, nq, s2, topk, n, ktiles, kib = tinfo(i)
                    nk = len(ktiles)
                    kvs = {}

                    def emitS(ti):
                        kidx, ns = ktiles[ti]
                        slot = kvcount[0] % NKB
                        kvcount[0] += 1
                        kb, vbf = ktb[slot], vtb[slot]
                        kvs[ti] = (kb, vbf)
                        k.load(kb.ap[:, :, :ns], SC["KT"][kidx].rearrange("p (h t) -> p h t", t=128)[:, :, :ns], reads=[k.dbuf(("KT", kidx))], writes=[kb])
                        k.load(vbf.ap[:ns, :], SC["VV"][kidx][:ns, :], reads=[k.dbuf(("VV", kidx))], writes=[vbf])
                        for hg in range(2):
                            bi = (2 * ti + hg) % 4
                            pss = sbank[bi]
                            pv = sview[bi]
                            mbc_ = maskT.ap[:ns, ti, :nq].unsqueeze(1).to_broadcast([ns, 4, nq])
                            op(pe, lambda e, pv=pv, mbc_=mbc_, ns=ns: e.matmul(pv[:ns, :, :nq], identb.ap[:ns, :ns], mbc_, start=True, stop=False, skip_group_check=True),
                               reads=[maskT, identb], writes=[pss])
                            for h4 in range(4):
                                hh = hg * 4 + h4
                                op(pe, lambda e, pv=pv, hh=hh, h4=h4, kb=kb, ns=ns: e.matmul(pv[:ns, h4, :nq], kb.ap[:, hh, :ns], qT[s2].ap[:, hh, :nq], start=False, stop=(h4 == 3), skip_group_check=True),
                                   reads=[kb, qT[s2]])
                            pss.w = (pe.sem, pe.sem.n)
                            p_ = pT[bi]
                            op(act, lambda e, pv=pv, p_=p_, ns=ns: e.activation(p_.ap[:ns, :, :nq], pv[:ns, :, :nq], AF.Exp, scale=SCALE),
                               reads=[pss], writes=[p_])

                    def emitPV(ti):
                        kidx, ns = ktiles[ti]
                        kb, vbf = kvs.pop(ti)
                        for hg in range(2):
                            p_ = pT[(2 * ti + hg) % 4]
                            for h4 in range(4):
                                hh = hg * 4 + h4
                                op(pe, lambda e, hg=hg, hh=hh, h4=h4, vbf=vbf, p_=p_, ns=ns, ti=ti: e.matmul(ps_o[hg].ap[:, h4, :nq], vbf.ap[:ns, hh * 128:(hh + 1) * 128], p_.ap[:ns, h4, :nq],
                                                                                                         start=(ti == 0 and h4 == 0), stop=(ti == nk - 1), skip_group_check=True),
                                   reads=[vbf, p_], writes=[ps_o[hg]] if (ti == 0 and h4 == 0) else [])
                            op(pe, lambda e, hg=hg, p_=p_, ns=ns, ti=ti: e.matmul(ps_l[hg].ap[:, :, :nq], onesb.ap[:ns, :], p_.ap[:ns, :, :nq],
                                                                                  start=(ti == 0), stop=(ti == nk - 1), skip_group_check=True),
                               reads=[p_, onesb], writes=[ps_l[hg]] if ti == 0 else [])

                    emitS(0)
                    for ti in range(nk):
                        if ti + 1 < nk:
                            emitS(ti + 1)
                        emitPV(ti)
                        if pending:
                            pending.pop(0)()
                    while pending:
                        pending.pop(0)()
                    for hg in range(2):
                        ps_o[hg].w = (pe.sem, pe.sem.n)
                        ps_l[hg].w = (pe.sem, pe.sem.n)
                        op(dve, lambda e, hg=hg: e.reciprocal(rcp.ap[:, :, :nq], ps_l[hg].ap[:, :, :nq]), reads=[ps_l[hg]], writes=[rcp])
                        op(dve, lambda e, hg=hg: e.tensor_tensor(otmp.ap[:, :, :nq], ps_o[hg].ap[:, :, :nq], rcp.ap[:, :, :nq], ALU.mult),
                           reads=[ps_o[hg], rcp], writes=[otmp])
                        op(dve, lambda e, hg=hg: e.tensor_tensor(oT.ap[:, hg * 4:hg * 4 + 4, :nq], otmp.ap[:, :, :nq], gT[s2].ap[:, hg * 4:hg * 4 + 4, :nq], ALU.mult),
                           reads=[otmp, gT[s2]], writes=[oT])
                    post(l, i, oT, wout, xt[s2], ps_d, pb["z"], pb["stats"], pb["mv"], pb["rstd"], pb["xn"], gbc, bbc)

                loads(0)
                for f in stageA1(0):
                    f()
                stageA2(0)
                for i in range(NT):
                    pend = []
                    if i + 1 < NT:
                        loads(i + 1)
                        pend = stageA1(i + 1)
                    stageB(i, pend)
                    if i + 1 < NT:
                        stageA2(i + 1)
                phase_barrier()
                k.flush()

        def layer_BC(l, kind):
            with ExitStack() as es:
                isB = kind == "B"
                NW = 3 * D if isB else 2 * D
                win = sb(es, "win", [128, 8, NW], BF16)
                wout = sb(es, "wout", [128, 8, D], BF16)
                stage = [sb(es, "wst%d" % s, [128, D], F32) for s in range(2)]
                load_weight_bf16(es, win, I["w_in_b"] if isB else I["w_in_c"], NW, stage, pieces=NW // D)
                load_weight_bf16(es, wout, I["w_out_b"] if isB else I["w_out_c"], D, stage, pieces=1)
                gbc, bbc = load_ln(es, l)
                pb = post_bufs(es)
                xt = [sb(es, "xt%d" % s, [128, D], F32) for s in range(2)]
                xbs = [sb(es, "xb%d" % q, [128, D], BF16) for q in range(2)]
                xTs = [sb(es, "xT%d" % q, [128, 8, 128], BF16) for q in range(2)]
                gs = sb(es, "gs", [128, 8, 128], BF16)
                aT = sb(es, "aT", [128, 8, 128], BF16)
                ut = pb["xn"]
                pstr = psb(es, "pstr", [128, 8, 128], BF16)
                psA = [psb(es, "psA%d" % s, [128, 4, 128], F32) for s in range(2)]
                psB = [psb(es, "psB%d" % s, [128, 4, 128], F32) for s in range(2)]
                psG = [psb(es, "psG%d" % s, [128, 4, 128], F32) for s in range(2)]
                psS = psb(es, "psS", [128, 2, 128], F32)
                HP = 30 if isB else 16
                if isB:
                    cw = sb(es, "cw", [128, 8, 31], F32)
                    cb = sb(es, "cb", [128, 8], F32)
                    ngp = sb(es, "ngp", [128, 8], F32)
                    nbp = sb(es, "nbp", [128, 8], F32)
                    Dg = sb(es, "Dg", [128, 8, 31, 128], BF16)
                    for c in range(8):
                        k.load(cw.ap[:, c, :], I["conv_w"][:, c * 128:(c + 1) * 128].rearrange("j p -> p j"), writes=[cw], allow_slow_non_contiguous=True)
                    k.load(cb.ap[:], I["conv_b"].rearrange("(c p) -> p c", p=128), writes=[cb], allow_slow_non_contiguous=True)
                    k.load(ngp.ap[:], I["ng"].rearrange("(c p) -> p c", p=128), writes=[ngp], allow_slow_non_contiguous=True)
                    k.load(nbp.ap[:], I["nb"].rearrange("(c p) -> p c", p=128), writes=[nbp], allow_slow_non_contiguous=True)
                    for c in range(8):
                        for jj in range(31):
                            op(dve, lambda e, c=c, jj=jj: e.tensor_scalar(Dg.ap[:, c, jj, :], identb.ap[:], cw.ap[:, c, jj:jj + 1], None, ALU.mult),
                               reads=[identb, cw], writes=[Dg])
                    sig = sb(es, "sig", [128, 8, 128], F32)
                    u32 = sb(es, "u32", [128, 8, 128], F32)
                    ext = [sb(es, "ext%d" % s, [128, 8, HP + 128], BF16) for s in range(2)]
                    cT = sb(es, "cT", [128, 8, 128], F32)
                    sq = sig
                    mean = sb(es, "mean", [128, 128], F32)
                    msq = sb(es, "msq", [128, 128], F32)
                    var = sb(es, "var", [128, 128], F32)
                    sn = sb(es, "sn", [128, 8, 128], BF16)
                    prevf = sb(es, "prevf", [32, D], F32)
                    prevb = sb(es, "prevb", [32, D], BF16)
                else:
                    wgf = sb(es, "wgf", [128, 2, 256], F32)
                    scb = sb(es, "scb", [128, D], F32)
                    wg = sb(es, "wg", [128, 4, 2, 256], BF16)
                    k.load(scb.ap[:], I["scale_c"].partition_broadcast(128), writes=[scb])
                    for g in range(4):
                        k.load(wgf.ap[:], I["w_grp"][g].rearrange("(c p) d -> p c d", p=128), writes=[wgf])
                        for cc in range(2):
                            op(dve, lambda e, g=g, cc=cc: e.tensor_tensor(wg.ap[:, g, cc, :], wgf.ap[:, cc, :], scb.ap[:, g * 256:(g + 1) * 256], ALU.mult),
                               reads=[wgf, scb], writes=[wg])
                    ext = [sb(es, "ext%d" % s, [128, 8, HP + 128], F32) for s in range(2)]
                    wa = sb(es, "wa", [128, 2, HP + 128], F32)
                    wb2 = sb(es, "wb2", [128, 2, HP + 128], F32)
                    dT = sb(es, "dT", [128, 8, 128], BF16)
                    ftmp = sb(es, "ftmp", [128, 2, 16], F32)
                    prevf = sb(es, "prevf", [32, D], F32)

                def loads(i):
                    r0, nt = tile_rows(i)
                    k.load(xt[i % 2].ap[:nt, :], xsrc(l, i), reads=[k.dbuf(("x", l, i))], writes=[xt[i % 2]])

                def mk_next(i):
                    r0, nt = tile_rows(i)
                    make_xT(xt[i % 2], xbs[i % 2], pstr, xTs[i % 2], nt)

                def proj(ps2, col0, nt, xT):
                    for fc in range(8):
                        ps = ps2[fc // 4]
                        for kk in range(8):
                            op(pe, lambda e, ps=ps, fc=fc, kk=kk: e.matmul(ps.ap[:, fc % 4, :nt], win.ap[:, kk, col0 + fc * 128:col0 + (fc + 1) * 128], xT.ap[:, kk, :nt],
                                                                         start=(kk == 0), stop=(kk == 7)),
                               reads=[win, xT], writes=[ps] if (fc % 4 == 0 and kk == 0) else [])
                        if fc % 4 == 3:
                            ps.w = (pe.sem, pe.sem.n)

                def state_out(src32, nt, nrows, dst):
                    for c in range(8):
                        op(pe, lambda e, c=c: e.transpose(psG[c // 4].ap[:nt, c % 4, :], src32[:, c, 0:nt], identf.ap[:, :]),
                           reads=[identf], writes=[psG[c // 4]] if c % 4 == 0 else [], extra=[srcbuf[0].w])
                        if c % 4 == 3:
                            psG[c // 4].w = (pe.sem, pe.sem.n)
                    for hf in range(2):
                        op(act, lambda e, hf=hf: e.activation(ut.ap[:nt, hf * 512:(hf + 1) * 512], psG[hf].ap[:nt, :, :], AF.Copy), reads=[psG[hf]], writes=[ut])
                    k.store(dst, ut.ap[nt - nrows:nt, :], reads=[ut])

                srcbuf = [None]

                def compute(i):
                    r0, nt = tile_rows(i)
                    x_ = xt[i % 2]
                    e_cur = ext[i % 2]
                    e_prev = ext[(i + 1) % 2]
                    xT = xTs[i % 2]
                    if i == 0:
                        op(dve, lambda e: e.memset(e_cur.ap[:, :, 0:HP], 0.0), writes=[e_cur])
                    elif i < NTP:
                        op(dve, lambda e: e.tensor_copy(e_cur.ap[:, :, 0:HP], e_prev.ap[:, :, 128:128 + HP]), reads=[e_prev], writes=[e_cur])
                    else:
                        s = i - NTP
                        if isB:
                            k.load(prevf.ap[0:30, :], I["sconv"][s], writes=[prevf])
                            op(act, lambda e: e.activation(prevb.ap[0:30, :], prevf.ap[0:30, :], AF.Copy), reads=[prevf], writes=[prevb])
                            for c in range(8):
                                op(pe, lambda e, c=c: e.transpose(pstr.ap[:, c, 0:30], prevb.ap[0:30, c * 128:(c + 1) * 128], identb.ap[0:30, 0:30]),
                                   reads=[prevb, identb], writes=[pstr] if c == 0 else [])
                            pstr.w = (pe.sem, pe.sem.n)
                            op(dve, lambda e: e.tensor_copy(e_cur.ap[:, :, 0:30], pstr.ap[:, :, 0:30]), reads=[pstr], writes=[e_cur])
                        else:
                            k.load(prevf.ap[0:15, :], I["spool"][s], writes=[prevf])
                            for c in range(8):
                                op(pe, lambda e, c=c: e.transpose(psG[c // 4].ap[:, c % 4, 0:15], prevf.ap[0:15, c * 128:(c + 1) * 128], identf.ap[0:15, 0:15]),
                                   reads=[prevf, identf], writes=[psG[c // 4]] if c % 4 == 0 else [])
                                if c % 4 == 3:
                                    psG[c // 4].w = (pe.sem, pe.sem.n)
                            for hf in range(2):
                                op(dve, lambda e, hf=hf: e.tensor_copy(e_cur.ap[:, hf * 4:hf * 4 + 4, 1:16], psG[hf].ap[:, :, 0:15]), reads=[psG[hf]], writes=[e_cur])
                    if isB:
                        proj(psA, 0, nt, xT)
                        proj(psB, D, nt, xT)
                        proj(psG, 2 * D, nt, xT)
                        if i + 1 < NT:
                            mk_next(i + 1)
                        for hf in range(2):
                            op(act, lambda e, hf=hf: e.activation(sig.ap[:, hf * 4:hf * 4 + 4, :nt], psB[hf].ap[:, :, :nt], AF.Sigmoid), reads=[psB[hf]], writes=[sig])
                            op(act, lambda e, hf=hf: e.activation(gs.ap[:, hf * 4:hf * 4 + 4, :nt], psG[hf].ap[:, :, :nt], AF.Silu), reads=[psG[hf]], writes=[gs])
                            op(dve, lambda e, hf=hf: e.tensor_tensor(u32.ap[:, hf * 4:hf * 4 + 4, :nt], psA[hf].ap[:, :, :nt], sig.ap[:, hf * 4:hf * 4 + 4, :nt], ALU.mult),
                               reads=[psA[hf], sig], writes=[u32])
                        op(act, lambda e: e.activation(e_cur.ap[:, :, HP:HP + nt], u32.ap[:, :, :nt], AF.Copy), reads=[u32], writes=[e_cur])
                        for c in range(8):
                            ps = psA[c // 4]
                            for jj in range(31):
                                op(pe, lambda e, ps=ps, c=c, jj=jj: e.matmul(ps.ap[:, c % 4, :nt], Dg.ap[:, c, jj, :], e_cur.ap[:, c, jj:jj + nt], start=(jj == 0), stop=(jj == 30)),
                                   reads=[Dg, e_cur], writes=[ps] if (c % 4 == 0 and jj == 0) else [])
                            if c % 4 == 3:
                                ps.w = (pe.sem, pe.sem.n)
                        for c in range(8):
                            op(act, lambda e, c=c: e.activation(cT.ap[:, c, :nt], psA[c // 4].ap[:, c % 4, :nt], AF.Identity, bias=cb.ap[:, c:c + 1], scale=1.0),
                               reads=[psA[c // 4], cb], writes=[cT])
                        op(act, lambda e: e.activation(sq.ap[:, :, :nt], cT.ap[:, :, :nt], AF.Square), reads=[cT], writes=[sq])
                        for which, src in ((0, cT), (1, sq)):
                            for c in range(8):
                                op(pe, lambda e, which=which, src=src, c=c: e.matmul(psS.ap[:, which, :nt], onesf.ap[:, :], src.ap[:, c, :nt], start=(c == 0), stop=(c == 7)),
                                   reads=[onesf, src], writes=[psS] if (which == 0 and c == 0) else [])
                        psS.w = (pe.sem, pe.sem.n)
                        op(dve, lambda e: e.tensor_scalar(mean.ap[:, :nt], psS.ap[:, 0, :nt], 1.0 / D, None, ALU.mult), reads=[psS], writes=[mean])
                        op(dve, lambda e: e.tensor_tensor(msq.ap[:, :nt], mean.ap[:, :nt], mean.ap[:, :nt], ALU.mult), reads=[mean], writes=[msq])
                        op(dve, lambda e: e.scalar_tensor_tensor(var.ap[:, :nt], psS.ap[:, 1, :nt], 1.0 / D, msq.ap[:, :nt], ALU.mult, ALU.subtract),
                           reads=[psS, msq], writes=[var])
                        op(act, lambda e: e.activation(var.ap[:, :nt], var.ap[:, :nt], AF.Sqrt, bias=epsb.ap[:, :], scale=1.0), reads=[epsb], writes=[var])
                        op(dve, lambda e: e.reciprocal(var.ap[:, :nt], var.ap[:, :nt]), writes=[var])
                        mbc = mean.ap[:, :nt].unsqueeze(1).to_broadcast([128, 8, nt])
                        vbc = var.ap[:, :nt].unsqueeze(1).to_broadcast([128, 8, nt])
                        op(dve, lambda e: e.tensor_tensor(cT.ap[:, :, :nt], cT.ap[:, :, :nt], mbc, ALU.subtract), reads=[mean], writes=[cT])
                        op(dve, lambda e: e.tensor_tensor(cT.ap[:, :, :nt], cT.ap[:, :, :nt], vbc, ALU.mult), reads=[var], writes=[cT])
                        for c in range(8):
                            op(act, lambda e, c=c: e.activation(sn.ap[:, c, :nt], cT.ap[:, c, :nt], AF.Silu, bias=nbp.ap[:, c:c + 1], scale=ngp.ap[:, c:c + 1]),
                               reads=[cT, ngp, nbp], writes=[sn])
                        op(dve, lambda e: e.tensor_tensor(aT.ap[:, :, :nt], sn.ap[:, :, :nt], gs.ap[:, :, :nt], ALU.mult), reads=[sn, gs], writes=[aT])
                        if i == NTP - 1 or i >= NTP:
                            srcbuf[0] = u32
                            dst = O["ncp"][:, :] if i < NTP else O["ncs"][i - NTP]
                            state_out(u32.ap, nt, 30, dst)
                    else:
                        proj(psA, 0, nt, xT)
                        proj(psG, D, nt, xT)
                        if i + 1 < NT:
                            mk_next(i + 1)
                        for hf in range(2):
                            op(act, lambda e, hf=hf: e.activation(gs.ap[:, hf * 4:hf * 4 + 4, :nt], psG[hf].ap[:, :, :nt], AF.Silu), reads=[psG[hf]], writes=[gs])
                            op(act, lambda e, hf=hf: e.activation(e_cur.ap[:, hf * 4:hf * 4 + 4, HP:HP + nt], psA[hf].ap[:, :, :nt], AF.Copy), reads=[psA[hf]], writes=[e_cur])
                        L = HP + nt
                        for g in range(4):
                            E = e_cur.ap[:, 2 * g:2 * g + 2, :]
                            cur = None
                            bufs = [wa, wb2]
                            for lv in range(g + 1):
                                sh = 2 ** lv
                                lo = 2 ** (lv + 1)
                                dstb = bufs[lv % 2]
                                if lv == 0:
                                    op(dve, lambda e, dstb=dstb, E=E, lo=lo, sh=sh: e.tensor_tensor(dstb.ap[:, :, lo:L], E[:, :, lo:L], E[:, :, lo - sh:L - sh], ALU.add),
                                       reads=[e_cur], writes=[dstb])
                                else:
                                    srcb = bufs[(lv + 1) % 2]
                                    op(dve, lambda e, dstb=dstb, srcb=srcb, lo=lo, sh=sh: e.tensor_tensor(dstb.ap[:, :, lo:L], srcb.ap[:, :, lo:L], srcb.ap[:, :, lo - sh:L - sh], ALU.add),
                                       reads=[srcb], writes=[dstb])
                                cur = dstb
                            w = 2 ** (g + 1)
                            op(dve, lambda e, cur=cur, E=E, g=g, w=w: e.scalar_tensor_tensor(dT.ap[:, 2 * g:2 * g + 2, :nt], cur.ap[:, :, HP:HP + nt], 1.0 / w, E[:, :, HP:HP + nt], ALU.mult, ALU.subtract),
                               reads=[cur, e_cur], writes=[dT])
                            if i == 0:
                                ic = invc.ap[:, g, :].unsqueeze(1).to_broadcast([128, 2, 16])
                                op(dve, lambda e, cur=cur, ic=ic: e.tensor_tensor(ftmp.ap[:, :, :], cur.ap[:, :, HP:HP + 16], ic, ALU.mult), reads=[cur, invc], writes=[ftmp])
                                op(dve, lambda e, E=E, g=g: e.tensor_tensor(dT.ap[:, 2 * g:2 * g + 2, 0:16], ftmp.ap[:, :, :], E[:, :, HP:HP + 16], ALU.subtract),
                                   reads=[ftmp, e_cur], writes=[dT])
                        for g in range(4):
                            for dc in range(2):
                                oc = 2 * g + dc
                                ps = psA[oc // 4]
                                for cc in range(2):
                                    op(pe, lambda e, ps=ps, g=g, dc=dc, cc=cc, oc=oc: e.matmul(ps.ap[:, oc % 4, :nt], wg.ap[:, g, cc, dc * 128:(dc + 1) * 128], dT.ap[:, 2 * g + cc, :nt],
                                                                                          start=(cc == 0), stop=(cc == 1)),
                                       reads=[wg, dT], writes=[ps] if (oc % 4 == 0 and cc == 0) else [])
                                if oc % 4 == 3:
                                    ps.w = (pe.sem, pe.sem.n)
                        for hf in range(2):
                            op(dve, lambda e, hf=hf: e.tensor_tensor(aT.ap[:, hf * 4:hf * 4 + 4, :nt], psA[hf].ap[:, :, :nt], gs.ap[:, hf * 4:hf * 4 + 4, :nt], ALU.mult),
                               reads=[psA[hf], gs], writes=[aT])
                        if i == NTP - 1 or i >= NTP:
                            srcbuf[0] = e_cur
                            dst = O["npp"][:, :] if i < NTP else O["nps"][i - NTP]
                            state_out(e_cur.ap[:, :, HP:HP + 128], nt, 15, dst)
                    post(l, i, aT, wout, x_, psB, pb["z"], pb["stats"], pb["mv"], pb["rstd"], pb["xn"], gbc, bbc)

                loads(0)
                mk_next(0)
                for i in range(NT):
                    if i + 1 < NT:
                        loads(i + 1)
                    compute(i)
                phase_barrier()
                k.flush()

        for spec_ in getattr(cfg, "layers", [("A", 0, 0), ("B", 1), ("C", 2), ("A", 3, 1)]):
            if spec_[0] == "A":
                layer_A(spec_[1], spec_[2])
            else:
                layer_BC(spec_[1], spec_[0])
        toks = k.barrier_tokens()
        k.sp.add(lambda e: e.nop(), toks)
        k.flush()
    return nc


def rope_table(cfg):
    def tab(pos, r):
        half = r // 2
        inv = (ROPE_THETA ** (-np.arange(half, dtype=np.float32) * 2.0 / r)).astype(np.float32)
        ang = pos.astype(np.float32)[:, None] * inv[None, :]
        return np.cos(ang).astype(np.float32), np.sin(ang).astype(np.float32)
    posp = np.arange(cfg.SEQ)
    poss = cfg.PAST + np.arange(DS)
    rows = []
    for pos in (posp, poss):
        c16, s16 = tab(pos, 32)
        c8, s8 = tab(pos, 16)
        rows.append(np.concatenate([c16, s16, c8, s8], axis=1))
    return np.concatenate([rows[0]] + [rows[1]] * cfg.NS, axis=0).astype(np.float32)


def make_in_maps(cfg, inp, ncores, nb):
    f = lambda a: np.ascontiguousarray(np.asarray(a, dtype=np.float32))
    rope = rope_table(cfg)
    maps = []
    NS = cfg.NS
    for c in range(ncores):
        b = c % nb
        ss = slice(c * NS, (c + 1) * NS)
        maps.append(dict(
            xp=f(inp["x_prompt"][b]), xs=f(inp["x_sample"][ss]).reshape(NS * DS, D),
            ck=f(inp["cache_k"][:, ss]).reshape(2, NS, cfg.PAST, D), cv=f(inp["cache_v"][:, ss]).reshape(2, NS, cfg.PAST, D),
            cki=f(inp["cache_kidx"][:, ss]), sconv=f(inp["state_conv"][0, ss]), spool=f(inp["state_pool"][0, ss]),
            w_in_a=f(inp["w_in_a"]), w_out_a=f(inp["w_out_a"]), w_in_b=f(inp["w_in_b"][0]), conv_w=f(inp["conv_w_b"][0]),
            conv_b=f(inp["conv_bias_b"][0]), ng=f(inp["norm_g_b"][0]), nb=f(inp["norm_b_b"][0]), w_out_b=f(inp["w_out_b"][0]),
            w_in_c=f(inp["w_in_c"][0]), w_grp=f(inp["w_grp_c"][0]), scale_c=f(inp["scale_c"][0]), w_out_c=f(inp["w_out_c"][0]),
            ln_g=f(inp["ln_g"]), ln_b=f(inp["ln_b"]), rope=rope,
        ))
    return maps


def assemble(cfg, res, ncores, nb):
    NS = cfg.NS
    R = res
    cat = lambda key, cores: np.stack([R[c][key] for c in cores])
    pc = list(range(nb))
    ac = list(range(ncores))
    yp = cat("yp", pc)
    ys = np.concatenate([R[c]["ys"].reshape(NS, DS, D) for c in ac])
    nkp = np.stack([R[c]["nkp"] for c in pc], axis=1).reshape(2, nb, cfg.SEQ, NH, 128)
    nvp = np.stack([R[c]["nvp"] for c in pc], axis=1).reshape(2, nb, cfg.SEQ, NH, 128)
    nkip = np.stack([R[c]["nkip"] for c in pc], axis=1)
    ncp = cat("ncp", pc)[None]
    npp = cat("npp", pc)[None]
    nks = np.concatenate([R[c]["nks"].reshape(2, NS, DS, NH, 128) for c in ac], axis=1)
    nvs = np.concatenate([R[c]["nvs"].reshape(2, NS, DS, NH, 128) for c in ac], axis=1)
    nkis = np.concatenate([R[c]["nkis"].reshape(2, NS, DS, 64) for c in ac], axis=1)
    ncs = np.concatenate([R[c]["ncs"] for c in ac])[None]
    nps = np.concatenate([R[c]["nps"] for c in ac])[None]
    return tuple(np.ascontiguousarray(a, dtype=np.float32) for a in (yp, ys, nkp, nvp, nkip, ncp, npp, nks, nvs, nkis, ncs, nps))


def kernel(**inputs):
    cfg = Cfg()
    nc = build(cfg)
    maps = make_in_maps(cfg, inputs, 8, 4)
    res = run_bass_kernel_spmd(nc, maps, core_ids=list(range(8)))
    return assemble(cfg, res.results, 8, 4)
```
